# Optimizing a Trainium2 kernel written in Bass

```python
import math
import jax
import jax.numpy as jnp
from jax import lax
import numpy as np

D_MODEL = 1024
BATCH = 16
SEQ = 256
DEPTH = 2
DEC_BATCH = 8
DEC_SEQ = 1024
PAST_LEN = 512

GRID_W = 64
HEAD_DIM = 64
ROPE_BASE = 10000.0
Q_BLOCK = 128
NEG_INF = -1e30
MOD_CHUNKS = 6
GQA_HEADS = 4
GQA_KV_HEADS = 2
GQA_GROUP = GQA_HEADS // GQA_KV_HEADS
NA_HEADS = 4
NA_KH = 8
NA_KW = 16
DN_HEADS = 4
DN_DK = 64
DN_DV = 64
DN_CONV = 4
DN_CHUNK = 64
DN_QKV = DN_HEADS * (2 * DN_DK + DN_DV)
MLA_HEADS = 4
MLA_Q_LORA = 256
MLA_KV_LORA = 128
MLA_NOPE = 64
MLA_ROPE = 32
MLA_V = 64
MLA_SCALE = (MLA_NOPE + MLA_ROPE) ** -0.5

MIX_W = GQA_HEADS * HEAD_DIM + NA_HEADS * HEAD_DIM + DN_HEADS * DN_DV + MLA_HEADS * MLA_V
D_FF = -(-8 * D_MODEL // (3 * 256)) * 256
IN_SIZES = (GQA_HEADS * HEAD_DIM, GQA_KV_HEADS * HEAD_DIM, GQA_KV_HEADS * HEAD_DIM,
            NA_HEADS * HEAD_DIM, NA_HEADS * HEAD_DIM, NA_HEADS * HEAD_DIM,
            DN_QKV, DN_HEADS * DN_DV, 2 * DN_HEADS, 2 * DN_HEADS,
            MLA_Q_LORA, MLA_KV_LORA, MLA_ROPE)
IN_COLS = sum(IN_SIZES)

kernel_name = 'hybrid_flow_prefix_step'


def rms_norm(x, g, eps=1e-6):
    xf = x.astype(jnp.float32)
    y = xf * lax.rsqrt(jnp.mean(xf * xf, axis=-1, keepdims=True) + eps)
    return (y * g.astype(jnp.float32)).astype(x.dtype)


def l2_norm(x, eps=1e-6):
    xf = x.astype(jnp.float32)
    return (xf * lax.rsqrt(jnp.sum(xf * xf, axis=-1, keepdims=True) + eps)).astype(x.dtype)


def rope_axis(x, pos):
    half = x.shape[-1] // 2
    inv = ROPE_BASE ** (-jnp.arange(half, dtype=jnp.float32) / half)
    ang = pos.astype(jnp.float32)[:, None] * inv[None, :]
    cos = jnp.cos(ang)[:, None, :]
    sin = jnp.sin(ang)[:, None, :]
    xf = x.astype(jnp.float32)
    x1, x2 = xf[..., :half], xf[..., half:]
    return jnp.concatenate([x1 * cos - x2 * sin, x2 * cos + x1 * sin], axis=-1).astype(x.dtype)


def rope_2d(x):
    t = jnp.arange(x.shape[1])
    half = x.shape[-1] // 2
    return jnp.concatenate([rope_axis(x[..., :half], t // GRID_W),
                            rope_axis(x[..., half:], t % GRID_W)], axis=-1)


def block_attention(q, k, v, scale):
    b, lq = q.shape[0], q.shape[1]
    nb = lq // Q_BLOCK
    qb = jnp.swapaxes(q.reshape(b, nb, Q_BLOCK, *q.shape[2:]), 0, 1)

    def one_block(qi):
        s = jnp.einsum('bqhgd,bkhd->bhgqk', qi, k, preferred_element_type=jnp.float32) * scale
        pr = jax.nn.softmax(s, axis=-1).astype(v.dtype)
        return jnp.einsum('bhgqk,bkhd->bqhgd', pr, v)

    o = lax.map(one_block, qb)
    return jnp.swapaxes(o, 0, 1).reshape(b, lq, *o.shape[3:])


def natten_latent(q, k, v, k_ctx, v_ctx, bias):
    b, n, h, d = q.shape
    rows = n // GRID_W
    kh = min(NA_KH, rows)
    kw = NA_KW
    r = jnp.arange(rows)
    col = jnp.arange(GRID_W)
    r0 = jnp.clip(r - kh // 2, 0, rows - kh)
    band = r0[:, None] + jnp.arange(kh)[None, :]
    c0 = jnp.clip(col - kw // 2, 0, GRID_W - kw)
    allowed = (col[None, :] >= c0[:, None]) & (col[None, :] < c0[:, None] + kw)
    ri = band - r[:, None] + (NA_KH - 1)
    ci = jnp.clip(col[None, :] - col[:, None] + (kw - 1), 0, 2 * kw - 2)
    rel = bias[:, ri[:, None, :, None], ci[None, :, None, :]].astype(jnp.float32)
    rel = jnp.where(allowed[None, None, :, None, :], rel, NEG_INF)
    rel = rel.transpose(1, 0, 2, 3, 4).reshape(rows, h, GRID_W, kh * GRID_W)
    qg = q.reshape(b, rows, GRID_W, h, d)
    kb = k.reshape(b, rows, GRID_W, h, d)[:, band].reshape(b, rows, kh * GRID_W, h, d)
    vb = v.reshape(b, rows, GRID_W, h, d)[:, band].reshape(b, rows, kh * GRID_W, h, d)
    scale = d ** -0.5
    s_loc = jnp.einsum('brchd,brkhd->brhck', qg, kb, preferred_element_type=jnp.float32) * scale + rel[None]
    s_ctx = jnp.einsum('brchd,blhd->brhcl', qg, k_ctx, preferred_element_type=jnp.float32) * scale
    pr = jax.nn.softmax(jnp.concatenate([s_loc, s_ctx], axis=-1), axis=-1).astype(v.dtype)
    nloc = kh * GRID_W
    o = (jnp.einsum('brhck,brkhd->brchd', pr[..., :nloc], vb)
         + jnp.einsum('brhcl,blhd->brchd', pr[..., nloc:], v_ctx))
    return o.reshape(b, n, h * d)


def short_conv(x, w):
    return lax.conv_general_dilated(
        x, w[:, None, :].astype(x.dtype), window_strides=(1,),
        padding=[((DN_CONV - 1) // 2, DN_CONV // 2)],
        dimension_numbers=('NWC', 'WIO', 'NWC'), feature_group_count=x.shape[-1])


def gated_delta_chunked(q, k, v, beta, logd, s0):
    b, l, h, dk = q.shape
    dv = v.shape[-1]
    c = DN_CHUNK
    n = l // c

    def chunks(t):
        t = t.astype(jnp.float32).reshape(b, n, c, h, *t.shape[3:])
        return jnp.moveaxis(t, (1, 3), (0, 2))

    qc, kc, vc = chunks(q), chunks(k), chunks(v)
    bc, gc = chunks(beta), chunks(logd)
    gcum = jnp.cumsum(gc, axis=-1)
    tril = jnp.tril(jnp.ones((c, c), dtype=bool))
    strict = jnp.tril(jnp.ones((c, c), dtype=bool), -1)
    dmask = jnp.where(tril, jnp.exp(jnp.where(tril, gcum[..., :, None] - gcum[..., None, :], 0.0)), 0.0)
    kb = kc * bc[..., None]
    a = jnp.where(strict, jnp.einsum('nbhid,nbhjd->nbhij', kb, kc) * dmask, 0.0)
    rhs = jnp.concatenate([vc * bc[..., None], kb * jnp.exp(gcum)[..., None]], axis=-1)
    sol = lax.linalg.triangular_solve(a + jnp.eye(c, dtype=jnp.float32), rhs,
                                      left_side=True, lower=True, unit_diagonal=True)
    u, w = sol[..., :dv], sol[..., dv:]
    qk = jnp.where(tril, jnp.einsum('nbhid,nbhjd->nbhij', qc, kc) * dmask, 0.0)
    q_dec = qc * jnp.exp(gcum)[..., None]
    k_dec = kc * jnp.exp(gcum[..., -1:] - gcum)[..., None]
    g_last = jnp.exp(gcum[..., -1])

    def step(s, xs):
        u_i, w_i, qk_i, qd_i, kd_i, gl_i = xs
        v_new = u_i - jnp.einsum('bhcd,bhde->bhce', w_i, s)
        o_i = jnp.einsum('bhcd,bhde->bhce', qd_i, s) + jnp.einsum('bhij,bhje->bhie', qk_i, v_new)
        s = s * gl_i[..., None, None] + jnp.einsum('bhcd,bhce->bhde', kd_i, v_new)
        return s, o_i

    s_fin, o = lax.scan(step, s0.astype(jnp.float32), (u, w, qk, q_dec, k_dec, g_last))
    o = jnp.moveaxis(o, (0, 2), (1, 3)).reshape(b, l, h, dv)
    return o.astype(v.dtype), s_fin.astype(v.dtype)


def flip_seq(t):
    return jnp.flip(t, axis=1)


def deltanet_mixer(zqkv, zg, za, zb, s0, p):
    b, l, _ = zqkv.shape
    qkv = jax.nn.silu(short_conv(zqkv, p['dn_conv_w']))
    q, k, v = jnp.split(qkv, [DN_HEADS * DN_DK, 2 * DN_HEADS * DN_DK], axis=-1)
    q = l2_norm(q.reshape(b, l, DN_HEADS, DN_DK)) * (DN_DK ** -0.5)
    k = l2_norm(k.reshape(b, l, DN_HEADS, DN_DK))
    v = v.reshape(b, l, DN_HEADS, DN_DV)
    a = za.reshape(b, l, 2, DN_HEADS).astype(jnp.float32)
    beta = jax.nn.sigmoid(zb.reshape(b, l, 2, DN_HEADS).astype(jnp.float32))
    logd = -jnp.exp(p['dn_a_log'].astype(jnp.float32)) * jax.nn.softplus(a + p['dn_dt_bias'].astype(jnp.float32))
    o_f, s_f = gated_delta_chunked(q, k, v, beta[:, :, 0], logd[:, :, 0], s0[:, 0])
    o_b, s_b = gated_delta_chunked(flip_seq(q), flip_seq(k), flip_seq(v),
                                   flip_seq(beta[:, :, 1]), flip_seq(logd[:, :, 1]), s0[:, 1])
    o = o_f + flip_seq(o_b)
    o = rms_norm(o, p['dn_out_g']) * jax.nn.silu(zg.reshape(b, l, DN_HEADS, DN_DV))
    return o.reshape(b, l, DN_HEADS * DN_DV), jnp.stack([s_f, s_b], axis=1)


def gqa_qkv(zq, zk, zv, p):
    b, l, _ = zq.shape
    q = rms_norm(zq.reshape(b, l, GQA_HEADS, HEAD_DIM), p['gqa_qn_g'])
    k = rms_norm(zk.reshape(b, l, GQA_KV_HEADS, HEAD_DIM), p['gqa_kn_g'])
    v = zv.reshape(b, l, GQA_KV_HEADS, HEAD_DIM)
    return q, k, v


def gqa_attend(q, k, v):
    b, lq = q.shape[0], q.shape[1]
    qg = q.reshape(b, lq, GQA_KV_HEADS, GQA_GROUP, HEAD_DIM)
    return block_attention(qg, k, v, HEAD_DIM ** -0.5).reshape(b, lq, GQA_HEADS * HEAD_DIM)


def mla_query(zcq, p, positioned):
    b, l, _ = zcq.shape
    q = (rms_norm(zcq, p['mla_qn_g']) @ p['mla_wq_up']).reshape(b, l, MLA_HEADS, MLA_NOPE + MLA_ROPE)
    if positioned:
        q = jnp.concatenate([q[..., :MLA_NOPE], rope_2d(q[..., MLA_NOPE:])], axis=-1)
    return q


def mla_keys_values(ckv, krope, p):
    b, l, _ = ckv.shape
    kv = (ckv @ p['mla_wkv_up']).reshape(b, l, MLA_HEADS, MLA_NOPE + MLA_V)
    k_rope = jnp.broadcast_to(krope[:, :, None, :], (b, l, MLA_HEADS, MLA_ROPE))
    k = jnp.concatenate([kv[..., :MLA_NOPE], k_rope], axis=-1)
    return k, kv[..., MLA_NOPE:]


def project_in(h, p):
    z = jnp.einsum('bld,de->ble', h, p['w_in'])
    offsets = np.cumsum(IN_SIZES)[:-1].tolist()
    return jnp.split(z, offsets, axis=-1)


def mix_context(h, p):
    b, l, _ = h.shape
    (a_q, a_k, a_v, b_q, b_k, b_v, c_qkv, c_g, c_a, c_b, d_cq, d_ckv, d_kr) = project_in(h, p)
    q, k, v = gqa_qkv(a_q, a_k, a_v, p)
    o_a = gqa_attend(q, k, v)
    nq = b_q.reshape(b, l, NA_HEADS, HEAD_DIM)
    nk = b_k.reshape(b, l, NA_HEADS, HEAD_DIM)
    nv = b_v.reshape(b, l, NA_HEADS, HEAD_DIM)
    o_b = block_attention(nq[:, :, :, None], nk, nv, HEAD_DIM ** -0.5).reshape(b, l, NA_HEADS * HEAD_DIM)
    s0 = jnp.zeros((b, 2, DN_HEADS, DN_DK, DN_DV), h.dtype)
    o_c, s_dn = deltanet_mixer(c_qkv, c_g, c_a, c_b, s0, p)
    q_m = mla_query(d_cq, p, False)
    ckv = rms_norm(d_ckv, p['mla_kvn_g'])
    k_m, v_m = mla_keys_values(ckv, d_kr, p)
    o_d = block_attention(q_m[:, :, :, None], k_m, v_m, MLA_SCALE).reshape(b, l, MLA_HEADS * MLA_V)
    out = jnp.concatenate([o_a, o_b, o_c, o_d], axis=-1) @ p['w_out']
    return out, (k, v, nk, nv, s_dn, ckv, d_kr)


def mix_latent(h, ck_a, cv_a, ck_b, cv_b, s_dn, c_ckv, c_kr, p):
    b, n, _ = h.shape
    (a_q, a_k, a_v, b_q, b_k, b_v, c_qkv, c_g, c_a, c_b, d_cq, d_ckv, d_kr) = project_in(h, p)
    q, k, v = gqa_qkv(a_q, a_k, a_v, p)
    o_a = gqa_attend(rope_2d(q), jnp.concatenate([ck_a, rope_2d(k)], axis=1), jnp.concatenate([cv_a, v], axis=1))
    nq = b_q.reshape(b, n, NA_HEADS, HEAD_DIM)
    nk = b_k.reshape(b, n, NA_HEADS, HEAD_DIM)
    nv = b_v.reshape(b, n, NA_HEADS, HEAD_DIM)
    o_b = natten_latent(nq, nk, nv, ck_b, cv_b, p['na_bias'])
    o_c, _ = deltanet_mixer(c_qkv, c_g, c_a, c_b, s_dn, p)
    q_m = mla_query(d_cq, p, True)
    ckv = rms_norm(d_ckv, p['mla_kvn_g'])
    kr = rope_2d(d_kr[:, :, None, :])[:, :, 0, :]
    k_lat, v_lat = mla_keys_values(ckv, kr, p)
    k_ctx, v_ctx = mla_keys_values(c_ckv, c_kr, p)
    o_d = block_attention(q_m[:, :, :, None], jnp.concatenate([k_ctx, k_lat], axis=1),
                          jnp.concatenate([v_ctx, v_lat], axis=1), MLA_SCALE).reshape(b, n, MLA_HEADS * MLA_V)
    return jnp.concatenate([o_a, o_b, o_c, o_d], axis=-1) @ p['w_out']


def modulation(cond, p):
    m = jax.nn.silu(cond) @ p['w_mod'] + p['b_mod']
    return jnp.split(m[:, None, :], MOD_CHUNKS, axis=-1)


def modulated_norm(x, g, shift, scale):
    return rms_norm(x, g) * (1 + scale) + shift


def ffn_sublayer(x, mods, p):
    h = modulated_norm(x, p['norm2_g'], mods[3], mods[4])
    y = (jax.nn.silu(h @ p['ffn_w_gate']) * (h @ p['ffn_w_up'])) @ p['ffn_w_down']
    return x + mods[5] * y


def setup_inputs(seed: int = 0) -> dict:
    key = jax.random.key(seed)
    ks = jax.random.split(key, 32)

    def nrm(i, shape, s=1.0):
        return s * jax.random.normal(ks[i], shape, dtype=jnp.float32)

    def gain(i, shape):
        return 1.0 + 0.05 * jax.random.normal(ks[i], shape, dtype=jnp.float32)

    dt = jnp.exp(jax.random.uniform(ks[20], (DEPTH, 2, DN_HEADS), minval=math.log(1e-3), maxval=math.log(1e-1)))
    return {
        'x_prompt': nrm(0, (BATCH, SEQ, D_MODEL)),
        'x_sample': nrm(1, (DEC_BATCH, DEC_SEQ, D_MODEL)),
        'cache_gqa_k': nrm(2, (DEC_BATCH, DEPTH, PAST_LEN, GQA_KV_HEADS, HEAD_DIM)),
        'cache_gqa_v': nrm(3, (DEC_BATCH, DEPTH, PAST_LEN, GQA_KV_HEADS, HEAD_DIM)),
        'cache_na_k': nrm(4, (DEC_BATCH, DEPTH, PAST_LEN, NA_HEADS, HEAD_DIM)),
        'cache_na_v': nrm(5, (DEC_BATCH, DEPTH, PAST_LEN, NA_HEADS, HEAD_DIM)),
        'state_dn': nrm(6, (DEC_BATCH, DEPTH, 2, DN_HEADS, DN_DK, DN_DV), 0.1),
        'cache_mla_ckv': nrm(7, (DEC_BATCH, DEPTH, PAST_LEN, MLA_KV_LORA)),
        'cache_mla_krope': nrm(8, (DEC_BATCH, DEPTH, PAST_LEN, MLA_ROPE)),
        'c': nrm(9, (DEC_BATCH, D_MODEL)),
        'c_ctx': nrm(10, (D_MODEL,)),
        'norm1_g': gain(11, (DEPTH, D_MODEL)),
        'norm2_g': gain(12, (DEPTH, D_MODEL)),
        'w_mod': nrm(13, (DEPTH, D_MODEL, MOD_CHUNKS * D_MODEL), 0.5 * D_MODEL ** -0.5),
        'b_mod': nrm(14, (DEPTH, MOD_CHUNKS * D_MODEL), 0.01),
        'w_in': nrm(15, (DEPTH, D_MODEL, IN_COLS), D_MODEL ** -0.5),
        'w_out': nrm(16, (DEPTH, MIX_W, D_MODEL), MIX_W ** -0.5),
        'gqa_qn_g': gain(17, (DEPTH, HEAD_DIM)),
        'gqa_kn_g': gain(18, (DEPTH, HEAD_DIM)),
        'na_bias': nrm(19, (DEPTH, NA_HEADS, 2 * NA_KH - 1, 2 * NA_KW - 1), 0.1),
        'dn_conv_w': nrm(21, (DEPTH, DN_CONV, DN_QKV), DN_CONV ** -0.5),
        'dn_a_log': jnp.log(jax.random.uniform(ks[22], (DEPTH, 2, DN_HEADS), minval=1.0, maxval=16.0)),
        'dn_dt_bias': dt + jnp.log(-jnp.expm1(-dt)),
        'dn_out_g': gain(23, (DEPTH, DN_DV)),
        'mla_qn_g': gain(24, (DEPTH, MLA_Q_LORA)),
        'mla_wq_up': nrm(25, (DEPTH, MLA_Q_LORA, MLA_HEADS * (MLA_NOPE + MLA_ROPE)), MLA_Q_LORA ** -0.5),
        'mla_kvn_g': gain(26, (DEPTH, MLA_KV_LORA)),
        'mla_wkv_up': nrm(27, (DEPTH, MLA_KV_LORA, MLA_HEADS * (MLA_NOPE + MLA_V)), MLA_KV_LORA ** -0.5),
        'ffn_w_gate': nrm(28, (DEPTH, D_MODEL, D_FF), D_MODEL ** -0.5),
        'ffn_w_up': nrm(29, (DEPTH, D_MODEL, D_FF), D_MODEL ** -0.5),
        'ffn_w_down': nrm(30, (DEPTH, D_FF, D_MODEL), D_FF ** -0.5),
        'final_g': gain(31, (D_MODEL,)),
    }


def reference(x_prompt, x_sample, cache_gqa_k, cache_gqa_v, cache_na_k, cache_na_v, state_dn,
              cache_mla_ckv, cache_mla_krope, c, c_ctx, norm1_g, norm2_g, w_mod, b_mod, w_in, w_out,
              gqa_qn_g, gqa_kn_g, na_bias, dn_conv_w, dn_a_log, dn_dt_bias, dn_out_g, mla_qn_g,
              mla_wq_up, mla_kvn_g, mla_wkv_up, ffn_w_gate, ffn_w_up, ffn_w_down, final_g):
    stacked = {
        'norm1_g': norm1_g, 'norm2_g': norm2_g, 'w_mod': w_mod, 'b_mod': b_mod, 'w_in': w_in,
        'w_out': w_out, 'gqa_qn_g': gqa_qn_g, 'gqa_kn_g': gqa_kn_g, 'na_bias': na_bias,
        'dn_conv_w': dn_conv_w, 'dn_a_log': dn_a_log, 'dn_dt_bias': dn_dt_bias, 'dn_out_g': dn_out_g,
        'mla_qn_g': mla_qn_g, 'mla_wq_up': mla_wq_up, 'mla_kvn_g': mla_kvn_g, 'mla_wkv_up': mla_wkv_up,
        'ffn_w_gate': ffn_w_gate, 'ffn_w_up': ffn_w_up, 'ffn_w_down': ffn_w_down,
    }

    x = x_prompt
    ctx_cond = c_ctx[None, :]
    per_layer = []
    for l in range(DEPTH):
        p = {name: arr[l] for name, arr in stacked.items()}
        mods = modulation(ctx_cond, p)
        mix, st = mix_context(modulated_norm(x, p['norm1_g'], mods[0], mods[1]), p)
        x = ffn_sublayer(x + mods[2] * mix, mods, p)
        per_layer.append(st)
    y_prompt = rms_norm(x, final_g)
    new_gqa_k = jnp.stack([s[0] for s in per_layer], axis=1)
    new_gqa_v = jnp.stack([s[1] for s in per_layer], axis=1)
    new_na_k = jnp.stack([s[2] for s in per_layer], axis=1)
    new_na_v = jnp.stack([s[3] for s in per_layer], axis=1)
    new_dn_state = jnp.stack([s[4] for s in per_layer], axis=1)
    new_mla_ckv = jnp.stack([s[5] for s in per_layer], axis=1)
    new_mla_krope = jnp.stack([s[6] for s in per_layer], axis=1)

    x = x_sample
    for l in range(DEPTH):
        p = {name: arr[l] for name, arr in stacked.items()}
        mods = modulation(c, p)
        mix = mix_latent(modulated_norm(x, p['norm1_g'], mods[0], mods[1]),
                         cache_gqa_k[:, l], cache_gqa_v[:, l], cache_na_k[:, l], cache_na_v[:, l],
                         state_dn[:, l], cache_mla_ckv[:, l], cache_mla_krope[:, l], p)
        x = ffn_sublayer(x + mods[2] * mix, mods, p)
    y_sample = rms_norm(x, final_g)

    return (y_prompt, y_sample, new_gqa_k, new_gqa_v, new_na_k, new_na_v, new_dn_state, new_mla_ckv, new_mla_krope)
```

```python
import numpy as np
from contextlib import ExitStack
import concourse.bass as bass
import concourse.mybir as mybir
from concourse.bass_utils import run_bass_kernel_spmd

F32 = mybir.dt.float32
BF16 = mybir.dt.bfloat16
ALU = mybir.AluOpType
ACTF = mybir.ActivationFunctionType
AX = mybir.AxisListType

NCORES = 8
NL = 2
D = 1024
T = 1536
NB = 3
EPS = 1e-6
NEG = -30000.0
MLA_SCALE = 96 ** -0.5
ENGS = ("pe", "act", "dve", "pool", "sp")
EPOCH = 3000


class Tok:
    __slots__ = ("name", "w", "r")

    def __init__(self, name):
        self.name = name
        self.w = None
        self.r = {}


class _Rec:
    def __init__(self):
        self.calls = []

    def __getattr__(self, name):
        def f(*a, **k):
            self.calls.append((name, a, k))
            return self
        return f


def _record(fn):
    r = _Rec()
    fn(r)
    calls = r.calls
    assert calls

    def replay(e):
        ins = None
        for name, a, k in calls:
            ins = getattr(e, name)(*a, **k)
        return ins
    n = 0
    for name, a, k in calls:
        if name == "matmul" and k.get("lhsT") is not None and k["lhsT"].dtype == F32:
            n += 2
        else:
            n += 1
    replay.n = n
    return replay


class Prog:
    def __init__(self, nc, n_dma_sems=28):
        self.nc = nc
        self.es = ExitStack()
        self.ops = {e: [] for e in ENGS}
        self.cnt = {e: 0 for e in ENGS}
        self.sems = {e: [] for e in ENGS}
        self.known = {e: {} for e in ENGS}
        self.dma_sems = [self.es.enter_context(nc.semaphore(f"dq{i}")) for i in range(n_dma_sems)]
        self.dma_val = [0] * n_dma_sems
        self.dma_i = 0
        self.dma_cnt = {}
        self.toks = {}
        self.nbuf = 0
        self.bank_i = {"s": 0, "a": 0}
        self.n_pe = 0
        self.marks = []

    def sb(self, shape, dtype, name=None):
        self.nbuf += 1
        return self.es.enter_context(self.nc.sbuf_tensor(name or f"b{self.nbuf}", list(shape), dtype))

    def ps(self, shape, dtype, name=None):
        self.nbuf += 1
        return self.es.enter_context(self.nc.psum_tensor(name or f"p{self.nbuf}", list(shape), dtype))

    def tok(self, name):
        t = self.toks.get(name)
        if t is None:
            t = self.toks[name] = Tok(name)
        return t

    def _sem_for(self, eng, k):
        ep = (k - 1) // EPOCH
        while len(self.sems[eng]) <= ep:
            self.sems[eng].append(self.es.enter_context(
                self.nc.semaphore(f"s_{eng}_{len(self.sems[eng])}")))
        return self.sems[eng][ep], (k - 1) % EPOCH + 1

    def _need(self, waiter, dep, waits):
        if dep is None:
            return
        if dep[0] == "e":
            _, eng, k = dep
            if eng == waiter and eng == "pe":
                return
            kn = self.known[waiter].get(("e", eng), 0)
            if kn >= k:
                return
            self.known[waiter][("e", eng)] = k
            waits.append(dep)
        else:
            _, si, val = dep
            kn = self.known[waiter].get(("d", si), 0)
            if kn >= val:
                return
            self.known[waiter][("d", si)] = val
            waits.append(dep)

    def _collect(self, eng, reads, writes):
        waits = []
        for t in reads:
            self._need(eng, t.w, waits)
        for t in writes:
            self._need(eng, t.w, waits)
            for d in t.r.values():
                self._need(eng, d, waits)
        best = {}
        for w in waits:
            key = w[:2]
            if key not in best or best[key][2] < w[2]:
                best[key] = w
        return list(best.values())

    def _commit(self, dep, reads, writes):
        for t in reads:
            old = t.r.get(dep[:2])
            if old is None or old[2] < dep[2]:
                t.r[dep[:2]] = dep
        for t in writes:
            t.w = dep
            t.r = {}

    def _toks(self, names):
        return [self.tok(t) if isinstance(t, str) else t for t in names]

    def op(self, eng, fn, reads=(), writes=()):
        reads, writes = self._toks(reads), self._toks(writes)
        waits = self._collect(eng, reads, writes)
        self.cnt[eng] += 1
        k = self.cnt[eng]
        self._sem_for(eng, k)
        rp = _record(fn)
        if eng == "pe":
            self.n_pe += rp.n
        self.ops[eng].append((waits, rp, ("e", eng, k)))
        self._commit(("e", eng, k), reads, writes)

    def dma(self, fn, reads=(), writes=(), eng="sp"):
        reads, writes = self._toks(reads), self._toks(writes)
        waits = self._collect(eng, reads, writes)
        half = len(self.dma_sems) // 2
        cnt = self.dma_cnt.setdefault(eng, 0)
        self.dma_cnt[eng] = cnt + 1
        si = (cnt % half) + (0 if eng == "sp" else half)
        prev = self.dma_val[si]
        if prev:
            self._need(eng, ("d", si, prev), waits)
        self.dma_val[si] = prev + 16
        dep = ("d", si, prev + 16)
        self.ops[eng].append((waits, _record(fn), dep))
        self._commit(dep, reads, writes)

    def mark(self, name):
        self.marks.append((name, self.n_pe))

    def barrier(self):
        deps = [("e", e, self.cnt[e]) for e in ENGS if self.cnt[e]]
        deps += [("d", i, v) for i, v in enumerate(self.dma_val) if v]
        for e in ENGS:
            waits = []
            for d in deps:
                self._need(e, d, waits)
            if waits:
                self.ops[e].append((waits, None, None))

    def final_wait(self, eng="sp"):
        waits = []
        for i, v in enumerate(self.dma_val):
            if v:
                self._need(eng, ("d", i, v), waits)
        for e in ENGS:
            if self.cnt[e]:
                self._need(eng, ("e", e, self.cnt[e]), waits)
        self.ops[eng].append((waits, None, None))

    def _emit_engine(self, ename, e):
        for waits, fn, dep in self.ops[ename]:
            for w in waits:
                if w[0] == "e":
                    sem, val = self._sem_for(w[1], w[2])
                    e.wait_ge(sem, val)
                else:
                    e.wait_ge(self.dma_sems[w[1]], w[2])
            if fn is None:
                continue
            ins = fn(e)
            if dep[0] == "e":
                sem, _ = self._sem_for(dep[1], dep[2])
                ins.then_inc(sem, 1)
            else:
                ins.then_inc(self.dma_sems[dep[1]], 16)

    def emit(self):
        with self.nc.Block() as block:
            @block.tensor
            def _(e):
                self._emit_engine("pe", e)

            @block.scalar
            def _(e):
                self._emit_engine("act", e)

            @block.vector
            def _(e):
                self._emit_engine("dve", e)

            @block.gpsimd
            def _(e):
                self._emit_engine("pool", e)

            @block.sync
            def _(e):
                self._emit_engine("sp", e)
        self.es.close()


def _tile_k(w):
    k, n = w.shape
    return np.ascontiguousarray(w.reshape(k // 128, 128, n).transpose(1, 0, 2))


def _r(a, b):
    return list(range(a, b))


O_AQ, O_AK, O_AV, O_BQ, O_BK, O_BV, O_CQKV, O_CG, O_CA, O_CB, O_DCQ, O_DCKV, O_DKR = (
    0, 256, 384, 512, 768, 1024, 1280, 2048, 2304, 2312, 2320, 2576, 2704)

PANELS_IN = {}


def _def_panels():
    p = {}
    p["G"] = _r(0, 256) + _r(256, 320) * 2 + _r(320, 384) * 2
    p["GT"] = _r(384, 448) * 2 + _r(448, 512) * 2 + _r(O_AK, O_AK + 128)
    p["N"] = _r(O_BQ, O_BQ + 256) + _r(O_BK, O_BK + 256)
    nt = []
    for h in range(4):
        nt += _r(O_BV + 64 * h, O_BV + 64 * h + 64) * 2
    p["NT"] = nt
    p["MT"] = _r(O_DCKV, O_DCKV + 128) + _r(O_DKR, O_DKR + 32) + _r(O_BK, O_BK + 256)
    m = _r(O_DCQ, O_DCQ + 256) + _r(O_DCKV, O_DCKV + 128) + _r(O_DKR, O_DKR + 32) * 3
    p["M"] = m
    dq = []
    for h in range(4):
        dq += _r(O_CQKV + 64 * h, O_CQKV + 64 * h + 64)
    dk = []
    for h in range(4):
        dk += _r(O_CQKV + 256 + 64 * h, O_CQKV + 256 + 64 * h + 64)
    p["D1"] = dq + dk
    p["D2"] = _r(O_CQKV + 512, O_CQKV + 768) + _r(O_CG, O_CG + 256)
    p["DAB"] = _r(O_CA, O_CA + 8) + _r(O_CB, O_CB + 8)
    return p


PANELS_IN = _def_panels()
PANEL_ORDER = ["G", "GT", "N", "NT", "MT", "M", "D1", "D2", "DAB"]
PANEL_OFF = {}
_o = 0
for _n in PANEL_ORDER:
    PANEL_OFF[_n] = _o
    _o += len(PANELS_IN[_n])
NCD = _o

FF = 2816
NFF = 22
FF_SPLIT = [(0, 12), (12, 22)]


def _na_tables(bias):
    c = np.arange(64)
    c0 = np.clip(c - 8, 0, 48)
    kq = c[:, None] - c[None, :]
    ci = np.clip(kq + 15, 0, 30)
    allowed = (c[:, None] >= c0[None, :]) & (c[:, None] < c0[None, :] + 16)
    out = np.full((128, 4, 2, 18, 64), NEG, np.float32)
    for h in range(4):
        for e in range(0, 15):
            dr = 7 - e
            tile = np.where(allowed, bias[h, 14 - e][ci], np.float32(NEG)).astype(np.float32)
            for tab in range(2):
                if tab == 1 and not (-4 <= dr <= 3):
                    continue
                out[0:64, h, tab, e + 1, :] = tile
                out[64:128, h, tab, e + 2, :] = tile
    return out.reshape(128, 4, 2, 18 * 64)


def _rope_tables():
    t = np.arange(1024)
    rowp, colp = (t // 64).astype(np.float64), (t % 64).astype(np.float64)

    def tabs(nd, part0, ndim_total):
        cos = np.ones((ndim_total, 1024)); sin = np.zeros((ndim_total, 1024))
        rot = np.zeros((ndim_total, ndim_total))
        half = nd // 2
        q = half // 2
        inv = 10000.0 ** (-np.arange(q) / q)
        for ax, pos in enumerate((rowp, colp)):
            base = part0 + ax * half
            ang = inv[:, None] * pos[None, :]
            for i in range(q):
                a, b = base + i, base + q + i
                cos[a] = np.cos(ang[i]); cos[b] = np.cos(ang[i])
                sin[a] = np.sin(ang[i]); sin[b] = -np.sin(ang[i])
                rot[a, b] = 1.0
                rot[b, a] = 1.0
        return cos, sin, rot
    cg = np.ones((128, 1024)); sg = np.zeros((128, 1024)); rg = np.zeros((128, 128))
    for hh in range(2):
        c_, s_, r_ = tabs(64, 64 * hh, 128)
        m = slice(64 * hh, 64 * hh + 64)
        cg[m] = c_[m]; sg[m] = s_[m]; rg[m, m] = r_[m, m]
    cm, sm, rm = tabs(32, 64, 128)
    f = np.float32
    return cg.astype(f), sg.astype(f), cm.astype(f), sm.astype(f), rg.astype(f), rm.astype(f)


def _consts():
    cg, sg, cm, sm, rg, rm = _rope_tables()
    ident = np.eye(128, dtype=np.float32)
    ones = np.ones((128, 128), np.float32)
    blk = np.zeros((128, 128), np.float32)
    blk[:64, :64] = 1; blk[64:, 64:] = 1
    p = np.arange(64)[:, None]; fr = np.arange(64)[None, :]
    masks = np.stack([(fr < p), (fr > p), (fr <= p), (fr >= p)]).astype(np.float32)
    masks128 = np.zeros((128, 4, 64), np.float32)
    masks128[:64] = masks.transpose(1, 0, 2)
    return {
        "c_rope": np.ascontiguousarray(np.stack([cg, sg, cm, sm], 1)),
        "c_mats": np.ascontiguousarray(np.stack([ident, ones, blk, rg, rm], 1)),
        "c_masks": masks128,
    }


def prep_inputs(inp):
    f = np.float32
    sh = dict(_consts())
    w_in = inp["w_in"]
    cols = []
    for n in PANEL_ORDER:
        cols += PANELS_IN[n]
    cols = np.asarray(cols)
    sh["w_in_d"] = np.stack([_tile_k(w_in[l][:, cols]) for l in range(NL)])
    sh["w_mod_d"] = np.stack([_tile_k(inp["w_mod"][l]) for l in range(NL)])
    sh["b_mod_d"] = np.ascontiguousarray(np.broadcast_to(inp["b_mod"][:, None, :], (NL, 2, 6144))).astype(f)
    sh["w_out_d"] = np.stack([_tile_k(inp["w_out"][l]) for l in range(NL)])
    gu = []
    for l in range(NL):
        g = _tile_k(inp["ffn_w_gate"][l]).reshape(128, 8, 11, 256)
        u = _tile_k(inp["ffn_w_up"][l]).reshape(128, 8, 11, 256)
        gu.append(np.concatenate([g, u], -1).reshape(128, 8, 11 * 512))
    sh["w_gu_d"] = np.stack(gu)
    sh["w_dn_d"] = np.stack([_tile_k(inp["ffn_w_down"][l]) for l in range(NL)])
    sh["wq_d"] = np.stack([_tile_k(inp["mla_wq_up"][l]) for l in range(NL)])
    wkv = inp["mla_wkv_up"]
    kcols = []
    vcols = []
    for h in range(4):
        kcols += _r(128 * h, 128 * h + 64)
        vcols += _r(128 * h + 64, 128 * h + 128) * 2
    sh["wkv_d"] = np.ascontiguousarray(wkv[:, :, np.asarray(kcols + vcols)])
    def fm(v):
        L_, n = v.shape
        return np.ascontiguousarray(v.reshape(L_, n // 128, 128).transpose(2, 0, 1))
    sh["g1_d"] = fm(inp["norm1_g"]); sh["g2_d"] = fm(inp["norm2_g"])
    sh["gf_d"] = fm(inp["final_g"][None])
    sh["gqg_d"] = np.ascontiguousarray(np.stack([np.tile(inp["gqa_qn_g"], (1, 2)), np.tile(inp["gqa_kn_g"], (1, 2))], -1).transpose(1, 0, 2))
    sh["gqk_row_d"] = np.ascontiguousarray(np.broadcast_to(inp["gqa_kn_g"][None, :, None, :], (128, NL, 2, 64))).astype(f)
    sh["mqg_d"] = fm(inp["mla_qn_g"])
    sh["mkg_d"] = fm(inp["mla_kvn_g"])
    sh["mkg_row_d"] = np.ascontiguousarray(np.broadcast_to(inp["mla_kvn_g"][None], (128, NL, 128))).astype(f)
    cw = inp["dn_conv_w"]
    cq = cw[:, :, 0:512].reshape(NL, 4, 8, 64).transpose(3, 0, 2, 1)
    sh["cwqk_d"] = np.ascontiguousarray(cq)
    cv = cw[:, :, 512:768].reshape(NL, 4, 2, 128).transpose(3, 0, 2, 1)
    sh["cwv_d"] = np.ascontiguousarray(cv)
    sh["dng_row_d"] = np.ascontiguousarray(np.broadcast_to(inp["dn_out_g"][None], (64, NL, 64))).astype(f)
    sh["dnal_d"] = np.ascontiguousarray(np.broadcast_to(inp["dn_a_log"].reshape(1, NL, 8), (64, NL, 8))).astype(f)
    sh["dndt_d"] = np.ascontiguousarray(np.broadcast_to(inp["dn_dt_bias"].reshape(1, NL, 8), (64, NL, 8))).astype(f)
    sh["natab_d"] = np.stack([_na_tables(inp["na_bias"][l]) for l in range(NL)])
    per = []
    for c in range(NCORES):
        d = {}
        d["x_d"] = np.ascontiguousarray(np.concatenate([inp["x_sample"][c], inp["x_prompt"][2 * c], inp["x_prompt"][2 * c + 1]], 0))
        cond = np.stack([inp["c"][c], inp["c_ctx"]], -1)
        d["cond_d"] = np.ascontiguousarray(cond.reshape(8, 128, 2).transpose(1, 0, 2))
        gk = inp["cache_gqa_k"][c]
        d["cgk_d"] = np.ascontiguousarray(gk[:, :, [0, 0, 1, 1], :].reshape(NL, 512, 256))
        gv = inp["cache_gqa_v"][c]
        d["cgv_d"] = np.ascontiguousarray(gv[:, :, [0, 0, 1, 1], :].reshape(NL, 512, 256))
        d["cnk_d"] = np.ascontiguousarray(inp["cache_na_k"][c].reshape(NL, 512, 256))
        nv = inp["cache_na_v"][c]
        d["cnv_d"] = np.ascontiguousarray(nv[:, :, [0, 0, 1, 1, 2, 2, 3, 3], :].reshape(NL, 512, 512))
        d["sdn_d"] = np.ascontiguousarray(inp["state_dn"][c])
        d["cckv_d"] = np.ascontiguousarray(inp["cache_mla_ckv"][c])
        d["ckr_d"] = np.ascontiguousarray(inp["cache_mla_krope"][c])
        per.append(d)
    return sh, per


class Arena:
    def __init__(self, ten, n):
        self.ten = ten
        self.n = n
        self.off = 0

    def reset(self, off=0):
        self.off = off

    def f32(self, n, parts=128):
        assert self.off + n <= self.n, (self.off, n, self.n)
        ap = self.ten[0:parts, self.off:self.off + n]
        self.off += n
        return ap

    def bf16(self, n, parts=128):
        m = (n + 1) // 2
        assert self.off + m <= self.n, (self.off, m, self.n)
        ap = self.ten[0:parts, self.off:self.off + m].bitcast(BF16)
        self.off += m
        return ap[:, 0:n]


def build(stage=99, debug=(), skip=()):
    nc = bass.Bass("TRN2", target_bir_lowering=False)
    P = Prog(nc)
    dbg_outs = {}

    def din(name, shape):
        return nc.dram_tensor(name, list(shape), F32, kind="ExternalInput").ap()

    def dout(name, shape):
        return nc.dram_tensor(name, list(shape), F32, kind="ExternalOutput").ap()

    x_d = din("x_d", [T, D]); cond_d = din("cond_d", [128, 8, 2])
    cgk_d = din("cgk_d", [NL, 512, 256]); cgv_d = din("cgv_d", [NL, 512, 256])
    cnk_d = din("cnk_d", [NL, 512, 256]); cnv_d = din("cnv_d", [NL, 512, 512])
    sdn_d = din("sdn_d", [NL, 2, 4, 64, 64]); cckv_d = din("cckv_d", [NL, 512, 128]); ckr_d = din("ckr_d", [NL, 512, 32])
    c_rope = din("c_rope", [128, 4, 1024]); c_mats = din("c_mats", [128, 5, 128]); c_masks = din("c_masks", [128, 4, 64])
    w_in_d = din("w_in_d", [NL, 128, 8, NCD]); w_mod_d = din("w_mod_d", [NL, 128, 8, 6144]); b_mod_d = din("b_mod_d", [NL, 2, 6144])
    w_out_d = din("w_out_d", [NL, 128, 8, 1024]); w_gu_d = din("w_gu_d", [NL, 128, 8, 5632]); w_dn_d = din("w_dn_d", [NL, 128, 22, 1024])
    wq_d = din("wq_d", [NL, 128, 2, 384]); wkv_d = din("wkv_d", [NL, 128, 768])
    g1_d = din("g1_d", [128, NL, 8]); g2_d = din("g2_d", [128, NL, 8]); gf_d = din("gf_d", [128, 1, 8])
    gqg_d = din("gqg_d", [128, NL, 2]); gqk_row_d = din("gqk_row_d", [128, NL, 2, 64])
    mqg_d = din("mqg_d", [128, NL, 2]); mkg_d = din("mkg_d", [128, NL, 1]); mkg_row_d = din("mkg_row_d", [128, NL, 128])
    cwqk_d = din("cwqk_d", [64, NL, 8, 4]); cwv_d = din("cwv_d", [128, NL, 2, 4])
    dng_row_d = din("dng_row_d", [64, NL, 64]); dnal_d = din("dnal_d", [64, NL, 8]); dndt_d = din("dndt_d", [64, NL, 8])
    natab_d = din("natab_d", [NL, 128, 4, 2, 1152])

    y_o = dout("y_o", [T, D])
    ngk_o = dout("ngk_o", [2, NL, 256, 128]); ngv_o = dout("ngv_o", [2, NL, 256, 128])
    nnk_o = dout("nnk_o", [2, NL, 256, 256]); nnv_o = dout("nnv_o", [2, NL, 256, 256])
    ndn_o = dout("ndn_o", [2, NL, 2, 4, 64, 64]); nckv_o = dout("nckv_o", [2, NL, 256, 128]); nkr_o = dout("nkr_o", [2, NL, 256, 32])

    xT = P.sb([128, 8, T], F32, "xT")
    hT = P.sb([128, 8, T], BF16, "hT")
    mixT = P.sb([128, 8, T], BF16, "mixT")
    NSLOT = 3
    slots = [P.sb([128, 4096], BF16, f"wslot{i}") for i in range(NSLOT)]
    rope = P.sb([128, 4, 1024], BF16, "rope")
    matsb = P.sb([128, 5, 128], BF16, "matsb")
    matsf = P.sb([128, 2, 128], F32, "matsf")
    masks = P.sb([128, 4, 64], F32, "masks")
    modsT = P.sb([128, NL, 48, 2], F32, "modsT")
    gsb = P.sb([128, NL, 2, 8, 2], F32, "gsb")
    g12 = P.sb([128, 2, NL, 8], F32, "g12")
    gfin = P.sb([128, 8], F32, "gfin")
    smallp = P.sb([128, NL, 8], F32, "smallp")
    scT = P.sb([128, 16], BF16, "scT")
    ARENA_N = 19200
    arena_t = P.sb([128, ARENA_N], F32, "arena")
    A = Arena(arena_t, ARENA_N)
    banks = [P.ps([128, 512], F32, f"bank{i}") for i in range(8)]

    ident_b = matsb[:, 0, :]; ones_b = matsb[:, 1, :]; blk_b = matsb[:, 2, :]; rotG = matsb[:, 3, :]; rotM = matsb[:, 4, :]
    ident_f = matsf[:, 0, :]; ones_f = matsf[:, 1, :]
    cosG = rope[:, 0, :]; sinG = rope[:, 1, :]; cosM = rope[:, 2, :]; sinM = rope[:, 3, :]

    def bank(kind="s"):
        if kind == "s":
            i = P.bank_i["s"] % 5
            P.bank_i["s"] += 1
        else:
            i = 5 + P.bank_i["a"] % 3
            P.bank_i["a"] += 1
        return banks[i], f"bank{i}"

    def grp_of_blk(b):
        return 0 if b < 2 else 1

    def blk(b):
        return slice(b * 512, (b + 1) * 512)

    sched = []

    def add_panel(key, ap, nk, ncols):
        sched.append((key, ap, nk, ncols))

    MODS_TODO = [(0, j) for j in range(4, 12)] + [(l2, j) for l2 in range(1, NL) for j in range(12)]
    MODS_AFTER = {"GT": (0, 10), "MT": (10, 20)}
    mods_pos = [0]
    for j in range(4):
        add_panel(("mod", 0, j), w_mod_d[0][:, :, j * 512:(j + 1) * 512], 8, 512)
    for l in range(NL):
        for n in ["G", "GT", "MODS", "N", "NT", "MT", "M", "D1", "D2", "DAB", "D1", "D2", "DAB"]:
            if n == "MODS":
                continue
            o = PANEL_OFF[n]
            add_panel(("in", l, n), w_in_d[l][:, :, o:o + len(PANELS_IN[n])], 8, len(PANELS_IN[n]))
            if l == 0 and n in MODS_AFTER:
                a_, b_ = MODS_AFTER[n]
                for (l2, j) in MODS_TODO[a_:b_]:
                    add_panel(("mod", l2, j), w_mod_d[l2][:, :, j * 512:(j + 1) * 512], 8, 512)
        for j in range(2):
            add_panel(("out", l, j), w_out_d[l][:, :, j * 512:(j + 1) * 512], 8, 512)
        for hf, (f0, f1) in enumerate(FF_SPLIT):
            for j in range(f0 // 2, f1 // 2):
                add_panel(("gu", l, j), w_gu_d[l][:, :, j * 512:(j + 1) * 512], 8, 512)
            for q in range(4):
                add_panel(("dn", l, hf, q), w_dn_d[l][:, f0:f1, q * 256:(q + 1) * 256], f1 - f0, 256)
    pstate = {"issued": 0, "next": 0}
    LOOKAHEAD = 2

    def _issue_panel(i):
        key, ap, nk, ncols = sched[i]
        s = i % NSLOT
        dst = slots[s][:, 0:nk * ncols].rearrange("p (k n) -> p k n", k=nk)
        P.dma(lambda e, dst=dst, ap=ap: e.dma_start(out=dst, in_=ap), writes=[f"wslot{s}"], eng="pool")

    def panel(key):
        i = pstate["next"]
        assert sched[i][0] == key, (sched[i][0], key)
        while pstate["issued"] < min(len(sched), i + LOOKAHEAD):
            _issue_panel(pstate["issued"])
            pstate["issued"] += 1
        pstate["next"] += 1
        s = i % NSLOT
        _, _, nk, ncols = sched[i]
        return slots[s][:, 0:nk * ncols].rearrange("p (k n) -> p k n", k=nk), f"wslot{s}"

    def dbg(name, ap, shape, toks):
        if name not in debug:
            return
        o = dout("dbg_" + name, shape)
        P.dma(lambda e: e.dma_start(out=o, in_=ap), reads=toks, eng="pool")
        dbg_outs[name] = shape

    P.dma(lambda e: e.dma_start(out=rope[:], in_=c_rope), writes=["rope"], eng="pool")
    P.dma(lambda e: e.dma_start(out=matsb[:], in_=c_mats), writes=["matsb"], eng="pool")
    P.dma(lambda e: e.dma_start(out=matsf[:], in_=c_mats[:, 0:2, :]), writes=["matsf"])
    P.dma(lambda e: e.dma_start(out=masks[:], in_=c_masks), writes=["masks"])
    P.dma(lambda e: e.dma_start(out=g12[:, 0], in_=g1_d), writes=["g12"])
    P.dma(lambda e: e.dma_start(out=g12[:, 1], in_=g2_d), writes=["g12"])
    P.dma(lambda e: e.dma_start(out=gfin[:], in_=gf_d[:, 0, :]), writes=["gfin"])
    P.dma(lambda e: e.dma_start(out=smallp[:, :, 0:2], in_=gqg_d, allow_slow_non_contiguous=True), writes=["smallp"])
    P.dma(lambda e: e.dma_start(out=smallp[:, :, 2:4], in_=mqg_d, allow_slow_non_contiguous=True), writes=["smallp"])
    P.dma(lambda e: e.dma_start(out=smallp[:, :, 4:5], in_=mkg_d, allow_slow_non_contiguous=True), writes=["smallp"])
    CONST = ["rope", "matsb", "matsf", "masks", "smallp"]

    A.reset()
    xin = [A.f32(1024), A.f32(1024)]
    for t in range(12):
        xi = xin[t % 2]
        tk = f"xin{t % 2}"
        P.dma(lambda e, xi=xi, t=t: e.dma_start(out=xi, in_=x_d[t * 128:(t + 1) * 128, :]), writes=[tk])
        for half in range(2):
            bk, bt = bank()
            def tr(e, xi=xi, bk=bk, half=half):
                ins = None
                for c in range(4):
                    cc = half * 4 + c
                    ins = e.transpose(bk[:, c * 128:(c + 1) * 128], xi[:, cc * 128:(cc + 1) * 128], ident_f)
                return ins
            P.op("pe", tr, reads=[tk, "matsf"], writes=[bt])
            dst = xT[:, half * 4:half * 4 + 4, t * 128:(t + 1) * 128]
            src = bk[:, :].rearrange("p (c n) -> p c n", c=4)
            eng = "act" if half == 0 else "dve"
            if eng == "act":
                P.op("act", lambda e, dst=dst, src=src: e.activation(out=dst, in_=src, func=ACTF.Copy), reads=[bt], writes=[f"xT{t // 4}"])
            else:
                P.op("dve", lambda e, dst=dst, src=src: e.tensor_copy(out=dst, in_=src), reads=[bt], writes=[f"xT{t // 4}"])

    P.mark("mods")
    condf = A.f32(16); tmp16 = A.f32(16)
    condf3 = condf.rearrange("p (k g) -> p k g", k=8); scT3 = scT[:, :].rearrange("p (k g) -> p k g", k=8)
    P.dma(lambda e: e.dma_start(out=condf3, in_=cond_d), writes=["condf"])
    P.op("act", lambda e: e.activation(out=tmp16, in_=condf, func=ACTF.Exp, scale=-1.0), reads=["condf"], writes=["tmp16"])
    P.op("dve", lambda e: e.tensor_scalar_add(out=tmp16, in0=tmp16, scalar1=1.0), reads=["tmp16"], writes=["tmp16"])
    P.op("dve", lambda e: e.reciprocal(out=tmp16, in_=tmp16), reads=["tmp16"], writes=["tmp16"])
    P.op("dve", lambda e: e.tensor_tensor(out=scT[:, :], in0=condf, in1=tmp16, op=ALU.mult), reads=["tmp16", "condf"], writes=["scT"])
    modbuf = {"rowm": [A.f32(512, parts=2), A.f32(512, parts=2)], "bmr": [A.f32(512, parts=2), A.f32(512, parts=2)]}

    def mods_step(l, j):
        rowm, bmr = modbuf["rowm"], modbuf["bmr"]
        wp, wt = panel(("mod", l, j))
        bk, bt = bank()
        i2 = j % 2
        P.dma(lambda e: e.dma_start(out=bmr[i2], in_=b_mod_d[l][:, j * 512:(j + 1) * 512]), writes=[f"bmr{i2}"])
        def mm(e):
            ins = None
            for k in range(8):
                ins = e.matmul(bk[0:2, :], lhsT=scT3[:, k, :], rhs=wp[:, k, :], start=(k == 0), stop=(k == 7))
            return ins
        P.op("pe", mm, reads=[wt, "scT"], writes=[bt])
        P.op("dve", lambda e: e.tensor_tensor(out=rowm[i2], in0=bk[0:2, :], in1=bmr[i2], op=ALU.add), reads=[bt, f"bmr{i2}"], writes=[f"rowm{i2}"])
        bk2, bt2 = bank()
        def trm(e):
            ins = None
            for c in range(4):
                ins = e.matmul(bk2[:, 2 * c:2 * c + 2], lhsT=rowm[i2][:, c * 128:(c + 1) * 128], rhs=ident_f[0:2, 0:2], start=True, stop=True)
            return ins
        P.op("pe", trm, reads=[f"rowm{i2}", "matsf"], writes=[bt2])
        P.op("act", lambda e: e.activation(out=modsT[:, l, 4 * j:4 * j + 4, :], in_=bk2[:, 0:8].rearrange("p (c g) -> p c g", c=4), func=ACTF.Copy), reads=[bt2], writes=[f"modsT{l}"])

    def mods_finish(l):
        for w, base in ((0, 8), (1, 32)):
            P.op("dve", lambda e, w=w, base=base: e.tensor_scalar_add(out=gsb[:, l, w], in0=modsT[:, l, base:base + 8, :], scalar1=1.0), reads=[f"modsT{l}"], writes=[f"gsb{l}"])
            P.op("dve", lambda e, w=w: e.tensor_tensor(out=gsb[:, l, w], in0=gsb[:, l, w], in1=g12[:, w, l, :].unsqueeze(2).to_broadcast([128, 8, 2]), op=ALU.mult), reads=[f"gsb{l}", "g12"], writes=[f"gsb{l}"])

    def mods_gs(l, which):
        w, base = ((0, 8), (1, 32))[which]
        P.op("dve", lambda e: e.tensor_scalar_add(out=gsb[:, l, w], in0=modsT[:, l, base:base + 8, :], scalar1=1.0), reads=[f"modsT{l}"], writes=[f"gsb{l}"])
        P.op("dve", lambda e: e.tensor_tensor(out=gsb[:, l, w], in0=gsb[:, l, w], in1=g12[:, w, l, :].unsqueeze(2).to_broadcast([128, 8, 2]), op=ALU.mult), reads=[f"gsb{l}", "g12"], writes=[f"gsb{l}"])

    def mods_tick(upto):
        while mods_pos[0] < upto:
            l2, j = MODS_TODO[mods_pos[0]]
            mods_pos[0] += 1
            mods_step(l2, j)
            if (l2, j) == (0, 11):
                mods_gs(0, 1)
            if j == 11 and l2 >= 1:
                mods_finish(l2)

    for j in range(4):
        mods_step(0, j)
    mods_gs(0, 0)
    dbg("modsT", modsT[:], [128, NL, 48, 2], ["modsT0", "modsT1"])
    dbg("xT", xT[:], [128, 8, T], ["xT0", "xT1", "xT2"])

    def rstd_from_ss(ps_ap, n, dst, rtoks, wtok):
        P.op("act", lambda e: e.activation(out=dst, in_=ps_ap, func=ACTF.Ln, bias=EPS, scale=1.0 / n), reads=rtoks, writes=[wtok])
        P.op("act", lambda e: e.activation(out=dst, in_=dst, func=ACTF.Exp, scale=-0.5), reads=[wtok], writes=[wtok])

    def norm_fm(gs_ap_fn, shift_ap_fn, out_fn, out_tok_fn, tmpn):
        sq = [A.bf16(512), A.bf16(512)]
        rst = A.f32(512)
        tmpf = [A.f32(512), A.f32(512)]
        for b in range(NB):
            g = grp_of_blk(b)
            bk, bt = bank("a")
            for c in range(8):
                s = sq[c % 2]
                P.op("act", lambda e, s=s, c=c, b=b: e.activation(out=s, in_=xT[:, c, blk(b)], func=ACTF.Square), reads=[f"xT{b}"], writes=[f"{tmpn}sq{c % 2}"])
                P.op("pe", lambda e, s=s, c=c, bk=bk: e.matmul(bk[:, :], lhsT=ones_b, rhs=s, start=(c == 0), stop=(c == 7)), reads=[f"{tmpn}sq{c % 2}", "matsb"], writes=[bt])
            rstd_from_ss(bk[:, :], float(D), rst, [bt], f"{tmpn}rst")
            for c in range(8):
                tf = tmpf[c % 2]
                P.op("dve", lambda e, tf=tf, c=c, b=b: e.tensor_tensor(out=tf, in0=xT[:, c, blk(b)], in1=rst, op=ALU.mult), reads=[f"xT{b}", f"{tmpn}rst"], writes=[f"{tmpn}tf{c % 2}"])
                o = out_fn(c, b)
                if shift_ap_fn is not None:
                    P.op("act", lambda e, tf=tf, o=o, c=c, g=g: e.activation(out=o, in_=tf, func=ACTF.Identity, bias=shift_ap_fn(c, g), scale=gs_ap_fn(c, g)), reads=[f"{tmpn}tf{c % 2}", "gsb0", "gsb1", "modsT0", "modsT1", "gfin"], writes=[out_tok_fn(b)])
                else:
                    P.op("act", lambda e, tf=tf, o=o, c=c, g=g: e.activation(out=o, in_=tf, func=ACTF.Copy, scale=gs_ap_fn(c, g)), reads=[f"{tmpn}tf{c % 2}", "gsb0", "gsb1", "modsT0", "modsT1", "gfin"], writes=[out_tok_fn(b)])

    def proj_fm(wp, wt, c0, m, b, kind="s"):
        bk, bt = bank(kind)
        def f(e):
            ins = None
            for k in range(8):
                ins = e.matmul(bk[0:m, :], lhsT=wp[:, k, c0:c0 + m], rhs=hT[:, k, blk(b)], start=(k == 0), stop=(k == 7))
            return ins
        P.op("pe", f, reads=[wt, f"hT{b}"], writes=[bt])
        return bk, bt

    def proj_tm(wp, wt, c0, n, t0, m=128, kind="s", bk_bt=None, col0=0):
        bk, bt = bk_bt if bk_bt is not None else bank(kind)
        def f(e):
            ins = None
            for k in range(8):
                ins = e.matmul(bk[0:m, col0:col0 + n], lhsT=hT[:, k, t0:t0 + m], rhs=wp[:, k, c0:c0 + n], start=(k == 0), stop=(k == 7))
            return ins
        P.op("pe", f, reads=[wt, f"hT{t0 // 512}"], writes=[bt])
        return bk, bt

    HT = ["hT0", "hT1", "hT2"]
    MIXT = ["mixT0", "mixT1", "mixT2"]

    def attend(QT, nq, chunks, dst, half, rd_toks, wr_tok, tmp):
        bo, bot = bank("a"); bd, bdt = bank("a")
        n = len(chunks)
        pendq = []
        sl = slice(half * 64, half * 64 + 64)
        for i, (KT, V, brhs) in enumerate(chunks):
            bs, bst = bank()
            def qk(e, KT=KT, bs=bs, brhs=brhs):
                ins = e.matmul(bs[:, 0:nq], lhsT=KT, rhs=QT, start=True, stop=(brhs is None))
                if brhs is not None:
                    ins = e.matmul(bs[:, 0:nq], lhsT=ident_b, rhs=brhs, start=False, stop=True)
                return ins
            P.op("pe", qk, reads=rd_toks + ["matsb"], writes=[bst])
            if len(pendq) >= 2:
                pendq.pop(0)()
            pt = tmp["pt"][i % len(tmp["pt"])]
            ptt = f"{tmp['name']}pt{i % len(tmp['pt'])}"
            P.op("act", lambda e, pt=pt, bs=bs: e.activation(out=pt[:, 0:nq], in_=bs[:, 0:nq], func=ACTF.Exp), reads=[bst], writes=[ptt])
            acc, acct = tmp["acc"], tmp["acc_tok"]
            if i == 0:
                P.op("dve", lambda e, pt=pt: e.tensor_copy(out=acc[:, 0:nq], in_=pt[:, 0:nq]), reads=[ptt], writes=[acct])
            else:
                P.op("dve", lambda e, pt=pt: e.tensor_tensor(out=acc[:, 0:nq], in0=acc[:, 0:nq], in1=pt[:, 0:nq], op=ALU.add), reads=[ptt, acct], writes=[acct])
            def pv(i=i, V=V, pt=pt, ptt=ptt):
                P.op("pe", lambda e: e.matmul(bo[:, 0:nq], lhsT=V, rhs=pt[:, 0:nq], start=(i == 0), stop=(i == n - 1)), reads=rd_toks + [ptt], writes=[bot])
            pendq.append(pv)
        for f_ in pendq:
            f_()
        hi, lo = tmp["hl"]
        hit, lot = tmp["name"] + "hi", tmp["name"] + "lo"
        P.op("act", lambda e: e.activation(out=hi[:, 0:nq], in_=acc[:, 0:nq], func=ACTF.Copy), reads=[acct], writes=[hit])
        P.op("dve", lambda e: e.tensor_tensor(out=lo[:, 0:nq], in0=acc[:, 0:nq], in1=hi[:, 0:nq], op=ALU.subtract), reads=[acct, hit], writes=[lot])
        def denf(e):
            e.matmul(bd[:, 0:nq], lhsT=ones_b, rhs=hi[:, 0:nq], start=True, stop=False)
            return e.matmul(bd[:, 0:nq], lhsT=ones_b, rhs=lo[:, 0:nq], start=False, stop=True)
        P.op("pe", denf, reads=[hit, lot, "matsb"], writes=[bdt])
        rd = tmp["rd"]
        P.op("dve", lambda e: e.reciprocal(out=rd[sl, 0:nq], in_=bd[sl, 0:nq]), reads=[bdt], writes=[tmp["name"] + "rd"])
        P.op("dve", lambda e: e.tensor_tensor(out=dst, in0=bo[sl, 0:nq], in1=rd[sl, 0:nq], op=ALU.mult), reads=[bot, tmp["name"] + "rd"], writes=[wr_tok])

    ARENA0 = A.off

    def load_cache_T(src_dram, ncol, dstT_fn, name, pad_to=None):
        pass

    for l in range(NL):
        if stage < 1:
            break
        P.barrier()
        A.reset()
        P.mark(f"L{l} norm1")
        norm_fm(lambda c, g: gsb[:, l, 0, c, g:g + 1], lambda c, g: modsT[:, l, c, g:g + 1],
                lambda c, b: hT[:, c, blk(b)], lambda b: f"hT{b}", f"n1_{l}")
        if l == 0:
            dbg("hT", hT[:], [128, 8, T], HT)
        if stage < 2:
            break
        P.barrier()
        A.reset()

        P.mark(f"L{l} gqa")
        QT = A.bf16(2 * T).rearrange("p (c t) -> p c t", c=2)
        KT = A.bf16(2 * 2048).rearrange("p (c t) -> p c t", c=2)
        Vg = A.bf16(16 * 256).rearrange("p (t c) -> p t c", t=16)
        ctm = A.bf16(4 * 256).rearrange("p (t c) -> p t c", t=4)
        tqs = [{"sq": A.bf16(512), "f1": A.f32(512), "f2": A.f32(512), "xc": A.bf16(512), "xs": A.bf16(512), "n": f"tq{i}_"} for i in range(2)]
        tqi = [0]
        tq = tqs[0]
        atmp = {"name": f"ga{l}", "pt": [A.bf16(512), A.bf16(512), A.bf16(512)], "rd": A.f32(512), "acc": A.f32(512), "acc_tok": "att_acc", "hl": (A.bf16(512), A.bf16(512))}
        otile = [A.f32(256), A.f32(256)]
        krow = A.f32(2 * 64).rearrange("p (h d) -> p h d", h=2)
        P.dma(lambda e: e.dma_start(out=krow, in_=gqk_row_d[:, l]), writes=["krow"])
        if l == 0:
            modbuf["rowm"] = [A.f32(512, parts=2), A.f32(512, parts=2)]
            modbuf["bmr"] = [A.f32(512, parts=2), A.f32(512, parts=2)]
        P.dma(lambda e: e.dma_start(out=ctm, in_=cgk_d[l].rearrange("(t p) c -> p t c", p=128)), writes=["ctm"], eng="pool")
        P.dma(lambda e: e.dma_start(out=Vg[:, 0:4, :], in_=cgv_d[l].rearrange("(t p) c -> p t c", p=128)), writes=["Vg"], eng="pool")
        for kv in range(2):
            bk, bt = bank()
            bkb = bk[:, :].bitcast(BF16)
            def trc(e, kv=kv, bkb=bkb):
                ins = None
                for t in range(4):
                    ins = e.transpose(bkb[:, t * 128:(t + 1) * 128], ctm[:, t, kv * 128:(kv + 1) * 128], ident_b)
                return ins
            P.op("pe", trc, reads=["ctm", "matsb"], writes=[bt])
            P.op("act", lambda e, kv=kv, bkb=bkb: e.activation(out=KT[:, kv, 0:512], in_=bkb[:, 0:512], func=ACTF.Copy), reads=[bt], writes=["KT"])

        def qk_chunk(bk, bt, gain_ap, extra, do_rope, dst, t0, wtok, nparts=128, blkmat=None, cos=None, sin=None, rot=None, n=512):
            blkmat = blk_b if blkmat is None else blkmat
            tqi[0] += 1
            tq = tqs[tqi[0] % 2]
            tn = tq["n"]
            P.op("act", lambda e: e.activation(out=tq["sq"][:, 0:n], in_=bk[:, 0:n], func=ACTF.Square), reads=[bt], writes=[tn + "sq"])
            b2, b2t = bank()
            P.op("pe", lambda e: e.matmul(b2[:, 0:n], lhsT=blkmat, rhs=tq["sq"][:, 0:n], start=True, stop=True), reads=[tn + "sq", "matsb"], writes=[b2t])
            rstd_from_ss(b2[:, 0:n], 64.0, tq["f1"][:, 0:n], [b2t], tn + "f1")
            P.op("dve", lambda e: e.tensor_tensor(out=tq["f2"][:, 0:n], in0=bk[:, 0:n], in1=tq["f1"][:, 0:n], op=ALU.mult), reads=[bt, tn + "f1"], writes=[tn + "f2"])
            if not do_rope:
                P.op("dve", lambda e: e.tensor_scalar(out=dst, in0=tq["f2"][:, 0:n], scalar1=gain_ap, scalar2=extra, op0=ALU.mult, op1=ALU.mult), reads=[tn + "f2", "smallp"], writes=[wtok])
                return
            P.op("dve", lambda e: e.tensor_scalar(out=tq["f2"][:, 0:n], in0=tq["f2"][:, 0:n], scalar1=gain_ap, scalar2=extra, op0=ALU.mult, op1=ALU.mult), reads=[tn + "f2", "smallp"], writes=[tn + "f2"])
            rope_apply(tq["f2"][:, 0:n], dst, t0, wtok, cosG, sinG, rotG, [tn + "f2"], n, tq=tq)

        def rope_apply(src, dst, t0, wtok, cos, sin, rot, rtoks, n, parts=128, pre=None, tq=None):
            if tq is None:
                tqi[0] += 1
                tq = tqs[tqi[0] % 2]
            tn = tq["n"]
            xc = tq["xc"][0:parts, 0:n]; xs = tq["xs"][0:parts, 0:n]
            if pre is None:
                P.op("dve", lambda e: e.tensor_tensor(out=xc, in0=src, in1=cos[0:parts, t0:t0 + n], op=ALU.mult), reads=rtoks + ["rope"], writes=[tn + "xc"])
                P.op("dve", lambda e: e.tensor_tensor(out=xs, in0=src, in1=sin[0:parts, t0:t0 + n], op=ALU.mult), reads=rtoks + ["rope"], writes=[tn + "xs"])
            else:
                P.op("dve", lambda e: e.scalar_tensor_tensor(out=xc, in0=src, scalar=pre, in1=cos[0:parts, t0:t0 + n], op0=ALU.mult, op1=ALU.mult), reads=rtoks + ["rope"], writes=[tn + "xc"])
                P.op("dve", lambda e: e.scalar_tensor_tensor(out=xs, in0=src, scalar=pre, in1=sin[0:parts, t0:t0 + n], op0=ALU.mult, op1=ALU.mult), reads=rtoks + ["rope"], writes=[tn + "xs"])
            b3, b3t = bank()
            def f(e):
                e.matmul(b3[0:parts, 0:n], lhsT=ident_b[0:parts, 0:parts], rhs=xc, start=True, stop=False)
                return e.matmul(b3[0:parts, 0:n], lhsT=rot[0:parts, 0:parts], rhs=xs, start=False, stop=True)
            P.op("pe", f, reads=[tn + "xc", tn + "xs", "matsb"], writes=[b3t])
            P.op("act", lambda e: e.activation(out=dst, in_=b3[0:parts, 0:n], func=ACTF.Copy), reads=[b3t], writes=[wtok])

        wp, wt = panel(("in", l, "G"))
        for b in range(NB):
            smp = b < 2
            for c in range(2):
                bk, bt = proj_fm(wp, wt, c * 128, 128, b)
                qk_chunk(bk, bt, smallp[:, l, 0:1], 0.125, smp, QT[:, c, blk(b)], b * 512, "QT")
            for kv in range(2):
                bk, bt = proj_fm(wp, wt, 256 + kv * 128, 128, b)
                qk_chunk(bk, bt, smallp[:, l, 1:2], 1.0, smp, KT[:, kv, 512 + b * 512:1024 + b * 512], b * 512, "KT")
        wp, wt = panel(("in", l, "GT"))
        for tt in range(12):
            pr = tt >= 8
            bk, bt = proj_tm(wp, wt, 0, 384 if pr else 256, tt * 128)
            P.op("act", lambda e, bk=bk, tt=tt: e.activation(out=Vg[:, 4 + tt, :], in_=bk[:, 0:256], func=ACTF.Copy), reads=[bt], writes=["Vg"])
            if pr:
                sq_, tl = tt - 8, otile[tt % 2]
                tn = f"otile{tt % 2}"
                seq, half = sq_ // 2, sq_ % 2
                P.op("dve", lambda e, bk=bk, tl=tl: e.tensor_copy(out=tl[:, 0:128].rearrange("p (h d) -> p h d", h=2), in_=bk[:, 0:256].rearrange("p (h r d) -> p h r d", h=2, r=2)[:, :, 0, :]), reads=[bt], writes=[tn])
                P.dma(lambda e, tl=tl, seq=seq, half=half: e.dma_start(out=ngv_o[seq, l, half * 128:(half + 1) * 128, :], in_=tl[:, 0:128]), reads=[tn])
                P.op("act", lambda e, bk=bk: e.activation(out=tq["f1"][:, 0:128], in_=bk[:, 256:384], func=ACTF.Square), reads=[bt], writes=["tq0_f1"])
                P.op("dve", lambda e: e.tensor_reduce(out=tq["f2"][:, 0:2], in_=tq["f1"][:, 0:128].rearrange("p (h d) -> p h d", h=2), axis=AX.X, op=ALU.add), reads=["tq0_f1"], writes=["tq0_f2"])
                rstd_from_ss(tq["f2"][:, 0:2], 64.0, tq["f2"][:, 0:2], ["tq0_f2"], "tq0_f2")
                P.op("dve", lambda e, bk=bk, tl=tl: e.tensor_tensor(out=tl[:, 128:256].rearrange("p (h d) -> p h d", h=2), in0=bk[:, 256:384].rearrange("p (h d) -> p h d", h=2), in1=tq["f2"][:, 0:2].unsqueeze(2).to_broadcast([128, 2, 64]), op=ALU.mult), reads=[bt, "tq0_f2"], writes=[tn])
                P.op("dve", lambda e, tl=tl: e.tensor_tensor(out=tl[:, 128:256].rearrange("p (h d) -> p h d", h=2), in0=tl[:, 128:256].rearrange("p (h d) -> p h d", h=2), in1=krow, op=ALU.mult), reads=[tn, "krow"], writes=[tn])
                P.dma(lambda e, tl=tl, seq=seq, half=half: e.dma_start(out=ngk_o[seq, l, half * 128:(half + 1) * 128, :], in_=tl[:, 128:256]), reads=[tn])
        for h in range(4):
            c, half = h // 2, h % 2
            kv = h // 2
            rows = slice(half * 64, half * 64 + 64)
            for qb in range(2):
                chunks = [(KT[rows, kv, i * 128:(i + 1) * 128], Vg[:, i, kv * 128:(kv + 1) * 128], None) for i in range(12)]
                attend(QT[rows, c, blk(qb)], 512, chunks, mixT[rows, c, blk(qb)], half, ["QT", "KT", "Vg"], f"mixT{qb}", atmp)
                if l == 0:
                    mods_tick(min(10, ((h * 2 + qb + 1) * 10) // 8))
            for s in range(2):
                t0 = 1024 + 256 * s
                chunks = [(KT[rows, kv, 512 + t0 + i * 128:512 + t0 + (i + 1) * 128], Vg[:, 12 + 2 * s + i, kv * 128:(kv + 1) * 128], None) for i in range(2)]
                attend(QT[rows, c, t0:t0 + 256], 256, chunks, mixT[rows, c, t0:t0 + 256], half, ["QT", "KT", "Vg"], "mixT2", atmp)
        if l == 0:
            dbg("oa", mixT[:, 0:2, :], [128, 2, T], MIXT)
        if stage < 3:
            break
        P.barrier()
        A.reset()

        P.mark(f"L{l} na")
        QT = A.bf16(2 * T).rearrange("p (c t) -> p c t", c=2)
        KT = A.bf16(2 * 2048).rearrange("p (c t) -> p c t", c=2)
        Vn = A.bf16(16 * 512).rearrange("p (t c) -> p t c", t=16)
        ctm = A.bf16(4 * 256).rearrange("p (t c) -> p t c", t=4)
        natab = A.bf16(4 * 2 * 1152).rearrange("p (h t n) -> p h t n", h=4, t=2)
        atmp = {"name": f"na{l}", "pt": [A.bf16(512), A.bf16(512), A.bf16(512)], "rd": A.f32(512), "acc": A.f32(512), "acc_tok": "att_acc", "hl": (A.bf16(512), A.bf16(512))}
        otile = [A.f32(256), A.f32(256)]
        if l == 0:
            modbuf["rowm"] = [A.f32(512, parts=2), A.f32(512, parts=2)]
            modbuf["bmr"] = [A.f32(512, parts=2), A.f32(512, parts=2)]
        if "ntl" not in skip:
            P.dma(lambda e: e.dma_start(out=natab, in_=natab_d[l]), writes=["natab"], eng="pool")
        if "nch" not in skip:
            P.dma(lambda e: e.dma_start(out=ctm, in_=cnk_d[l].rearrange("(t p) c -> p t c", p=128)), writes=["ctm"], eng="pool")
            P.dma(lambda e: e.dma_start(out=Vn[:, 0:4, :], in_=cnv_d[l].rearrange("(t p) c -> p t c", p=128)), writes=["Vn"], eng="pool")
        for c in (range(2) if "nch" not in skip else []):
            bk, bt = bank()
            bkb = bk[:, :].bitcast(BF16)
            def trc(e, c=c, bkb=bkb):
                ins = None
                for t in range(4):
                    ins = e.transpose(bkb[:, t * 128:(t + 1) * 128], ctm[:, t, c * 128:(c + 1) * 128], ident_b)
                return ins
            P.op("pe", trc, reads=["ctm", "matsb"], writes=[bt])
            P.op("act", lambda e, c=c, bkb=bkb: e.activation(out=KT[:, c, 0:512], in_=bkb[:, 0:512], func=ACTF.Copy), reads=[bt], writes=["KT"])
        wp, wt = panel(("in", l, "N"))
        for b in (range(NB) if "npj" not in skip else []):
            for c in range(2):
                bk, bt = proj_fm(wp, wt, c * 128, 128, b)
                P.op("act", lambda e, bk=bk, c=c, b=b: e.activation(out=QT[:, c, blk(b)], in_=bk[:, :], func=ACTF.Copy, scale=0.125), reads=[bt], writes=["QT"])
                bk, bt = proj_fm(wp, wt, 256 + c * 128, 128, b)
                P.op("dve", lambda e, bk=bk, c=c, b=b: e.tensor_copy(out=KT[:, c, 512 + b * 512:1024 + b * 512], in_=bk[:, :]), reads=[bt], writes=["KT"])
        wp, wt = panel(("in", l, "NT"))
        for tt in (range(12) if "nvp" not in skip else []):
            bk, bt = proj_tm(wp, wt, 0, 512, tt * 128)
            if "nvc" not in skip:
                P.op("act", lambda e, bk=bk, tt=tt: e.activation(out=Vn[:, 4 + tt, :], in_=bk[:, :], func=ACTF.Copy), reads=[bt], writes=["Vn"])
            if tt >= 8 and "nvo" not in skip:
                sq_, tl = tt - 8, otile[tt % 2]
                tn = f"otile{tt % 2}"
                seq, half = sq_ // 2, sq_ % 2
                if "nvo1" not in skip:
                    P.op("act", lambda e, bk=bk, tl=tl: e.activation(out=tl[:, 0:256].rearrange("p (h d) -> p h d", h=4), in_=bk[:, 0:512].rearrange("p (h r d) -> p h r d", h=4, r=2)[:, :, 0, :], func=ACTF.Copy), reads=[bt], writes=[tn])
                if "nvo2" not in skip:
                    P.dma(lambda e, tl=tl, seq=seq, half=half: e.dma_start(out=nnv_o[seq, l, half * 128:(half + 1) * 128, :], in_=tl[:, 0:256]), reads=[tn])
        wp, wt = panel(("in", l, "MT"))
        MT_RANGE = range(8, 12) if "mt" not in skip else range(0)
        mrow = A.f32(128)
        P.dma(lambda e: e.dma_start(out=mrow, in_=mkg_row_d[:, l]), writes=["mrow"])
        tf1 = A.f32(128); tf2 = A.f32(2)
        for tt in MT_RANGE:
            sq_ = tt - 8
            seq, half = sq_ // 2, sq_ % 2
            bk, bt = proj_tm(wp, wt, 0, 416, tt * 128)
            tl = otile[tt % 2]
            tn = f"otile{tt % 2}"
            P.op("dve", lambda e, bk=bk, tl=tl: e.tensor_copy(out=tl[:, 0:256], in_=bk[:, 160:416]), reads=[bt], writes=[tn])
            P.dma(lambda e, tl=tl, seq=seq, half=half: e.dma_start(out=nnk_o[seq, l, half * 128:(half + 1) * 128, :], in_=tl[:, 0:256]), reads=[tn])
            tl2 = A.f32(160) if tt == 8 else tl2
            P.op("act", lambda e, bk=bk: e.activation(out=tf1, in_=bk[:, 0:128], func=ACTF.Square), reads=[bt], writes=["mt_tf1"])
            P.op("dve", lambda e: e.tensor_reduce(out=tf2[:, 0:1], in_=tf1, axis=AX.X, op=ALU.add), reads=["mt_tf1"], writes=["mt_tf2"])
            rstd_from_ss(tf2[:, 0:1], 128.0, tf2[:, 1:2], ["mt_tf2"], "mt_tf2b")
            P.op("dve", lambda e, bk=bk, tl2=tl2: e.scalar_tensor_tensor(out=tl2[:, 0:128], in0=bk[:, 0:128], scalar=tf2[:, 1:2], in1=mrow, op0=ALU.mult, op1=ALU.mult), reads=[bt, "mt_tf2b", "mrow"], writes=["mt_tl2"])
            P.op("act", lambda e, bk=bk, tl2=tl2: e.activation(out=tl2[:, 128:160], in_=bk[:, 128:160], func=ACTF.Copy), reads=[bt], writes=["mt_tl2"])
            P.dma(lambda e, tl2=tl2, seq=seq, half=half: e.dma_start(out=nckv_o[seq, l, half * 128:(half + 1) * 128, :], in_=tl2[:, 0:128]), reads=["mt_tl2"])
            P.dma(lambda e, tl2=tl2, seq=seq, half=half: e.dma_start(out=nkr_o[seq, l, half * 128:(half + 1) * 128, :], in_=tl2[:, 128:160]), reads=["mt_tl2"])
        NA_GROUPS = [(0, 4, 0, 4, 0), (4, 8, 0, 6, 1), (8, 13, 2, 8, 1), (13, 16, 4, 8, 0)]
        for h in range(4):
            c, half = h // 2, h % 2
            rows = slice(half * 64, half * 64 + 64)
            for (r0, r1, kc0, kc1, tab) in (NA_GROUPS if 'nas' not in skip else []):
                nq = (r1 - r0) * 64
                q0 = r0 * 64
                chunks = []
                for kc in range(kc0, kc1):
                    e_top = r0 - 2 * kc + 7
                    brhs = natab[:, h, tab, (e_top + 1) * 64:(e_top + 1) * 64 + nq] if "nab" not in skip else None
                    chunks.append((KT[rows, c, 512 + kc * 128:512 + (kc + 1) * 128], Vn[:, 4 + kc, h * 128:(h + 1) * 128], brhs))
                for i in range(4):
                    chunks.append((KT[rows, c, i * 128:(i + 1) * 128], Vn[:, i, h * 128:(h + 1) * 128], None))
                attend(QT[rows, c, q0:q0 + nq], nq, chunks, mixT[rows, 2 + c, q0:q0 + nq], half, ["QT", "KT", "Vn", "natab"], "mixT0", atmp)
                if l == 0:
                    mods_tick(min(20, 10 + ((h * 4 + NA_GROUPS.index((r0, r1, kc0, kc1, tab)) + 1) * 10) // 16))
            for s in (range(2) if "nap" not in skip else []):
                t0 = 1024 + 256 * s
                chunks = [(KT[rows, c, 512 + t0 + i * 128:512 + t0 + (i + 1) * 128], Vn[:, 12 + 2 * s + i, h * 128:(h + 1) * 128], None) for i in range(2)]
                attend(QT[rows, c, t0:t0 + 256], 256, chunks, mixT[rows, 2 + c, t0:t0 + 256], half, ["QT", "KT", "Vn"], "mixT2", atmp)
        if l == 0:
            dbg("ob", mixT[:, 2:4, :], [128, 2, T], MIXT)
        if stage < 4:
            break
        P.barrier()
        A.reset()

        P.mark(f"L{l} mla")
        QTm = A.bf16(4 * T, parts=96).rearrange("p (h t) -> p h t", h=4)
        KTm = A.bf16(4 * 2048, parts=96).rearrange("p (h t) -> p h t", h=4)
        Vm_raw = A.bf16(16 * 512)
        Vm = Vm_raw.rearrange("p (t c) -> p t c", t=16)
        cqn = Vm_raw[:, 0:2 * T].rearrange("p (c t) -> p c t", c=2)
        ckvn = A.bf16(2048)
        krT = A.bf16(2048, parts=96)
        wq = A.bf16(2 * 384).rearrange("p (k n) -> p k n", k=2)
        wkv = A.bf16(768)
        ctm = A.bf16(4 * 128).rearrange("p (t c) -> p t c", t=4)
        ktm = A.bf16(4 * 96).rearrange("p (t c) -> p t c", t=4)
        tqs = [{"sq": A.bf16(512), "f1": A.f32(512), "xc": A.bf16(512), "xs": A.bf16(512), "sq2": A.bf16(512), "n": f"tq{i}_"} for i in range(2)]
        tqi = [0]
        atmp = {"name": "tq0_", "pt": [A.bf16(512), A.bf16(512), A.bf16(512)], "rd": A.f32(512), "acc": tqs[0]["f1"], "acc_tok": "tq0_f1", "hl": (tqs[0]["xc"], tqs[0]["xs"])}
        P.dma(lambda e: e.dma_start(out=wq, in_=wq_d[l]), writes=["wq"], eng="pool")
        P.dma(lambda e: e.dma_start(out=wkv, in_=wkv_d[l]), writes=["wkv"], eng="pool")
        P.dma(lambda e: e.dma_start(out=ctm, in_=cckv_d[l].rearrange("(t p) c -> p t c", p=128)), writes=["ctm"], eng="pool")
        P.op("pool", lambda e: e.memset(ktm, 0.0), writes=["ktm"])
        P.dma(lambda e: e.dma_start(out=ktm[:, :, 64:96], in_=ckr_d[l].rearrange("(t p) c -> p t c", p=128)), reads=["ktm"], writes=["ktm"], eng="pool")
        bk, bt = bank()
        bkb = bk[:, :].bitcast(BF16)
        def trc(e, bkb=bkb):
            ins = None
            for t in range(4):
                ins = e.transpose(bkb[:, t * 128:(t + 1) * 128], ctm[:, t, :], ident_b)
            return ins
        P.op("pe", trc, reads=["ctm", "matsb"], writes=[bt])
        P.op("act", lambda e, bkb=bkb: e.activation(out=ckvn[:, 0:512], in_=bkb[:, 0:512], func=ACTF.Copy), reads=[bt], writes=["ckvn"])
        bk, bt = bank()
        bkb = bk[:, :].bitcast(BF16)
        def trk(e, bkb=bkb):
            ins = None
            for t in range(4):
                ins = e.transpose(bkb[0:96, t * 128:(t + 1) * 128], ktm[:, t, :], ident_b)
            return ins
        P.op("pe", trk, reads=["ktm", "matsb"], writes=[bt])
        P.op("act", lambda e, bkb=bkb: e.activation(out=krT[:, 0:512], in_=bkb[0:96, 0:512], func=ACTF.Copy), reads=[bt], writes=["krT"])

        wp, wt = panel(("in", l, "M"))
        for b in range(NB):
            smp = b < 2
            tqi[0] += 1
            tq = tqs[tqi[0] % 2]; sq2 = tq["sq2"]; tn = tq["n"]
            ba, bat = proj_fm(wp, wt, 0, 128, b)
            bb, bbt = proj_fm(wp, wt, 128, 128, b)
            P.op("act", lambda e, ba=ba: e.activation(out=tq["sq"], in_=ba[:, :], func=ACTF.Square), reads=[bat], writes=[tn + "sq"])
            P.op("act", lambda e, bb=bb: e.activation(out=sq2, in_=bb[:, :], func=ACTF.Square), reads=[bbt], writes=[tn + "sq2"])
            b2, b2t = bank()
            def ssf(e, b2=b2):
                e.matmul(b2[:, :], lhsT=ones_b, rhs=tq["sq"], start=True, stop=False)
                return e.matmul(b2[:, :], lhsT=ones_b, rhs=sq2, start=False, stop=True)
            P.op("pe", ssf, reads=[tn + "sq", tn + "sq2", "matsb"], writes=[b2t])
            rstd_from_ss(b2[:, :], 256.0, tq["f1"], [b2t], tn + "f1")
            for c, (bq, bqt) in enumerate(((ba, bat), (bb, bbt))):
                P.op("dve", lambda e, bq=bq, c=c, b=b: e.scalar_tensor_tensor(out=cqn[:, c, blk(b)], in0=bq[:, :], scalar=smallp[:, l, 2 + c:3 + c], in1=tq["f1"], op0=ALU.mult, op1=ALU.mult), reads=[bqt, tn + "f1", "smallp"], writes=["cqn"])
            tqi[0] += 1
            tq = tqs[tqi[0] % 2]; tn = tq["n"]
            bc, bct = proj_fm(wp, wt, 256, 128, b)
            P.op("act", lambda e, bc=bc: e.activation(out=tq["sq"], in_=bc[:, :], func=ACTF.Square), reads=[bct], writes=[tn + "sq"])
            b2, b2t = bank()
            P.op("pe", lambda e, b2=b2: e.matmul(b2[:, :], lhsT=ones_b, rhs=tq["sq"], start=True, stop=True), reads=[tn + "sq", "matsb"], writes=[b2t])
            rstd_from_ss(b2[:, :], 128.0, tq["f1"], [b2t], tn + "f1")
            P.op("dve", lambda e, bc=bc, b=b: e.scalar_tensor_tensor(out=ckvn[:, 512 + b * 512:1024 + b * 512], in0=bc[:, :], scalar=smallp[:, l, 4:5], in1=tq["f1"], op0=ALU.mult, op1=ALU.mult), reads=[bct, tn + "f1", "smallp"], writes=["ckvn"])
            bkr, bkrt = proj_fm(wp, wt, 384, 96, b)
            dstk = krT[:, 512 + b * 512:1024 + b * 512]
            if smp:
                rope_apply(bkr[0:96, :], dstk, b * 512, "krT", cosM, sinM, rotM, [bkrt], 512, parts=96)
            else:
                P.op("act", lambda e, bkr=bkr, dstk=dstk: e.activation(out=dstk, in_=bkr[0:96, :], func=ACTF.Copy), reads=[bkrt], writes=["krT"])
        for b in range(NB):
            for h in range(4):
                bq, bqt = bank()
                def qf(e, bq=bq, h=h, b=b):
                    e.matmul(bq[0:96, :], lhsT=wq[:, 0, h * 96:(h + 1) * 96], rhs=cqn[:, 0, blk(b)], start=True, stop=False)
                    return e.matmul(bq[0:96, :], lhsT=wq[:, 1, h * 96:(h + 1) * 96], rhs=cqn[:, 1, blk(b)], start=False, stop=True)
                P.op("pe", qf, reads=["wq", "cqn"], writes=[bqt])
                if b < 2:
                    rope_apply(bq[0:96, :], QTm[:, h, blk(b)], b * 512, "QTm", cosM, sinM, rotM, [bqt], 512, parts=96, pre=MLA_SCALE)
                else:
                    P.op("act", lambda e, bq=bq, h=h, b=b: e.activation(out=QTm[:, h, blk(b)], in_=bq[0:96, :], func=ACTF.Copy, scale=MLA_SCALE), reads=[bqt], writes=["QTm"])
        P.barrier()
        for kb in range(4):
            for h in range(4):
                bq, bqt = bank()
                P.op("pe", lambda e, bq=bq, h=h, kb=kb: e.matmul(bq[0:64, :], lhsT=wkv[:, h * 64:(h + 1) * 64], rhs=ckvn[:, kb * 512:(kb + 1) * 512], start=True, stop=True), reads=["wkv", "ckvn"], writes=[bqt])
                if h % 2 == 0:
                    P.op("act", lambda e, bq=bq, h=h, kb=kb: e.activation(out=KTm[0:64, h, kb * 512:(kb + 1) * 512], in_=bq[0:64, :], func=ACTF.Copy), reads=[bqt], writes=["KTm"])
                else:
                    P.op("dve", lambda e, bq=bq, h=h, kb=kb: e.tensor_copy(out=KTm[0:64, h, kb * 512:(kb + 1) * 512], in_=bq[0:64, :]), reads=[bqt], writes=["KTm"])
        for h in range(4):
            P.op("pool", lambda e, h=h: e.tensor_copy(out=KTm[64:96, h, :], in_=krT[64:96, :]), reads=["krT"], writes=["KTm"])
        for kt in range(16):
            bq, bqt = bank()
            P.op("pe", lambda e, bq=bq, kt=kt: e.matmul(bq[:, :], lhsT=ckvn[:, kt * 128:(kt + 1) * 128], rhs=wkv[:, 256:768], start=True, stop=True), reads=["wkv", "ckvn"], writes=[bqt])
            P.op("act", lambda e, bq=bq, kt=kt: e.activation(out=Vm[:, kt, :], in_=bq[:, :], func=ACTF.Copy), reads=[bqt], writes=["Vm"])
        for h in range(4):
            c, half = h // 2, h % 2
            rows = slice(half * 64, half * 64 + 64)
            for qb in range(2):
                chunks = [(KTm[:, h, i * 128:(i + 1) * 128], Vm[:, i, h * 128:(h + 1) * 128], None) for i in range(12)]
                attend(QTm[:, h, blk(qb)], 512, chunks, mixT[rows, 6 + c, blk(qb)], half, ["QTm", "KTm", "Vm"], f"mixT{qb}", atmp)
            for s in range(2):
                t0 = 1024 + 256 * s
                chunks = [(KTm[:, h, 512 + t0 + i * 128:512 + t0 + (i + 1) * 128], Vm[:, 12 + 2 * s + i, h * 128:(h + 1) * 128], None) for i in range(2)]
                attend(QTm[:, h, t0:t0 + 256], 256, chunks, mixT[rows, 6 + c, t0:t0 + 256], half, ["QTm", "KTm", "Vm"], "mixT2", atmp)
        if l == 0:
            dbg("od", mixT[:, 6:8, :], [128, 2, T], MIXT)
        if stage < 5:
            break
        P.barrier()

        P.mark(f"L{l} dn")
        def dn_phase(tok0, ntok, seqs, is_sample):
            A.reset()
            NC = ntok // 64
            nseq = len(seqs)
            QTd = A.bf16(4 * ntok, parts=64).rearrange("p (h t) -> p h t", h=4)
            KTd = A.bf16(4 * ntok, parts=64).rearrange("p (h t) -> p h t", h=4)
            VTd = A.bf16(2 * ntok).rearrange("p (c t) -> p c t", c=2)
            sg = A.bf16(NC * 256, parts=64).rearrange("p (c n) -> p c n", c=NC)
            oacc = A.bf16(NC * 256, parts=64).rearrange("p (c n) -> p c n", c=NC)
            ab = A.f32(NC * 16, parts=64).rearrange("p (c n) -> p c n", c=NC)
            g_all = A.f32(NC * 8, parts=64).rearrange("p (c n) -> p c n", c=NC)
            beta_all = A.f32(NC * 8, parts=64).rearrange("p (c n) -> p c n", c=NC)
            smalls = A.f32(64 + 64 + 16 + 16 + 32 + 8, parts=128)
            dng_row = smalls[0:64, 0:64]; cwqk = smalls[0:64, 64:96].rearrange("p (c j) -> p c j", c=8)
            cwv = smalls[:, 96:104].rearrange("p (c j) -> p c j", c=2)
            dndt = smalls[0:64, 104:112]; nea = smalls[0:64, 112:120]
            P.dma(lambda e: e.dma_start(out=dng_row, in_=dng_row_d[:, l]), writes=["dn_small"])
            P.dma(lambda e: e.dma_start(out=cwqk, in_=cwqk_d[:, l]), writes=["dn_small"])
            P.dma(lambda e: e.dma_start(out=cwv, in_=cwv_d[:, l]), writes=["dn_small"])
            P.dma(lambda e: e.dma_start(out=dndt, in_=dndt_d[:, l]), writes=["dn_small"])
            P.dma(lambda e: e.dma_start(out=nea, in_=dnal_d[:, l]), writes=["dn_small"])
            mark = A.off
            slen = ntok // nseq
            W = ntok + 3 * nseq
            zpads = [A.f32(W), A.f32(W)]; accs = [A.f32(W), A.f32(W)]
            sqbs = [A.bf16(512, parts=64), A.bf16(512, parts=64)]; rsts = [A.f32(512, parts=64), A.f32(512, parts=64)]
            for i_ in range(2):
                P.op("pool", lambda e, i_=i_: e.memset(zpads[i_], 0.0), writes=[f"zpad{i_}"])
            cvi = [0]
            nblk = ntok // 512
            b0 = tok0 // 512

            def zcopy(bk, bt, parts):
                for bi in range(1):
                    pass

            def conv_chunk(parts, wcol_fn, c0, m, wp, wt):
                cvi[0] += 1
                zpad = zpads[cvi[0] % 2]; acc = accs[cvi[0] % 2]
                zt = f"zpad{cvi[0] % 2}"; at_ = f"acc{cvi[0] % 2}"
                for bi in range(nblk):
                    bk, bt = proj_fm(wp, wt, c0, m, b0 + bi)
                    if is_sample:
                        P.op("act", lambda e, bk=bk, bi=bi: e.activation(out=zpad[0:parts, 1 + bi * 512:1 + (bi + 1) * 512], in_=bk[0:parts, :], func=ACTF.Copy), reads=[bt], writes=[zt])
                    else:
                        for s in range(2):
                            P.op("act", lambda e, bk=bk, s=s: e.activation(out=zpad[0:parts, 1 + 259 * s:257 + 259 * s], in_=bk[0:parts, 256 * s:256 * (s + 1)], func=ACTF.Copy), reads=[bt], writes=[zt])
                n = W - 3
                P.op("dve", lambda e: e.tensor_scalar(out=acc[0:parts, 0:n], in0=zpad[0:parts, 0:n], scalar1=wcol_fn(0), scalar2=None, op0=ALU.mult), reads=[zt, "dn_small"], writes=[at_])
                for j in range(1, 4):
                    P.op("dve", lambda e, j=j: e.scalar_tensor_tensor(out=acc[0:parts, 0:n], in0=zpad[0:parts, j:n + j], scalar=wcol_fn(j), in1=acc[0:parts, 0:n], op0=ALU.mult, op1=ALU.add), reads=[zt, at_, "dn_small"], writes=[at_])
                return acc, at_

            def segs():
                if is_sample:
                    return [(0, 0, 512), (512, 512, 512)]
                return [(0, 0, 256), (259, 256, 256)]

            wp, wt = panel(("in", l, "D1"))
            for hc in range(8):
                acc, at_ = conv_chunk(64, lambda j, hc=hc: cwqk[:, hc, j:j + 1], hc * 64, 64, wp, wt)
                P.op("act", lambda e, acc=acc: e.activation(out=acc[0:64, 0:W - 3], in_=acc[0:64, 0:W - 3], func=ACTF.Silu), reads=[at_], writes=[at_])
                for si_, (ao, to, n) in enumerate(segs()):
                    sqb = sqbs[si_ % 2]; rst = rsts[si_ % 2]; sqt = f"dn_sqb{si_ % 2}"; rtt = f"dn_rst{si_ % 2}"
                    P.op("act", lambda e, ao=ao, n=n, acc=acc, sqb=sqb: e.activation(out=sqb[:, 0:n], in_=acc[0:64, ao:ao + n], func=ACTF.Square), reads=[at_], writes=[sqt])
                    b2, b2t = bank()
                    P.op("pe", lambda e, b2=b2, n=n, sqb=sqb: e.matmul(b2[0:64, 0:n], lhsT=ones_b[0:64, 0:64], rhs=sqb[:, 0:n], start=True, stop=True), reads=[sqt, "matsb"], writes=[b2t])
                    rstd_from_ss(b2[0:64, 0:n], 1.0, rst[:, 0:n], [b2t], rtt)
                    dst = (QTd if hc < 4 else KTd)[:, hc % 4, to:to + n]
                    sc = 0.125 if hc < 4 else 1.0
                    P.op("dve", lambda e, ao=ao, n=n, dst=dst, sc=sc, acc=acc, rst=rst: e.scalar_tensor_tensor(out=dst, in0=acc[0:64, ao:ao + n], scalar=sc, in1=rst[:, 0:n], op0=ALU.mult, op1=ALU.mult), reads=[at_, rtt], writes=["dn_qk"])
            wp, wt = panel(("in", l, "D2"))
            for vc in range(2):
                acc, at_ = conv_chunk(128, lambda j, vc=vc: cwv[:, vc, j:j + 1], vc * 128, 128, wp, wt)
                for (ao, to, n) in segs():
                    P.op("act", lambda e, ao=ao, to=to, n=n, vc=vc, acc=acc: e.activation(out=VTd[:, vc, to:to + n], in_=acc[:, ao:ao + n], func=ACTF.Silu), reads=[at_], writes=["dn_v"])
            for c in range(NC):
                bk, bt = proj_tm(wp, wt, 256, 256, tok0 + c * 64, m=64)
                P.op("act", lambda e, bk=bk, c=c: e.activation(out=sg[:, c, :], in_=bk[0:64, 0:256], func=ACTF.Silu), reads=[bt], writes=["dn_sg"])
            wp, wt = panel(("in", l, "DAB"))
            bkg, bkgt = bank("a")
            for c in range(NC):
                proj_tm(wp, wt, 0, 16, tok0 + c * 64, m=64, bk_bt=(bkg, bkgt), col0=c * 16)
            P.op("act", lambda e: e.activation(out=ab, in_=bkg[0:64, 0:NC * 16].rearrange("p (c n) -> p c n", c=NC), func=ACTF.Copy), reads=[bkgt], writes=["dn_ab"])
            tg = A.f32(NC * 8, parts=64).rearrange("p (c n) -> p c n", c=NC)
            P.op("dve", lambda e: e.tensor_tensor(out=tg, in0=ab[:, :, 0:8], in1=dndt.unsqueeze(1).to_broadcast([64, NC, 8]), op=ALU.add), reads=["dn_ab", "dn_small"], writes=["dn_tg"])
            P.op("act", lambda e: e.activation(out=tg, in_=tg, func=ACTF.Exp), reads=["dn_tg"], writes=["dn_tg"])
            P.op("act", lambda e: e.activation(out=tg, in_=tg, func=ACTF.Ln, bias=1.0), reads=["dn_tg"], writes=["dn_tg"])
            P.op("act", lambda e: e.activation(out=nea, in_=nea, func=ACTF.Exp), reads=["dn_small"], writes=["dn_nea"])
            P.op("dve", lambda e: e.tensor_scalar(out=nea, in0=nea, scalar1=-1.0, scalar2=None, op0=ALU.mult), reads=["dn_nea"], writes=["dn_nea"])
            P.op("dve", lambda e: e.tensor_tensor(out=g_all, in0=tg, in1=nea.unsqueeze(1).to_broadcast([64, NC, 8]), op=ALU.mult), reads=["dn_tg", "dn_nea"], writes=["dn_g"])
            P.op("act", lambda e: e.activation(out=beta_all, in_=ab[:, :, 8:16], func=ACTF.Exp, scale=-1.0), reads=["dn_ab"], writes=["dn_beta"])
            P.op("dve", lambda e: e.tensor_scalar_add(out=beta_all, in0=beta_all, scalar1=1.0), reads=["dn_beta"], writes=["dn_beta"])
            P.op("dve", lambda e: e.reciprocal(out=beta_all, in_=beta_all), reads=["dn_beta"], writes=["dn_beta"])
            if l == 0 and is_sample:
                dbg("dn_q", QTd, [64, 4, ntok], ["dn_qk"])
                dbg("dn_k", KTd, [64, 4, ntok], ["dn_qk"])
                dbg("dn_v", VTd, [128, 2, ntok], ["dn_v"])
                dbg("dn_g", g_all, [64, NC, 8], ["dn_g"])
                dbg("dn_beta", beta_all, [64, NC, 8], ["dn_beta"])
            if "dnscan" in skip:
                return
            dnstop = ([int(x[6:]) for x in skip if x.startswith('dnstop')] + [0])[0]
            P.mark(f"L{l} dnscan{tok0}")
            P.barrier()
            A.reset(mark)
            def t512(dt):
                return (A.f32(512, parts=64) if dt == F32 else A.bf16(512, parts=64))
            Dm, Am, Bm, U, AN, T32, P32, W_, Pm, WT = (t512(F32) for _ in range(10))
            Vb, KbEg, kdec = Dm, Am, Bm
            dg = U
            db, de, KbT, QdT, qkT, ANb, ATb, ANb2, ATb2, Pb = (t512(BF16) for _ in range(10))
            vnew = A.f32(256, parts=64); vnew_b = A.bf16(256, parts=64); S = A.f32(256, parts=64); Sb = A.bf16(256, parts=64)
            of = A.f32(256, parts=64); otmp = vnew; oo = A.bf16(256, parts=64)
            sm8 = A.f32(8 * 8, parts=64)
            g8, beta8, gc, eg, tmg, ekd, gl, beg = (sm8[:, i * 8:(i + 1) * 8] for i in range(8))
            ss4 = A.f32(8, parts=64)
            I64f = ident_f[0:64, 0:64]; I64b = ident_b[0:64, 0:64]; O64f = ones_f[0:64, 0:64]; O64b = ones_b[0:64, 0:64]

            def v3(t):
                return t.rearrange("p (u i) -> p u i", u=8)

            def vh(t):
                return t.rearrange("p (h n) -> p h n", h=4)

            def bc8(s):
                return s.unsqueeze(2).to_broadcast([64, 8, 64])

            def mask_b(i):
                return masks[0:64, i, :].unsqueeze(1).to_broadcast([64, 8, 64])

            for d in range(2):
                MI_tri = 3 if d == 0 else 2
                M_sN, M_tT, M_sT = (0, 3, 1) if d == 0 else (1, 2, 0)
                for si, (sc0, snc) in enumerate(seqs):
                    if is_sample:
                        P.dma(lambda e, d=d: e.dma_start(out=S.rearrange("p (h v) -> p h v", h=4), in_=sdn_d[l, d].rearrange("h k v -> k h v")), writes=["dn_S"])
                    else:
                        P.op("pool", lambda e: e.memset(S, 0.0), writes=["dn_S"])
                    P.op("act", lambda e: e.activation(out=Sb, in_=S, func=ACTF.Copy), reads=["dn_S"], writes=["dn_Sb"])
                    pairs = list(range(sc0, sc0 + snc, 2))
                    if d == 1:
                        pairs = pairs[::-1]
                    for c0 in pairs:
                        t0 = c0 * 64
                        KT2 = KTd[:, :, t0:t0 + 128]; QT2 = QTd[:, :, t0:t0 + 128]
                        P.op("act", lambda e, c0=c0, d=d: e.activation(out=g8.rearrange("p (h j) -> p h j", h=4), in_=g_all[:, c0:c0 + 2, d * 4:d * 4 + 4].rearrange("p j h -> p h j"), func=ACTF.Copy), reads=["dn_g"], writes=["dn_g8"])
                        P.op("act", lambda e, c0=c0, d=d: e.activation(out=beta8.rearrange("p (h j) -> p h j", h=4), in_=beta_all[:, c0:c0 + 2, d * 4:d * 4 + 4].rearrange("p j h -> p h j"), func=ACTF.Copy), reads=["dn_beta"], writes=["dn_b8"])
                        if dnstop == 1:
                            return
                        bk1, bk1t = bank()
                        def cs_f(e, bk1=bk1, MI_tri=MI_tri):
                            e.matmul(bk1[0:64, 0:8], lhsT=masks[0:64, MI_tri, :], rhs=g8, start=True, stop=True)
                            return e.matmul(bk1[0:64, 8:16], lhsT=O64f, rhs=g8, start=True, stop=True)
                        P.op("pe", cs_f, reads=["dn_g8", "masks", "matsf"], writes=[bk1t])
                        P.op("act", lambda e, bk1=bk1: e.activation(out=gc, in_=bk1[0:64, 0:8], func=ACTF.Copy), reads=[bk1t], writes=["dn_gc"])
                        P.op("act", lambda e, bk1=bk1: e.activation(out=eg, in_=bk1[0:64, 0:8], func=ACTF.Exp), reads=[bk1t], writes=["dn_eg"])
                        P.op("act", lambda e, bk1=bk1: e.activation(out=gl, in_=bk1[0:64, 8:16], func=ACTF.Exp), reads=[bk1t], writes=["dn_gl"])
                        P.op("dve", lambda e, bk1=bk1: e.tensor_tensor(out=tmg, in0=bk1[0:64, 8:16], in1=gc, op=ALU.subtract), reads=[bk1t, "dn_gc"], writes=["dn_tmg"])
                        P.op("act", lambda e: e.activation(out=ekd, in_=tmg, func=ACTF.Exp), reads=["dn_tmg"], writes=["dn_ekd"])
                        P.op("dve", lambda e: e.tensor_tensor(out=beg, in0=beta8, in1=eg, op=ALU.mult), reads=["dn_b8", "dn_eg"], writes=["dn_beg"])
                        if dnstop == 2:
                            return
                        P.op("dve", lambda e: e.tensor_tensor(out=v3(dg), in0=I64f.unsqueeze(1).to_broadcast([64, 8, 64]), in1=bc8(gc), op=ALU.mult), reads=["matsf", "dn_gc"], writes=["dn_U"])
                        P.op("pool", lambda e: e.tensor_tensor(out=v3(db), in0=I64f.unsqueeze(1).to_broadcast([64, 8, 64]), in1=bc8(beta8), op=ALU.mult), reads=["matsf", "dn_b8"], writes=["dn_db"])
                        P.op("pool", lambda e: e.tensor_tensor(out=v3(de), in0=I64f.unsqueeze(1).to_broadcast([64, 8, 64]), in1=bc8(eg), op=ALU.mult), reads=["matsf", "dn_eg"], writes=["dn_de"])
                        Rg, Rgt = bank(); Rb, Rbt = bank(); Re, Ret = bank()
                        P.op("pe", lambda e, Rg=Rg: e.matmul(Rg[0:64, :], lhsT=O64f, rhs=dg, start=True, stop=True), reads=["dn_U", "matsf"], writes=[Rgt])
                        P.op("pe", lambda e, Rb=Rb: e.matmul(Rb[0:64, :], lhsT=O64b, rhs=db, start=True, stop=True), reads=["dn_db", "matsb"], writes=[Rbt])
                        P.op("pe", lambda e, Re=Re: e.matmul(Re[0:64, :], lhsT=O64b, rhs=de, start=True, stop=True), reads=["dn_de", "matsb"], writes=[Ret])
                        if dnstop == 3:
                            return
                        P.op("dve", lambda e, Rg=Rg: e.tensor_tensor(out=v3(Dm), in0=bc8(gc), in1=v3(Rg[0:64, :]), op=ALU.subtract), reads=[Rgt, "dn_gc"], writes=["dn_Dm"])
                        P.op("dve", lambda e: e.tensor_scalar_min(out=Am, in0=Dm, scalar1=0.0), reads=["dn_Dm"], writes=["dn_Am"])
                        P.op("dve", lambda e: e.tensor_scalar(out=Dm, in0=Dm, scalar1=-1.0, scalar2=0.0, op0=ALU.mult, op1=ALU.min), reads=["dn_Dm"], writes=["dn_Dm"])
                        P.op("act", lambda e: e.activation(out=Am, in_=Am, func=ACTF.Exp), reads=["dn_Am"], writes=["dn_Am"])
                        P.op("act", lambda e: e.activation(out=Dm, in_=Dm, func=ACTF.Exp), reads=["dn_Dm"], writes=["dn_Dm"])
                        P.op("pool", lambda e, M_sN=M_sN: e.tensor_tensor(out=v3(Am), in0=v3(Am), in1=mask_b(M_sN), op=ALU.mult), reads=["dn_Am", "masks"], writes=["dn_Am"])
                        P.op("pool", lambda e, M_tT=M_tT: e.tensor_tensor(out=v3(Bm), in0=v3(Dm), in1=mask_b(M_tT), op=ALU.mult), reads=["dn_Dm", "masks"], writes=["dn_Bm"])
                        P.op("pool", lambda e, M_sT=M_sT: e.tensor_tensor(out=v3(Dm), in0=v3(Dm), in1=mask_b(M_sT), op=ALU.mult), reads=["dn_Dm", "masks", "dn_Bm"], writes=["dn_Dm"])
                        if dnstop == 4:
                            return
                        P.op("dve", lambda e, Rb=Rb, KT2=KT2: e.tensor_tensor(out=vh(KbT), in0=KT2, in1=vh(Rb[0:64, :]), op=ALU.mult), reads=[Rbt, "dn_qk"], writes=["dn_KbT"])
                        P.op("dve", lambda e, Re=Re, QT2=QT2: e.tensor_tensor(out=vh(QdT), in0=QT2, in1=vh(Re[0:64, :]), op=ALU.mult), reads=[Ret, "dn_qk"], writes=["dn_QdT"])
                        if dnstop == 5:
                            return
                        pAN, pANt = bank(); pAT, pATt = bank(); pQK, pQKt = bank()
                        def prods(e, pAN=pAN, pAT=pAT, pQK=pQK, KT2=KT2, QT2=QT2):
                            ins = None
                            for h in range(4):
                                for j in range(2):
                                    u = h * 2 + j
                                    cs = slice(u * 64, u * 64 + 64)
                                    ks = KT2[:, h, j * 64:j * 64 + 64]
                                    e.matmul(pAN[0:64, cs], lhsT=KbT[:, cs], rhs=ks, start=True, stop=True)
                                    e.matmul(pAT[0:64, cs], lhsT=ks, rhs=KbT[:, cs], start=True, stop=True)
                                    ins = e.matmul(pQK[0:64, cs], lhsT=ks, rhs=QT2[:, h, j * 64:j * 64 + 64], start=True, stop=True)
                            return ins
                        P.op("pe", prods, reads=["dn_KbT", "dn_qk"], writes=[pANt, pATt, pQKt])
                        P.op("dve", lambda e, pAN=pAN: e.tensor_tensor(out=AN, in0=pAN[0:64, :], in1=Am, op=ALU.mult), reads=[pANt, "dn_Am"], writes=["dn_AN"])
                        P.op("dve", lambda e, pAT=pAT: e.tensor_tensor(out=ATb, in0=pAT[0:64, :], in1=Dm, op=ALU.mult), reads=[pATt, "dn_Dm"], writes=["dn_ATb"])
                        P.op("act", lambda e: e.activation(out=ANb, in_=AN, func=ACTF.Copy), reads=["dn_AN"], writes=["dn_ANb"])
                        P.op("dve", lambda e, pQK=pQK: e.tensor_tensor(out=qkT, in0=pQK[0:64, :], in1=Bm, op=ALU.mult), reads=[pQKt, "dn_Bm"], writes=["dn_qkT"])
                        P.op("pool", lambda e: e.tensor_tensor(out=v3(Pb), in0=I64b.unsqueeze(1).to_broadcast([64, 8, 64]), in1=v3(ATb), op=ALU.subtract), reads=["dn_ATb", "matsb"], writes=["dn_Pb"])
                        if dnstop == 6:
                            return
                        an, at, ant, att = ANb, ATb, "dn_ANb", "dn_ATb"
                        an_n, at_n, ant_n, att_n = ANb2, ATb2, "dn_ANb2", "dn_ATb2"
                        for lev in range(4):
                            pa, pat = bank(); pb, pbt = bank()
                            def sqf(e, pa=pa, pb=pb, an=an, at=at, lev=lev):
                                ins = None
                                for u in range(8):
                                    cs = slice(u * 64, u * 64 + 64)
                                    ins = e.matmul(pa[0:64, cs], lhsT=at[:, cs], rhs=an[:, cs], start=True, stop=True)
                                    if lev < 3:
                                        ins = e.matmul(pb[0:64, cs], lhsT=an[:, cs], rhs=at[:, cs], start=True, stop=True)
                                return ins
                            P.op("pe", sqf, reads=[ant, att], writes=[pat, pbt])
                            P.op("act", lambda e, pa=pa, an_n=an_n: e.activation(out=an_n, in_=pa[0:64, :], func=ACTF.Copy), reads=[pat], writes=[ant_n])
                            if lev < 3:
                                P.op("dve", lambda e, pb=pb, at_n=at_n: e.tensor_copy(out=at_n, in_=pb[0:64, :]), reads=[pbt], writes=[att_n])
                            pp, ppt = bank()
                            def apf(e, pp=pp, an_n=an_n):
                                ins = None
                                for u in range(8):
                                    cs = slice(u * 64, u * 64 + 64)
                                    ins = e.matmul(pp[0:64, cs], lhsT=an_n[:, cs], rhs=Pb[:, cs], start=True, stop=True)
                                return ins
                            P.op("pe", apf, reads=[ant_n, "dn_Pb"], writes=[ppt])
                            P.op("dve", lambda e, pp=pp: e.tensor_tensor(out=Pb, in0=pp[0:64, :], in1=Pb, op=ALU.add), reads=[ppt, "dn_Pb"], writes=["dn_Pb"])
                            an, at, ant, att, an_n, at_n, ant_n, att_n = an_n, at_n, ant_n, att_n, an, at, ant, att
                        ptp, ptpt = bank()
                        def trp(e, ptp=ptp):
                            ins = None
                            for u in range(8):
                                cs = slice(u * 64, u * 64 + 64)
                                ins = e.matmul(ptp[0:64, cs], lhsT=Pb[:, cs], rhs=I64b, start=True, stop=True)
                            return ins
                        P.op("pe", trp, reads=["dn_Pb", "matsb"], writes=[ptpt])
                        P.op("act", lambda e, ptp=ptp: e.activation(out=T32, in_=ptp[0:64, :], func=ACTF.Copy), reads=[ptpt], writes=["dn_T32"])
                        P.op("dve", lambda e: e.tensor_copy(out=P32, in_=Pb), reads=["dn_Pb"], writes=["dn_P32"])
                        pw1, pw1t = bank()
                        def w1f(e, pw1=pw1):
                            ins = None
                            for u in range(8):
                                cs = slice(u * 64, u * 64 + 64)
                                ins = e.matmul(pw1[0:64, cs], lhsT=AN[:, cs], rhs=P32[:, cs], start=True, stop=True)
                            return ins
                        P.op("pe", w1f, reads=["dn_AN", "dn_P32"], writes=[pw1t])
                        P.op("dve", lambda e, pw1=pw1: e.tensor_tensor(out=W_, in0=pw1[0:64, :], in1=P32, op=ALU.add), reads=[pw1t, "dn_P32"], writes=["dn_W"])
                        pw2, pw2t = bank()
                        def w2f(e, pw2=pw2):
                            ins = None
                            for u in range(8):
                                cs = slice(u * 64, u * 64 + 64)
                                ins = e.matmul(pw2[0:64, cs], lhsT=T32[:, cs], rhs=W_[:, cs], start=True, stop=True)
                            return ins
                        P.op("pe", w2f, reads=["dn_T32", "dn_W"], writes=[pw2t])
                        P.op("dve", lambda e, pw2=pw2: e.scalar_tensor_tensor(out=Pm, in0=P32, scalar=2.0, in1=pw2[0:64, :], op0=ALU.mult, op1=ALU.subtract), reads=[pw2t, "dn_P32"], writes=["dn_P"])
                        if dnstop == 7:
                            return
                        pk, pkt = bank(); pv_, pvt = bank()
                        def trf(e, pk=pk, pv_=pv_, t0=t0):
                            ins = None
                            for h in range(4):
                                for j in range(2):
                                    u = h * 2 + j
                                    tk = slice(t0 + j * 64, t0 + j * 64 + 64)
                                    e.matmul(pk[0:64, u * 64:u * 64 + 64], lhsT=KTd[:, h, tk], rhs=I64b, start=True, stop=True)
                                    ins = None
                            for h in range(4):
                                for j in range(2):
                                    u = h * 2 + j
                                    tk = slice(t0 + j * 64, t0 + j * 64 + 64)
                                    hl = h % 2
                                    ins = e.matmul(pv_[0:64, u * 64:u * 64 + 64], lhsT=VTd[:, h // 2, tk], rhs=ident_b[:, hl * 64:hl * 64 + 64], start=True, stop=True)
                            return ins
                        P.op("pe", trf, reads=["dn_qk", "dn_v", "matsb"], writes=[pkt, pvt])
                        P.op("dve", lambda e, pv_=pv_: e.tensor_tensor(out=v3(Vb), in0=bc8(beta8), in1=v3(pv_[0:64, :]), op=ALU.mult), reads=[pvt, "dn_b8"], writes=["dn_Dm"])
                        P.op("dve", lambda e, pk=pk: e.tensor_tensor(out=v3(KbEg), in0=bc8(beg), in1=v3(pk[0:64, :]), op=ALU.mult), reads=[pkt, "dn_beg"], writes=["dn_Am"])
                        P.op("dve", lambda e, pk=pk: e.tensor_tensor(out=v3(kdec), in0=bc8(ekd), in1=v3(pk[0:64, :]), op=ALU.mult), reads=[pkt, "dn_ekd"], writes=["dn_Bm"])
                        if dnstop == 8:
                            return
                        pu, put = bank(); pw, pwt = bank()
                        def uwf(e, pu=pu, pw=pw):
                            ins = None
                            for u in range(8):
                                cs = slice(u * 64, u * 64 + 64)
                                e.matmul(pu[0:64, cs], lhsT=Pm[:, cs], rhs=Vb[:, cs], start=True, stop=True)
                                ins = e.matmul(pw[0:64, cs], lhsT=KbEg[:, cs], rhs=Pm[:, cs], start=True, stop=True)
                            return ins
                        P.op("pe", uwf, reads=["dn_P", "dn_Dm", "dn_Am"], writes=[put, pwt])
                        P.op("act", lambda e, pu=pu: e.activation(out=U, in_=pu[0:64, :], func=ACTF.Copy), reads=[put], writes=["dn_U"])
                        P.op("dve", lambda e, pw=pw: e.tensor_copy(out=WT, in_=pw[0:64, :]), reads=[pwt], writes=["dn_WT"])
                        if dnstop == 9:
                            return
                        for j in ((0, 1) if d == 0 else (1, 0)):
                            c = c0 + j
                            def cs_(h, j=j):
                                return slice((h * 2 + j) * 64, (h * 2 + j) * 64 + 64)
                            pws, pwst = bank()
                            def wsf(e, pws=pws, cs_=cs_):
                                ins = None
                                for h in range(4):
                                    ins = e.matmul(pws[0:64, h * 64:h * 64 + 64], lhsT=WT[:, cs_(h)], rhs=S[:, h * 64:h * 64 + 64], start=True, stop=True)
                                return ins
                            P.op("pe", wsf, reads=["dn_WT", "dn_S"], writes=[pwst])
                            Uj = U.rearrange("p (h j i) -> p h j i", h=4, j=2)[:, :, j, :]
                            P.op("dve", lambda e, pws=pws, Uj=Uj: e.tensor_tensor(out=vnew.rearrange("p (h i) -> p h i", h=4), in0=Uj, in1=pws[0:64, 0:256].rearrange("p (h i) -> p h i", h=4), op=ALU.subtract), reads=[pwst, "dn_U"], writes=["dn_vnew"])
                            P.op("act", lambda e: e.activation(out=vnew_b, in_=vnew, func=ACTF.Copy), reads=["dn_vnew"], writes=["dn_vnewb"])
                            po, pot = bank(); psn, psnt = bank()
                            def osf(e, po=po, psn=psn, cs_=cs_, j=j, QT2=QT2):
                                ins = None
                                for h in range(4):
                                    hs = slice(h * 64, h * 64 + 64)
                                    e.matmul(po[0:64, hs], lhsT=QdT[:, cs_(h)], rhs=Sb[:, hs], start=True, stop=False)
                                    e.matmul(po[0:64, hs], lhsT=qkT[:, cs_(h)], rhs=vnew_b[:, hs], start=False, stop=True)
                                    ins = e.matmul(psn[0:64, hs], lhsT=kdec[:, cs_(h)], rhs=vnew[:, hs], start=True, stop=True)
                                return ins
                            P.op("pe", osf, reads=["dn_QdT", "dn_Sb", "dn_qkT", "dn_vnew", "dn_vnewb", "dn_Bm"], writes=[pot, psnt])
                            glj = gl.rearrange("p (h j) -> p h j", h=4)[:, :, j:j + 1].to_broadcast([64, 4, 64])
                            P.op("dve", lambda e, glj=glj: e.tensor_tensor(out=S.rearrange("p (h v) -> p h v", h=4), in0=S.rearrange("p (h v) -> p h v", h=4), in1=glj, op=ALU.mult), reads=["dn_S", "dn_gl"], writes=["dn_S"])
                            P.op("dve", lambda e, psn=psn: e.tensor_tensor(out=S, in0=psn[0:64, 0:256], in1=S, op=ALU.add), reads=[psnt, "dn_S"], writes=["dn_S"])
                            P.op("act", lambda e: e.activation(out=Sb, in_=S, func=ACTF.Copy), reads=["dn_S"], writes=["dn_Sb"])
                            if d == 0:
                                P.op("act", lambda e, po=po, c=c: e.activation(out=oacc[:, c, :], in_=po[0:64, 0:256], func=ACTF.Copy), reads=[pot], writes=["dn_oacc"])
                            else:
                                P.op("dve", lambda e, po=po, c=c: e.tensor_tensor(out=of, in0=po[0:64, 0:256], in1=oacc[:, c, :], op=ALU.add), reads=[pot, "dn_oacc"], writes=["dn_of"])
                                P.op("act", lambda e: e.activation(out=otmp, in_=of, func=ACTF.Square), reads=["dn_of"], writes=["dn_vnew"])
                                P.op("dve", lambda e: e.tensor_reduce(out=ss4[:, 0:4], in_=otmp.rearrange("p (h v) -> p h v", h=4), axis=AX.X, op=ALU.add), reads=["dn_vnew"], writes=["dn_ss4"])
                                rstd_from_ss(ss4[:, 0:4], 64.0, ss4[:, 4:8], ["dn_ss4"], "dn_ss4b")
                                P.op("dve", lambda e: e.tensor_tensor(out=of.rearrange("p (h v) -> p h v", h=4), in0=of.rearrange("p (h v) -> p h v", h=4), in1=ss4[:, 4:8].unsqueeze(2).to_broadcast([64, 4, 64]), op=ALU.mult), reads=["dn_of", "dn_ss4b"], writes=["dn_of"])
                                P.op("pool", lambda e: e.tensor_tensor(out=of.rearrange("p (h v) -> p h v", h=4), in0=of.rearrange("p (h v) -> p h v", h=4), in1=dng_row.unsqueeze(1).to_broadcast([64, 4, 64]), op=ALU.mult), reads=["dn_of", "dn_small"], writes=["dn_of"])
                                P.op("dve", lambda e, c=c: e.tensor_tensor(out=oo, in0=of, in1=sg[:, c, :], op=ALU.mult), reads=["dn_of", "dn_sg"], writes=["dn_oo"])
                                ptr, ptrt = bank()
                                def otr(e, ptr=ptr):
                                    e.matmul(ptr[:, 0:64], lhsT=oo[:, 0:128], rhs=I64b, start=True, stop=True)
                                    return e.matmul(ptr[:, 64:128], lhsT=oo[:, 128:256], rhs=I64b, start=True, stop=True)
                                P.op("pe", otr, reads=["dn_oo", "matsb"], writes=[ptrt])
                                tks = slice(tok0 + c * 64, tok0 + c * 64 + 64)
                                P.op("act", lambda e, ptr=ptr, tks=tks: e.activation(out=mixT[:, 4:6, tks], in_=ptr[:, 0:128].rearrange("p (c t) -> p c t", c=2), func=ACTF.Copy), reads=[ptrt], writes=[f"mixT{tok0 // 512 + (c * 64) // 512}"])
                    if not is_sample:
                        P.dma(lambda e, si=si, d=d: e.dma_start(out=ndn_o[si, l, d].rearrange("h k v -> k h v"), in_=S.rearrange("p (h v) -> p h v", h=4)), reads=["dn_S"])

        dn_phase(0, 1024, [(0, 16)], True)
        P.barrier()
        dn_phase(1024, 512, [(0, 4), (4, 4)], False)
        if l == 0:
            dbg("oc", mixT[:, 4:6, :], [128, 2, T], MIXT)
        if stage < 6:
            break
        P.barrier()
        A.reset()

        P.mark(f"L{l} wout")
        for j in range(2):
            wp, wt = panel(("out", l, j))
            for o in range(4):
                oc = j * 4 + o
                for b in range(NB):
                    g = grp_of_blk(b)
                    bk, bt = bank()
                    def wof(e, bk=bk, wp=wp, o=o, b=b):
                        ins = None
                        for k in range(8):
                            ins = e.matmul(bk[:, :], lhsT=wp[:, k, o * 128:(o + 1) * 128], rhs=mixT[:, k, blk(b)], start=(k == 0), stop=(k == 7))
                        return ins
                    P.op("pe", wof, reads=[wt, f"mixT{b}"], writes=[bt])
                    P.op("dve", lambda e, bk=bk, oc=oc, b=b, g=g: e.scalar_tensor_tensor(out=xT[:, oc, blk(b)], in0=bk[:, :], scalar=modsT[:, l, 16 + oc, g:g + 1], in1=xT[:, oc, blk(b)], op0=ALU.mult, op1=ALU.add), reads=[bt, "modsT0", "modsT1", f"xT{b}"], writes=[f"xT{b}"])
        if l == 0:
            dbg("x1", xT[:], [128, 8, T], ["xT0", "xT1", "xT2"])
        if stage < 7:
            break
        P.barrier()
        A.reset()
        P.mark(f"L{l} ffn")
        norm_fm(lambda c, g: gsb[:, l, 1, c, g:g + 1], lambda c, g: modsT[:, l, 24 + c, g:g + 1],
                lambda c, b: hT[:, c, blk(b)], lambda b: f"hT{b}", f"n2_{l}")
        P.barrier()
        A.reset()
        actT = A.bf16(12 * T).rearrange("p (f t) -> p f t", f=12)
        sgt = [A.f32(512), A.f32(512)]
        for hf, (f0, f1) in enumerate(FF_SPLIT):
            nf = f1 - f0
            for jp in range(f0 // 2, f1 // 2):
                wp, wt = panel(("gu", l, jp))
                for fi in range(2):
                    f = 2 * jp + fi - f0
                    for b in range(NB):
                        bg, bgt = proj_fm(wp, wt, fi * 128, 128, b)
                        bu, but = proj_fm(wp, wt, 256 + fi * 128, 128, b)
                        st = sgt[(f + b) % 2]
                        stn = f"sgt{(f + b) % 2}"
                        P.op("act", lambda e, bg=bg, st=st: e.activation(out=st, in_=bg[:, :], func=ACTF.Silu), reads=[bgt], writes=[stn])
                        P.op("dve", lambda e, bu=bu, st=st, f=f, b=b: e.tensor_tensor(out=actT[:, f, blk(b)], in0=bu[:, :], in1=st, op=ALU.mult), reads=[but, stn], writes=[f"actT{b}"])
            for q in range(4):
                wp, wt = panel(("dn", l, hf, q))
                for o in range(2):
                    oc = q * 2 + o
                    for b in range(NB):
                        g = grp_of_blk(b)
                        bk, bt = bank()
                        def dnf(e, bk=bk, wp=wp, o=o, b=b, nf=nf):
                            ins = None
                            for k in range(nf):
                                ins = e.matmul(bk[:, :], lhsT=wp[:, k, o * 128:(o + 1) * 128], rhs=actT[:, k, blk(b)], start=(k == 0), stop=(k == nf - 1))
                            return ins
                        P.op("pe", dnf, reads=[wt, f"actT{b}"], writes=[bt])
                        P.op("dve", lambda e, bk=bk, oc=oc, b=b, g=g: e.scalar_tensor_tensor(out=xT[:, oc, blk(b)], in0=bk[:, :], scalar=modsT[:, l, 40 + oc, g:g + 1], in1=xT[:, oc, blk(b)], op0=ALU.mult, op1=ALU.add), reads=[bt, "modsT0", "modsT1", f"xT{b}"], writes=[f"xT{b}"])
        if l == 0:
            dbg("x2", xT[:], [128, 8, T], ["xT0", "xT1", "xT2"])
        if stage < 8:
            break

    P.mark("final")
    if stage >= 9:
        P.barrier()
        A.reset()
        yT = A.f32(8 * T).rearrange("p (c t) -> p c t", c=8)
        norm_fm(lambda c, g: gfin[:, c:c + 1], None, lambda c, b: yT[:, c, blk(b)], lambda b: f"yT{b}", "nf")
        ytm = [A.f32(1024), A.f32(1024)]
        for t in range(12):
            yt = ytm[t % 2]
            ytn = f"ytm{t % 2}"
            for half in range(2):
                bk, bt = bank()
                def trf2(e, bk=bk, t=t, half=half):
                    ins = None
                    for c in range(4):
                        ins = e.transpose(bk[:, c * 128:(c + 1) * 128], yT[:, half * 4 + c, t * 128:(t + 1) * 128], ident_f)
                    return ins
                P.op("pe", trf2, reads=[f"yT{t // 4}", "matsf"], writes=[bt])
                if half == 0:
                    P.op("act", lambda e, bk=bk, yt=yt: e.activation(out=yt[:, 0:512], in_=bk[:, :], func=ACTF.Copy), reads=[bt], writes=[ytn])
                else:
                    P.op("dve", lambda e, bk=bk, yt=yt: e.tensor_copy(out=yt[:, 512:1024], in_=bk[:, :]), reads=[bt], writes=[ytn])
            P.dma(lambda e, yt=yt, t=t: e.dma_start(out=y_o[t * 128:(t + 1) * 128, :], in_=yt), reads=[ytn])

    P.barrier()
    P.final_wait()
    P.mark("end")
    P.emit()
    build.marks = P.marks
    return nc, dbg_outs


_CACHE = {}


def kernel(**inputs):
    inputs = {k: np.asarray(v) for k, v in inputs.items()}
    sh, per = prep_inputs(inputs)
    if "nc" not in _CACHE:
        _CACHE["nc"] = build()[0]
    nc = _CACHE["nc"]
    in_maps = [{**sh, **per[c]} for c in range(NCORES)]
    res = run_bass_kernel_spmd(nc, in_maps, core_ids=list(range(NCORES)))
    R = res.results
    f = np.float32
    y_p = np.zeros((16, 256, 1024), f); y_s = np.zeros((8, 1024, 1024), f)
    ngk = np.zeros((16, NL, 256, 2, 64), f); ngv = np.zeros((16, NL, 256, 2, 64), f)
    nnk = np.zeros((16, NL, 256, 4, 64), f); nnv = np.zeros((16, NL, 256, 4, 64), f)
    ndn = np.zeros((16, NL, 2, 4, 64, 64), f); nckv = np.zeros((16, NL, 256, 128), f); nkr = np.zeros((16, NL, 256, 32), f)
    for c in range(NCORES):
        r = R[c]
        y = r["y_o"]
        y_s[c] = y[0:1024]
        y_p[2 * c] = y[1024:1280]; y_p[2 * c + 1] = y[1280:1536]
        for s in range(2):
            b = 2 * c + s
            ngk[b] = r["ngk_o"][s].reshape(NL, 256, 2, 64); ngv[b] = r["ngv_o"][s].reshape(NL, 256, 2, 64)
            nnk[b] = r["nnk_o"][s].reshape(NL, 256, 4, 64); nnv[b] = r["nnv_o"][s].reshape(NL, 256, 4, 64)
            ndn[b] = r["ndn_o"][s]; nckv[b] = r["nckv_o"][s]; nkr[b] = r["nkr_o"][s]
    return (y_p, y_s, ngk, ngv, nnk, nnv, ndn, nckv, nkr)
```

```python
import numpy as np
from contextlib import ExitStack
import concourse.bass as bass
import concourse.mybir as mybir
from concourse.bass_utils import run_bass_kernel_spmd

F32 = mybir.dt.float32
BF16 = mybir.dt.bfloat16
ALU = mybir.AluOpType
ACTF = mybir.ActivationFunctionType
AX = mybir.AxisListType

NCORES = 8
NL = 2
D = 1024
T = 1536
NB = 3
EPS = 1e-6
NEG = -30000.0
MLA_SCALE = 96 ** -0.5
ENGS = ("pe", "act", "dve", "pool", "sp")
EPOCH = 3000


class Tok:
    __slots__ = ("name", "w", "r")

    def __init__(self, name):
        self.name = name
        self.w = None
        self.r = {}


class _Rec:
    def __init__(self):
        self.calls = []

    def __getattr__(self, name):
        def f(*a, **k):
            self.calls.append((name, a, k))
            return self
        return f


def _record(fn):
    r = _Rec()
    fn(r)
    calls = r.calls
    assert calls

    def replay(e):
        ins = None
        for name, a, k in calls:
            ins = getattr(e, name)(*a, **k)
        return ins
    n = 0
    for name, a, k in calls:
        if name == "matmul" and k.get("lhsT") is not None and k["lhsT"].dtype == F32:
            n += 2
        else:
            n += 1
    replay.n = n
    return replay


class Prog:
    def __init__(self, nc, n_dma_sems=28):
        self.nc = nc
        self.es = ExitStack()
        self.ops = {e: [] for e in ENGS}
        self.cnt = {e: 0 for e in ENGS}
        self.sems = {e: [] for e in ENGS}
        self.known = {e: {} for e in ENGS}
        self.dma_sems = [self.es.enter_context(nc.semaphore(f"dq{i}")) for i in range(n_dma_sems)]
        self.dma_val = [0] * n_dma_sems
        self.dma_i = 0
        self.dma_cnt = {}
        self.toks = {}
        self.nbuf = 0
        self.bank_i = {"s": 0, "a": 0}
        self.n_pe = 0
        self.marks = []

    def sb(self, shape, dtype, name=None):
        self.nbuf += 1
        return self.es.enter_context(self.nc.sbuf_tensor(name or f"b{self.nbuf}", list(shape), dtype))

    def ps(self, shape, dtype, name=None):
        self.nbuf += 1
        return self.es.enter_context(self.nc.psum_tensor(name or f"p{self.nbuf}", list(shape), dtype))

    def tok(self, name):
        t = self.toks.get(name)
        if t is None:
            t = self.toks[name] = Tok(name)
        return t

    def _sem_for(self, eng, k):
        ep = (k - 1) // EPOCH
        while len(self.sems[eng]) <= ep:
            self.sems[eng].append(self.es.enter_context(
                self.nc.semaphore(f"s_{eng}_{len(self.sems[eng])}")))
        return self.sems[eng][ep], (k - 1) % EPOCH + 1

    def _need(self, waiter, dep, waits):
        if dep is None:
            return
        if dep[0] == "e":
            _, eng, k = dep
            if eng == waiter and eng == "pe":
                return
            kn = self.known[waiter].get(("e", eng), 0)
            if kn >= k:
                return
            self.known[waiter][("e", eng)] = k
            waits.append(dep)
        else:
            _, si, val = dep
            kn = self.known[waiter].get(("d", si), 0)
            if kn >= val:
                return
            self.known[waiter][("d", si)] = val
            waits.append(dep)

    def _collect(self, eng, reads, writes):
        waits = []
        for t in reads:
            self._need(eng, t.w, waits)
        for t in writes:
            self._need(eng, t.w, waits)
            for d in t.r.values():
                self._need(eng, d, waits)
        best = {}
        for w in waits:
            key = w[:2]
            if key not in best or best[key][2] < w[2]:
                best[key] = w
        return list(best.values())

    def _commit(self, dep, reads, writes):
        for t in reads:
            old = t.r.get(dep[:2])
            if old is None or old[2] < dep[2]:
                t.r[dep[:2]] = dep
        for t in writes:
            t.w = dep
            t.r = {}

    def _toks(self, names):
        return [self.tok(t) if isinstance(t, str) else t for t in names]

    def op(self, eng, fn, reads=(), writes=()):
        reads, writes = self._toks(reads), self._toks(writes)
        waits = self._collect(eng, reads, writes)
        self.cnt[eng] += 1
        k = self.cnt[eng]
        self._sem_for(eng, k)
        rp = _record(fn)
        if eng == "pe":
            self.n_pe += rp.n
        self.ops[eng].append((waits, rp, ("e", eng, k)))
        self._commit(("e", eng, k), reads, writes)

    def dma(self, fn, reads=(), writes=(), eng="sp"):
        reads, writes = self._toks(reads), self._toks(writes)
        waits = self._collect(eng, reads, writes)
        half = len(self.dma_sems) // 2
        cnt = self.dma_cnt.setdefault(eng, 0)
        self.dma_cnt[eng] = cnt + 1
        si = (cnt % half) + (0 if eng == "sp" else half)
        prev = self.dma_val[si]
        if prev:
            self._need(eng, ("d", si, prev), waits)
        self.dma_val[si] = prev + 16
        dep = ("d", si, prev + 16)
        self.ops[eng].append((waits, _record(fn), dep))
        self._commit(dep, reads, writes)

    def mark(self, name):
        self.marks.append((name, self.n_pe))

    def barrier(self):
        deps = [("e", e, self.cnt[e]) for e in ENGS if self.cnt[e]]
        deps += [("d", i, v) for i, v in enumerate(self.dma_val) if v]
        for e in ENGS:
            waits = []
            for d in deps:
                self._need(e, d, waits)
            if waits:
                self.ops[e].append((waits, None, None))

    def final_wait(self, eng="sp"):
        waits = []
        for i, v in enumerate(self.dma_val):
            if v:
                self._need(eng, ("d", i, v), waits)
        for e in ENGS:
            if self.cnt[e]:
                self._need(eng, ("e", e, self.cnt[e]), waits)
        self.ops[eng].append((waits, None, None))

    def _emit_engine(self, ename, e):
        for waits, fn, dep in self.ops[ename]:
            for w in waits:
                if w[0] == "e":
                    sem, val = self._sem_for(w[1], w[2])
                    e.wait_ge(sem, val)
                else:
                    e.wait_ge(self.dma_sems[w[1]], w[2])
            if fn is None:
                continue
            ins = fn(e)
            if dep[0] == "e":
                sem, _ = self._sem_for(dep[1], dep[2])
                ins.then_inc(sem, 1)
            else:
                ins.then_inc(self.dma_sems[dep[1]], 16)

    def emit(self):
        with self.nc.Block() as block:
            @block.tensor
            def _(e):
                self._emit_engine("pe", e)

            @block.scalar
            def _(e):
                self._emit_engine("act", e)

            @block.vector
            def _(e):
                self._emit_engine("dve", e)

            @block.gpsimd
            def _(e):
                self._emit_engine("pool", e)

            @block.sync
            def _(e):
                self._emit_engine("sp", e)
        self.es.close()


def _tile_k(w):
    k, n = w.shape
    return np.ascontiguousarray(w.reshape(k // 128, 128, n).transpose(1, 0, 2))


def _r(a, b):
    return list(range(a, b))


O_AQ, O_AK, O_AV, O_BQ, O_BK, O_BV, O_CQKV, O_CG, O_CA, O_CB, O_DCQ, O_DCKV, O_DKR = (
    0, 256, 384, 512, 768, 1024, 1280, 2048, 2304, 2312, 2320, 2576, 2704)

PANELS_IN = {}


def _def_panels():
    p = {}
    p["G"] = _r(0, 256) + _r(256, 320) * 2 + _r(320, 384) * 2
    p["GT"] = _r(384, 448) * 2 + _r(448, 512) * 2 + _r(O_AK, O_AK + 128)
    p["N"] = _r(O_BQ, O_BQ + 256) + _r(O_BK, O_BK + 256)
    nt = []
    for h in range(4):
        nt += _r(O_BV + 64 * h, O_BV + 64 * h + 64) * 2
    p["NT"] = nt
    p["MT"] = _r(O_DCKV, O_DCKV + 128) + _r(O_DKR, O_DKR + 32) + _r(O_BK, O_BK + 256)
    m = _r(O_DCQ, O_DCQ + 256) + _r(O_DCKV, O_DCKV + 128) + _r(O_DKR, O_DKR + 32) * 3
    p["M"] = m
    dq = []
    for h in range(4):
        dq += _r(O_CQKV + 64 * h, O_CQKV + 64 * h + 64)
    dk = []
    for h in range(4):
        dk += _r(O_CQKV + 256 + 64 * h, O_CQKV + 256 + 64 * h + 64)
    p["D1"] = dq + dk
    p["D2"] = _r(O_CQKV + 512, O_CQKV + 768) + _r(O_CG, O_CG + 256)
    p["DAB"] = _r(O_CA, O_CA + 8) + _r(O_CB, O_CB + 8)
    return p


PANELS_IN = _def_panels()
PANEL_ORDER = ["G", "GT", "N", "NT", "MT", "M", "D1", "D2", "DAB"]
PANEL_OFF = {}
_o = 0
for _n in PANEL_ORDER:
    PANEL_OFF[_n] = _o
    _o += len(PANELS_IN[_n])
NCD = _o

FF = 2816
NFF = 22
FF_SPLIT = [(0, 12), (12, 22)]


def _na_tables(bias):
    c = np.arange(64)
    c0 = np.clip(c - 8, 0, 48)
    kq = c[:, None] - c[None, :]
    ci = np.clip(kq + 15, 0, 30)
    allowed = (c[:, None] >= c0[None, :]) & (c[:, None] < c0[None, :] + 16)
    out = np.full((128, 4, 2, 18, 64), NEG, np.float32)
    for h in range(4):
        for e in range(0, 15):
            dr = 7 - e
            tile = np.where(allowed, bias[h, 14 - e][ci], np.float32(NEG)).astype(np.float32)
            for tab in range(2):
                if tab == 1 and not (-4 <= dr <= 3):
                    continue
                out[0:64, h, tab, e + 1, :] = tile
                out[64:128, h, tab, e + 2, :] = tile
    return out.reshape(128, 4, 2, 18 * 64)


def _rope_tables():
    t = np.arange(1024)
    rowp, colp = (t // 64).astype(np.float64), (t % 64).astype(np.float64)

    def tabs(nd, part0, ndim_total):
        cos = np.ones((ndim_total, 1024)); sin = np.zeros((ndim_total, 1024))
        rot = np.zeros((ndim_total, ndim_total))
        half = nd // 2
        q = half // 2
        inv = 10000.0 ** (-np.arange(q) / q)
        for ax, pos in enumerate((rowp, colp)):
            base = part0 + ax * half
            ang = inv[:, None] * pos[None, :]
            for i in range(q):
                a, b = base + i, base + q + i
                cos[a] = np.cos(ang[i]); cos[b] = np.cos(ang[i])
                sin[a] = np.sin(ang[i]); sin[b] = -np.sin(ang[i])
                rot[a, b] = 1.0
                rot[b, a] = 1.0
        return cos, sin, rot
    cg = np.ones((128, 1024)); sg = np.zeros((128, 1024)); rg = np.zeros((128, 128))
    for hh in range(2):
        c_, s_, r_ = tabs(64, 64 * hh, 128)
        m = slice(64 * hh, 64 * hh + 64)
        cg[m] = c_[m]; sg[m] = s_[m]; rg[m, m] = r_[m, m]
    cm, sm, rm = tabs(32, 64, 128)
    f = np.float32
    return cg.astype(f), sg.astype(f), cm.astype(f), sm.astype(f), rg.astype(f), rm.astype(f)


def _consts():
    cg, sg, cm, sm, rg, rm = _rope_tables()
    ident = np.eye(128, dtype=np.float32)
    ones = np.ones((128, 128), np.float32)
    blk = np.zeros((128, 128), np.float32)
    blk[:64, :64] = 1; blk[64:, 64:] = 1
    p = np.arange(64)[:, None]; fr = np.arange(64)[None, :]
    masks = np.stack([(fr < p), (fr > p), (fr <= p), (fr >= p)]).astype(np.float32)
    masks128 = np.zeros((128, 4, 64), np.float32)
    masks128[:64] = masks.transpose(1, 0, 2)
    return {
        "c_rope": np.ascontiguousarray(np.stack([cg, sg, cm, sm], 1)),
        "c_mats": np.ascontiguousarray(np.stack([ident, ones, blk, rg, rm], 1)),
        "c_masks": masks128,
    }


def prep_inputs(inp):
    f = np.float32
    sh = dict(_consts())
    w_in = inp["w_in"]
    cols = []
    for n in PANEL_ORDER:
        cols += PANELS_IN[n]
    cols = np.asarray(cols)
    sh["w_in_d"] = np.stack([_tile_k(w_in[l][:, cols]) for l in range(NL)])
    sh["w_mod_d"] = np.stack([_tile_k(inp["w_mod"][l]) for l in range(NL)])
    sh["b_mod_d"] = np.ascontiguousarray(np.broadcast_to(inp["b_mod"][:, None, :], (NL, 2, 6144))).astype(f)
    sh["w_out_d"] = np.stack([_tile_k(inp["w_out"][l]) for l in range(NL)])
    gu = []
    for l in range(NL):
        g = _tile_k(inp["ffn_w_gate"][l]).reshape(128, 8, 11, 256)
        u = _tile_k(inp["ffn_w_up"][l]).reshape(128, 8, 11, 256)
        gu.append(np.concatenate([g, u], -1).reshape(128, 8, 11 * 512))
    sh["w_gu_d"] = np.stack(gu)
    sh["w_dn_d"] = np.stack([_tile_k(inp["ffn_w_down"][l]) for l in range(NL)])
    sh["wq_d"] = np.stack([_tile_k(inp["mla_wq_up"][l]) for l in range(NL)])
    wkv = inp["mla_wkv_up"]
    kcols = []
    vcols = []
    for h in range(4):
        kcols += _r(128 * h, 128 * h + 64)
        vcols += _r(128 * h + 64, 128 * h + 128) * 2
    sh["wkv_d"] = np.ascontiguousarray(wkv[:, :, np.asarray(kcols + vcols)])
    def fm(v):
        L_, n = v.shape
        return np.ascontiguousarray(v.reshape(L_, n // 128, 128).transpose(2, 0, 1))
    sh["g1_d"] = fm(inp["norm1_g"]); sh["g2_d"] = fm(inp["norm2_g"])
    sh["gf_d"] = fm(inp["final_g"][None])
    sh["gqg_d"] = np.ascontiguousarray(np.stack([np.tile(inp["gqa_qn_g"], (1, 2)), np.tile(inp["gqa_kn_g"], (1, 2))], -1).transpose(1, 0, 2))
    sh["gqk_row_d"] = np.ascontiguousarray(np.broadcast_to(inp["gqa_kn_g"][None, :, None, :], (128, NL, 2, 64))).astype(f)
    sh["mqg_d"] = fm(inp["mla_qn_g"])
    sh["mkg_d"] = fm(inp["mla_kvn_g"])
    sh["mkg_row_d"] = np.ascontiguousarray(np.broadcast_to(inp["mla_kvn_g"][None], (128, NL, 128))).astype(f)
    cw = inp["dn_conv_w"]
    cq = cw[:, :, 0:512].reshape(NL, 4, 8, 64).transpose(3, 0, 2, 1)
    sh["cwqk_d"] = np.ascontiguousarray(cq)
    cv = cw[:, :, 512:768].reshape(NL, 4, 2, 128).transpose(3, 0, 2, 1)
    sh["cwv_d"] = np.ascontiguousarray(cv)
    sh["dng_row_d"] = np.ascontiguousarray(np.broadcast_to(inp["dn_out_g"][None], (64, NL, 64))).astype(f)
    sh["dnal_d"] = np.ascontiguousarray(np.broadcast_to(inp["dn_a_log"].reshape(1, NL, 8), (64, NL, 8))).astype(f)
    sh["dndt_d"] = np.ascontiguousarray(np.broadcast_to(inp["dn_dt_bias"].reshape(1, NL, 8), (64, NL, 8))).astype(f)
    sh["natab_d"] = np.stack([_na_tables(inp["na_bias"][l]) for l in range(NL)])
    per = []
    for c in range(NCORES):
        d = {}
        d["x_d"] = np.ascontiguousarray(np.concatenate([inp["x_sample"][c], inp["x_prompt"][2 * c], inp["x_prompt"][2 * c + 1]], 0))
        cond = np.stack([inp["c"][c], inp["c_ctx"]], -1)
        d["cond_d"] = np.ascontiguousarray(cond.reshape(8, 128, 2).transpose(1, 0, 2))
        gk = inp["cache_gqa_k"][c]
        d["cgk_d"] = np.ascontiguousarray(gk[:, :, [0, 0, 1, 1], :].reshape(NL, 512, 256))
        gv = inp["cache_gqa_v"][c]
        d["cgv_d"] = np.ascontiguousarray(gv[:, :, [0, 0, 1, 1], :].reshape(NL, 512, 256))
        d["cnk_d"] = np.ascontiguousarray(inp["cache_na_k"][c].reshape(NL, 512, 256))
        nv = inp["cache_na_v"][c]
        d["cnv_d"] = np.ascontiguousarray(nv[:, :, [0, 0, 1, 1, 2, 2, 3, 3], :].reshape(NL, 512, 512))
        d["sdn_d"] = np.ascontiguousarray(inp["state_dn"][c])
        d["cckv_d"] = np.ascontiguousarray(inp["cache_mla_ckv"][c])
        d["ckr_d"] = np.ascontiguousarray(inp["cache_mla_krope"][c])
        per.append(d)
    return sh, per


class Arena:
    def __init__(self, ten, n):
        self.ten = ten
        self.n = n
        self.off = 0

    def reset(self, off=0):
        self.off = off

    def f32(self, n, parts=128):
        assert self.off + n <= self.n, (self.off, n, self.n)
        ap = self.ten[0:parts, self.off:self.off + n]
        self.off += n
        return ap

    def bf16(self, n, parts=128):
        m = (n + 1) // 2
        assert self.off + m <= self.n, (self.off, m, self.n)
        ap = self.ten[0:parts, self.off:self.off + m].bitcast(BF16)
        self.off += m
        return ap[:, 0:n]


def build(stage=99, debug=(), skip=()):
    nc = bass.Bass("TRN2", target_bir_lowering=False)
    P = Prog(nc)
    dbg_outs = {}

    def din(name, shape):
        return nc.dram_tensor(name, list(shape), F32, kind="ExternalInput").ap()

    def dout(name, shape):
        return nc.dram_tensor(name, list(shape), F32, kind="ExternalOutput").ap()

    x_d = din("x_d", [T, D]); cond_d = din("cond_d", [128, 8, 2])
    cgk_d = din("cgk_d", [NL, 512, 256]); cgv_d = din("cgv_d", [NL, 512, 256])
    cnk_d = din("cnk_d", [NL, 512, 256]); cnv_d = din("cnv_d", [NL, 512, 512])
    sdn_d = din("sdn_d", [NL, 2, 4, 64, 64]); cckv_d = din("cckv_d", [NL, 512, 128]); ckr_d = din("ckr_d", [NL, 512, 32])
    c_rope = din("c_rope", [128, 4, 1024]); c_mats = din("c_mats", [128, 5, 128]); c_masks = din("c_masks", [128, 4, 64])
    w_in_d = din("w_in_d", [NL, 128, 8, NCD]); w_mod_d = din("w_mod_d", [NL, 128, 8, 6144]); b_mod_d = din("b_mod_d", [NL, 2, 6144])
    w_out_d = din("w_out_d", [NL, 128, 8, 1024]); w_gu_d = din("w_gu_d", [NL, 128, 8, 5632]); w_dn_d = din("w_dn_d", [NL, 128, 22, 1024])
    wq_d = din("wq_d", [NL, 128, 2, 384]); wkv_d = din("wkv_d", [NL, 128, 768])
    g1_d = din("g1_d", [128, NL, 8]); g2_d = din("g2_d", [128, NL, 8]); gf_d = din("gf_d", [128, 1, 8])
    gqg_d = din("gqg_d", [128, NL, 2]); gqk_row_d = din("gqk_row_d", [128, NL, 2, 64])
    mqg_d = din("mqg_d", [128, NL, 2]); mkg_d = din("mkg_d", [128, NL, 1]); mkg_row_d = din("mkg_row_d", [128, NL, 128])
    cwqk_d = din("cwqk_d", [64, NL, 8, 4]); cwv_d = din("cwv_d", [128, NL, 2, 4])
    dng_row_d = din("dng_row_d", [64, NL, 64]); dnal_d = din("dnal_d", [64, NL, 8]); dndt_d = din("dndt_d", [64, NL, 8])
    natab_d = din("natab_d", [NL, 128, 4, 2, 1152])

    y_o = dout("y_o", [T, D])
    ngk_o = dout("ngk_o", [2, NL, 256, 128]); ngv_o = dout("ngv_o", [2, NL, 256, 128])
    nnk_o = dout("nnk_o", [2, NL, 256, 256]); nnv_o = dout("nnv_o", [2, NL, 256, 256])
    ndn_o = dout("ndn_o", [2, NL, 2, 4, 64, 64]); nckv_o = dout("nckv_o", [2, NL, 256, 128]); nkr_o = dout("nkr_o", [2, NL, 256, 32])

    xT = P.sb([128, 8, T], F32, "xT")
    hT = P.sb([128, 8, T], BF16, "hT")
    mixT = P.sb([128, 8, T], BF16, "mixT")
    NSLOT = 3
    slots = [P.sb([128, 4096], BF16, f"wslot{i}") for i in range(NSLOT)]
    rope = P.sb([128, 4, 1024], BF16, "rope")
    matsb = P.sb([128, 5, 128], BF16, "matsb")
    matsf = P.sb([128, 2, 128], F32, "matsf")
    masks = P.sb([128, 4, 64], F32, "masks")
    modsT = P.sb([128, NL, 48, 2], F32, "modsT")
    gsb = P.sb([128, NL, 2, 8, 2], F32, "gsb")
    g12 = P.sb([128, 2, NL, 8], F32, "g12")
    gfin = P.sb([128, 8], F32, "gfin")
    smallp = P.sb([128, NL, 8], F32, "smallp")
    scT = P.sb([128, 16], BF16, "scT")
    ARENA_N = 19200
    arena_t = P.sb([128, ARENA_N], F32, "arena")
    A = Arena(arena_t, ARENA_N)
    banks = [P.ps([128, 512], F32, f"bank{i}") for i in range(8)]

    ident_b = matsb[:, 0, :]; ones_b = matsb[:, 1, :]; blk_b = matsb[:, 2, :]; rotG = matsb[:, 3, :]; rotM = matsb[:, 4, :]
    ident_f = matsf[:, 0, :]; ones_f = matsf[:, 1, :]
    cosG = rope[:, 0, :]; sinG = rope[:, 1, :]; cosM = rope[:, 2, :]; sinM = rope[:, 3, :]

    def bank(kind="s"):
        if kind == "s":
            i = P.bank_i["s"] % 5
            P.bank_i["s"] += 1
        else:
            i = 5 + P.bank_i["a"] % 3
            P.bank_i["a"] += 1
        return banks[i], f"bank{i}"

    def grp_of_blk(b):
        return 0 if b < 2 else 1

    def blk(b):
        return slice(b * 512, (b + 1) * 512)

    sched = []

    def add_panel(key, ap, nk, ncols):
        sched.append((key, ap, nk, ncols))

    for j in range(4):
        add_panel(("mod", 0, j), w_mod_d[0][:, :, j * 512:(j + 1) * 512], 8, 512)
    for l in range(NL):
        for n in ["G", "GT", "MODS", "N", "NT", "MT", "M", "D1", "D2", "DAB", "D1", "D2", "DAB"]:
            if n == "MODS":
                if l == 0:
                    for j in range(4, 12):
                        add_panel(("mod", 0, j), w_mod_d[0][:, :, j * 512:(j + 1) * 512], 8, 512)
                    for l2 in range(1, NL):
                        for j in range(12):
                            add_panel(("mod", l2, j), w_mod_d[l2][:, :, j * 512:(j + 1) * 512], 8, 512)
                continue
            o = PANEL_OFF[n]
            add_panel(("in", l, n), w_in_d[l][:, :, o:o + len(PANELS_IN[n])], 8, len(PANELS_IN[n]))
        for j in range(2):
            add_panel(("out", l, j), w_out_d[l][:, :, j * 512:(j + 1) * 512], 8, 512)
        for hf, (f0, f1) in enumerate(FF_SPLIT):
            for j in range(f0 // 2, f1 // 2):
                add_panel(("gu", l, j), w_gu_d[l][:, :, j * 512:(j + 1) * 512], 8, 512)
            for q in range(4):
                add_panel(("dn", l, hf, q), w_dn_d[l][:, f0:f1, q * 256:(q + 1) * 256], f1 - f0, 256)
    pstate = {"issued": 0, "next": 0}
    LOOKAHEAD = 2

    def _issue_panel(i):
        key, ap, nk, ncols = sched[i]
        s = i % NSLOT
        dst = slots[s][:, 0:nk * ncols].rearrange("p (k n) -> p k n", k=nk)
        P.dma(lambda e, dst=dst, ap=ap: e.dma_start(out=dst, in_=ap), writes=[f"wslot{s}"], eng="pool")

    def panel(key):
        i = pstate["next"]
        assert sched[i][0] == key, (sched[i][0], key)
        while pstate["issued"] < min(len(sched), i + LOOKAHEAD):
            _issue_panel(pstate["issued"])
            pstate["issued"] += 1
        pstate["next"] += 1
        s = i % NSLOT
        _, _, nk, ncols = sched[i]
        return slots[s][:, 0:nk * ncols].rearrange("p (k n) -> p k n", k=nk), f"wslot{s}"

    def dbg(name, ap, shape, toks):
        if name not in debug:
            return
        o = dout("dbg_" + name, shape)
        P.dma(lambda e: e.dma_start(out=o, in_=ap), reads=toks, eng="pool")
        dbg_outs[name] = shape

    P.dma(lambda e: e.dma_start(out=rope[:], in_=c_rope), writes=["rope"], eng="pool")
    P.dma(lambda e: e.dma_start(out=matsb[:], in_=c_mats), writes=["matsb"], eng="pool")
    P.dma(lambda e: e.dma_start(out=matsf[:], in_=c_mats[:, 0:2, :]), writes=["matsf"])
    P.dma(lambda e: e.dma_start(out=masks[:], in_=c_masks), writes=["masks"])
    P.dma(lambda e: e.dma_start(out=g12[:, 0], in_=g1_d), writes=["g12"])
    P.dma(lambda e: e.dma_start(out=g12[:, 1], in_=g2_d), writes=["g12"])
    P.dma(lambda e: e.dma_start(out=gfin[:], in_=gf_d[:, 0, :]), writes=["gfin"])
    P.dma(lambda e: e.dma_start(out=smallp[:, :, 0:2], in_=gqg_d, allow_slow_non_contiguous=True), writes=["smallp"])
    P.dma(lambda e: e.dma_start(out=smallp[:, :, 2:4], in_=mqg_d, allow_slow_non_contiguous=True), writes=["smallp"])
    P.dma(lambda e: e.dma_start(out=smallp[:, :, 4:5], in_=mkg_d, allow_slow_non_contiguous=True), writes=["smallp"])
    CONST = ["rope", "matsb", "matsf", "masks", "smallp"]

    A.reset()
    xin = [A.f32(1024), A.f32(1024)]
    for t in range(12):
        xi = xin[t % 2]
        tk = f"xin{t % 2}"
        P.dma(lambda e, xi=xi, t=t: e.dma_start(out=xi, in_=x_d[t * 128:(t + 1) * 128, :]), writes=[tk])
        for half in range(2):
            bk, bt = bank()
            def tr(e, xi=xi, bk=bk, half=half):
                ins = None
                for c in range(4):
                    cc = half * 4 + c
                    ins = e.transpose(bk[:, c * 128:(c + 1) * 128], xi[:, cc * 128:(cc + 1) * 128], ident_f)
                return ins
            P.op("pe", tr, reads=[tk, "matsf"], writes=[bt])
            dst = xT[:, half * 4:half * 4 + 4, t * 128:(t + 1) * 128]
            src = bk[:, :].rearrange("p (c n) -> p c n", c=4)
            eng = "act" if half == 0 else "dve"
            if eng == "act":
                P.op("act", lambda e, dst=dst, src=src: e.activation(out=dst, in_=src, func=ACTF.Copy), reads=[bt], writes=[f"xT{t // 4}"])
            else:
                P.op("dve", lambda e, dst=dst, src=src: e.tensor_copy(out=dst, in_=src), reads=[bt], writes=[f"xT{t // 4}"])

    P.mark("mods")
    condf = A.f32(16); tmp16 = A.f32(16)
    condf3 = condf.rearrange("p (k g) -> p k g", k=8); scT3 = scT[:, :].rearrange("p (k g) -> p k g", k=8)
    P.dma(lambda e: e.dma_start(out=condf3, in_=cond_d), writes=["condf"])
    P.op("act", lambda e: e.activation(out=tmp16, in_=condf, func=ACTF.Exp, scale=-1.0), reads=["condf"], writes=["tmp16"])
    P.op("dve", lambda e: e.tensor_scalar_add(out=tmp16, in0=tmp16, scalar1=1.0), reads=["tmp16"], writes=["tmp16"])
    P.op("dve", lambda e: e.reciprocal(out=tmp16, in_=tmp16), reads=["tmp16"], writes=["tmp16"])
    P.op("dve", lambda e: e.tensor_tensor(out=scT[:, :], in0=condf, in1=tmp16, op=ALU.mult), reads=["tmp16", "condf"], writes=["scT"])
    modbuf = {"rowm": [A.f32(512, parts=2), A.f32(512, parts=2)], "bmr": [A.f32(512, parts=2), A.f32(512, parts=2)]}

    def mods_step(l, j):
        rowm, bmr = modbuf["rowm"], modbuf["bmr"]
        wp, wt = panel(("mod", l, j))
        bk, bt = bank()
        i2 = j % 2
        P.dma(lambda e: e.dma_start(out=bmr[i2], in_=b_mod_d[l][:, j * 512:(j + 1) * 512]), writes=[f"bmr{i2}"])
        def mm(e):
            ins = None
            for k in range(8):
                ins = e.matmul(bk[0:2, :], lhsT=scT3[:, k, :], rhs=wp[:, k, :], start=(k == 0), stop=(k == 7))
            return ins
        P.op("pe", mm, reads=[wt, "scT"], writes=[bt])
        P.op("dve", lambda e: e.tensor_tensor(out=rowm[i2], in0=bk[0:2, :], in1=bmr[i2], op=ALU.add), reads=[bt, f"bmr{i2}"], writes=[f"rowm{i2}"])
        bk2, bt2 = bank()
        def trm(e):
            ins = None
            for c in range(4):
                ins = e.matmul(bk2[:, 2 * c:2 * c + 2], lhsT=rowm[i2][:, c * 128:(c + 1) * 128], rhs=ident_f[0:2, 0:2], start=True, stop=True)
            return ins
        P.op("pe", trm, reads=[f"rowm{i2}", "matsf"], writes=[bt2])
        P.op("act", lambda e: e.activation(out=modsT[:, l, 4 * j:4 * j + 4, :], in_=bk2[:, 0:8].rearrange("p (c g) -> p c g", c=4), func=ACTF.Copy), reads=[bt2], writes=[f"modsT{l}"])

    def mods_finish(l):
        for w, base in ((0, 8), (1, 32)):
            P.op("dve", lambda e, w=w, base=base: e.tensor_scalar_add(out=gsb[:, l, w], in0=modsT[:, l, base:base + 8, :], scalar1=1.0), reads=[f"modsT{l}"], writes=[f"gsb{l}"])
            P.op("dve", lambda e, w=w: e.tensor_tensor(out=gsb[:, l, w], in0=gsb[:, l, w], in1=g12[:, w, l, :].unsqueeze(2).to_broadcast([128, 8, 2]), op=ALU.mult), reads=[f"gsb{l}", "g12"], writes=[f"gsb{l}"])

    def mods_gs(l, which):
        w, base = ((0, 8), (1, 32))[which]
        P.op("dve", lambda e: e.tensor_scalar_add(out=gsb[:, l, w], in0=modsT[:, l, base:base + 8, :], scalar1=1.0), reads=[f"modsT{l}"], writes=[f"gsb{l}"])
        P.op("dve", lambda e: e.tensor_tensor(out=gsb[:, l, w], in0=gsb[:, l, w], in1=g12[:, w, l, :].unsqueeze(2).to_broadcast([128, 8, 2]), op=ALU.mult), reads=[f"gsb{l}", "g12"], writes=[f"gsb{l}"])

    for j in range(4):
        mods_step(0, j)
    mods_gs(0, 0)
    dbg("modsT", modsT[:], [128, NL, 48, 2], ["modsT0", "modsT1"])
    dbg("xT", xT[:], [128, 8, T], ["xT0", "xT1", "xT2"])

    def rstd_from_ss(ps_ap, n, dst, rtoks, wtok):
        P.op("act", lambda e: e.activation(out=dst, in_=ps_ap, func=ACTF.Ln, bias=EPS, scale=1.0 / n), reads=rtoks, writes=[wtok])
        P.op("act", lambda e: e.activation(out=dst, in_=dst, func=ACTF.Exp, scale=-0.5), reads=[wtok], writes=[wtok])

    def norm_fm(gs_ap_fn, shift_ap_fn, out_fn, out_tok_fn, tmpn):
        sq = [A.bf16(512), A.bf16(512)]
        rst = A.f32(512)
        tmpf = [A.f32(512), A.f32(512)]
        for b in range(NB):
            g = grp_of_blk(b)
            bk, bt = bank("a")
            for c in range(8):
                s = sq[c % 2]
                P.op("act", lambda e, s=s, c=c, b=b: e.activation(out=s, in_=xT[:, c, blk(b)], func=ACTF.Square), reads=[f"xT{b}"], writes=[f"{tmpn}sq{c % 2}"])
                P.op("pe", lambda e, s=s, c=c, bk=bk: e.matmul(bk[:, :], lhsT=ones_b, rhs=s, start=(c == 0), stop=(c == 7)), reads=[f"{tmpn}sq{c % 2}", "matsb"], writes=[bt])
            rstd_from_ss(bk[:, :], float(D), rst, [bt], f"{tmpn}rst")
            for c in range(8):
                tf = tmpf[c % 2]
                P.op("dve", lambda e, tf=tf, c=c, b=b: e.tensor_tensor(out=tf, in0=xT[:, c, blk(b)], in1=rst, op=ALU.mult), reads=[f"xT{b}", f"{tmpn}rst"], writes=[f"{tmpn}tf{c % 2}"])
                o = out_fn(c, b)
                if shift_ap_fn is not None:
                    P.op("act", lambda e, tf=tf, o=o, c=c, g=g: e.activation(out=o, in_=tf, func=ACTF.Identity, bias=shift_ap_fn(c, g), scale=gs_ap_fn(c, g)), reads=[f"{tmpn}tf{c % 2}", "gsb0", "gsb1", "modsT0", "modsT1", "gfin"], writes=[out_tok_fn(b)])
                else:
                    P.op("act", lambda e, tf=tf, o=o, c=c, g=g: e.activation(out=o, in_=tf, func=ACTF.Copy, scale=gs_ap_fn(c, g)), reads=[f"{tmpn}tf{c % 2}", "gsb0", "gsb1", "modsT0", "modsT1", "gfin"], writes=[out_tok_fn(b)])

    def proj_fm(wp, wt, c0, m, b, kind="s"):
        bk, bt = bank(kind)
        def f(e):
            ins = None
            for k in range(8):
                ins = e.matmul(bk[0:m, :], lhsT=wp[:, k, c0:c0 + m], rhs=hT[:, k, blk(b)], start=(k == 0), stop=(k == 7))
            return ins
        P.op("pe", f, reads=[wt, f"hT{b}"], writes=[bt])
        return bk, bt

    def proj_tm(wp, wt, c0, n, t0, m=128, kind="s", bk_bt=None, col0=0):
        bk, bt = bk_bt if bk_bt is not None else bank(kind)
        def f(e):
            ins = None
            for k in range(8):
                ins = e.matmul(bk[0:m, col0:col0 + n], lhsT=hT[:, k, t0:t0 + m], rhs=wp[:, k, c0:c0 + n], start=(k == 0), stop=(k == 7))
            return ins
        P.op("pe", f, reads=[wt, f"hT{t0 // 512}"], writes=[bt])
        return bk, bt

    HT = ["hT0", "hT1", "hT2"]
    MIXT = ["mixT0", "mixT1", "mixT2"]

    def attend(QT, nq, chunks, dst, half, rd_toks, wr_tok, tmp):
        bo, bot = bank("a"); bd, bdt = bank("a")
        n = len(chunks)
        pendq = []
        sl = slice(half * 64, half * 64 + 64)
        for i, (KT, V, brhs) in enumerate(chunks):
            bs, bst = bank()
            def qk(e, KT=KT, bs=bs, brhs=brhs):
                ins = e.matmul(bs[:, 0:nq], lhsT=KT, rhs=QT, start=True, stop=(brhs is None))
                if brhs is not None:
                    ins = e.matmul(bs[:, 0:nq], lhsT=ident_b, rhs=brhs, start=False, stop=True)
                return ins
            P.op("pe", qk, reads=rd_toks + ["matsb"], writes=[bst])
            if len(pendq) >= 2:
                pendq.pop(0)()
            pt = tmp["pt"][i % len(tmp["pt"])]
            ptt = f"{tmp['name']}pt{i % len(tmp['pt'])}"
            P.op("act", lambda e, pt=pt, bs=bs: e.activation(out=pt[:, 0:nq], in_=bs[:, 0:nq], func=ACTF.Exp), reads=[bst], writes=[ptt])
            acc, acct = tmp["acc"], tmp["acc_tok"]
            if i == 0:
                P.op("dve", lambda e, pt=pt: e.tensor_copy(out=acc[:, 0:nq], in_=pt[:, 0:nq]), reads=[ptt], writes=[acct])
            else:
                P.op("dve", lambda e, pt=pt: e.tensor_tensor(out=acc[:, 0:nq], in0=acc[:, 0:nq], in1=pt[:, 0:nq], op=ALU.add), reads=[ptt, acct], writes=[acct])
            def pv(i=i, V=V, pt=pt, ptt=ptt):
                P.op("pe", lambda e: e.matmul(bo[:, 0:nq], lhsT=V, rhs=pt[:, 0:nq], start=(i == 0), stop=(i == n - 1)), reads=rd_toks + [ptt], writes=[bot])
            pendq.append(pv)
        for f_ in pendq:
            f_()
        hi, lo = tmp["hl"]
        hit, lot = tmp["name"] + "hi", tmp["name"] + "lo"
        P.op("act", lambda e: e.activation(out=hi[:, 0:nq], in_=acc[:, 0:nq], func=ACTF.Copy), reads=[acct], writes=[hit])
        P.op("dve", lambda e: e.tensor_tensor(out=lo[:, 0:nq], in0=acc[:, 0:nq], in1=hi[:, 0:nq], op=ALU.subtract), reads=[acct, hit], writes=[lot])
        def denf(e):
            e.matmul(bd[:, 0:nq], lhsT=ones_b, rhs=hi[:, 0:nq], start=True, stop=False)
            return e.matmul(bd[:, 0:nq], lhsT=ones_b, rhs=lo[:, 0:nq], start=False, stop=True)
        P.op("pe", denf, reads=[hit, lot, "matsb"], writes=[bdt])
        rd = tmp["rd"]
        P.op("dve", lambda e: e.reciprocal(out=rd[sl, 0:nq], in_=bd[sl, 0:nq]), reads=[bdt], writes=[tmp["name"] + "rd"])
        P.op("dve", lambda e: e.tensor_tensor(out=dst, in0=bo[sl, 0:nq], in1=rd[sl, 0:nq], op=ALU.mult), reads=[bot, tmp["name"] + "rd"], writes=[wr_tok])

    ARENA0 = A.off

    def load_cache_T(src_dram, ncol, dstT_fn, name, pad_to=None):
        pass

    for l in range(NL):
        if stage < 1:
            break
        P.barrier()
        A.reset()
        P.mark(f"L{l} norm1")
        norm_fm(lambda c, g: gsb[:, l, 0, c, g:g + 1], lambda c, g: modsT[:, l, c, g:g + 1],
                lambda c, b: hT[:, c, blk(b)], lambda b: f"hT{b}", f"n1_{l}")
        if l == 0:
            dbg("hT", hT[:], [128, 8, T], HT)
        if stage < 2:
            break
        P.barrier()
        A.reset()

        P.mark(f"L{l} gqa")
        QT = A.bf16(2 * T).rearrange("p (c t) -> p c t", c=2)
        KT = A.bf16(2 * 2048).rearrange("p (c t) -> p c t", c=2)
        Vg = A.bf16(16 * 256).rearrange("p (t c) -> p t c", t=16)
        ctm = A.bf16(4 * 256).rearrange("p (t c) -> p t c", t=4)
        tqs = [{"sq": A.bf16(512), "f1": A.f32(512), "f2": A.f32(512), "xc": A.bf16(512), "xs": A.bf16(512), "n": f"tq{i}_"} for i in range(3)]
        tqi = [0]
        tq = tqs[0]
        atmp = {"name": f"ga{l}", "pt": [A.bf16(512), A.bf16(512), A.bf16(512)], "rd": A.f32(512), "acc": A.f32(512), "acc_tok": "att_acc", "hl": (A.bf16(512), A.bf16(512))}
        otile = [A.f32(256), A.f32(256)]
        krow = A.f32(2 * 64).rearrange("p (h d) -> p h d", h=2)
        P.dma(lambda e: e.dma_start(out=krow, in_=gqk_row_d[:, l]), writes=["krow"])
        if l == 0:
            modbuf["rowm"] = [A.f32(512, parts=2), A.f32(512, parts=2)]
            modbuf["bmr"] = [A.f32(512, parts=2), A.f32(512, parts=2)]
        P.dma(lambda e: e.dma_start(out=ctm, in_=cgk_d[l].rearrange("(t p) c -> p t c", p=128)), writes=["ctm"], eng="pool")
        P.dma(lambda e: e.dma_start(out=Vg[:, 0:4, :], in_=cgv_d[l].rearrange("(t p) c -> p t c", p=128)), writes=["Vg"], eng="pool")
        for kv in range(2):
            bk, bt = bank()
            bkb = bk[:, :].bitcast(BF16)
            def trc(e, kv=kv, bkb=bkb):
                ins = None
                for t in range(4):
                    ins = e.transpose(bkb[:, t * 128:(t + 1) * 128], ctm[:, t, kv * 128:(kv + 1) * 128], ident_b)
                return ins
            P.op("pe", trc, reads=["ctm", "matsb"], writes=[bt])
            P.op("act", lambda e, kv=kv, bkb=bkb: e.activation(out=KT[:, kv, 0:512], in_=bkb[:, 0:512], func=ACTF.Copy), reads=[bt], writes=["KT"])

        def qk_chunk(bk, bt, gain_ap, extra, do_rope, dst, t0, wtok, nparts=128, blkmat=None, cos=None, sin=None, rot=None, n=512):
            blkmat = blk_b if blkmat is None else blkmat
            tqi[0] += 1
            tq = tqs[tqi[0] % 2]
            tn = tq["n"]
            P.op("act", lambda e: e.activation(out=tq["sq"][:, 0:n], in_=bk[:, 0:n], func=ACTF.Square), reads=[bt], writes=[tn + "sq"])
            b2, b2t = bank()
            P.op("pe", lambda e: e.matmul(b2[:, 0:n], lhsT=blkmat, rhs=tq["sq"][:, 0:n], start=True, stop=True), reads=[tn + "sq", "matsb"], writes=[b2t])
            rstd_from_ss(b2[:, 0:n], 64.0, tq["f1"][:, 0:n], [b2t], tn + "f1")
            P.op("dve", lambda e: e.tensor_tensor(out=tq["f2"][:, 0:n], in0=bk[:, 0:n], in1=tq["f1"][:, 0:n], op=ALU.mult), reads=[bt, tn + "f1"], writes=[tn + "f2"])
            if not do_rope:
                P.op("dve", lambda e: e.tensor_scalar(out=dst, in0=tq["f2"][:, 0:n], scalar1=gain_ap, scalar2=extra, op0=ALU.mult, op1=ALU.mult), reads=[tn + "f2", "smallp"], writes=[wtok])
                return
            P.op("dve", lambda e: e.tensor_scalar(out=tq["f2"][:, 0:n], in0=tq["f2"][:, 0:n], scalar1=gain_ap, scalar2=extra, op0=ALU.mult, op1=ALU.mult), reads=[tn + "f2", "smallp"], writes=[tn + "f2"])
            rope_apply(tq["f2"][:, 0:n], dst, t0, wtok, cosG, sinG, rotG, [tn + "f2"], n, tq=tq)

        def rope_apply(src, dst, t0, wtok, cos, sin, rot, rtoks, n, parts=128, pre=None, tq=None):
            if tq is None:
                tqi[0] += 1
                tq = tqs[tqi[0] % 2]
            tn = tq["n"]
            xc = tq["xc"][0:parts, 0:n]; xs = tq["xs"][0:parts, 0:n]
            if pre is None:
                P.op("dve", lambda e: e.tensor_tensor(out=xc, in0=src, in1=cos[0:parts, t0:t0 + n], op=ALU.mult), reads=rtoks + ["rope"], writes=[tn + "xc"])
                P.op("dve", lambda e: e.tensor_tensor(out=xs, in0=src, in1=sin[0:parts, t0:t0 + n], op=ALU.mult), reads=rtoks + ["rope"], writes=[tn + "xs"])
            else:
                P.op("dve", lambda e: e.scalar_tensor_tensor(out=xc, in0=src, scalar=pre, in1=cos[0:parts, t0:t0 + n], op0=ALU.mult, op1=ALU.mult), reads=rtoks + ["rope"], writes=[tn + "xc"])
                P.op("dve", lambda e: e.scalar_tensor_tensor(out=xs, in0=src, scalar=pre, in1=sin[0:parts, t0:t0 + n], op0=ALU.mult, op1=ALU.mult), reads=rtoks + ["rope"], writes=[tn + "xs"])
            b3, b3t = bank()
            def f(e):
                e.matmul(b3[0:parts, 0:n], lhsT=ident_b[0:parts, 0:parts], rhs=xc, start=True, stop=False)
                return e.matmul(b3[0:parts, 0:n], lhsT=rot[0:parts, 0:parts], rhs=xs, start=False, stop=True)
            P.op("pe", f, reads=[tn + "xc", tn + "xs", "matsb"], writes=[b3t])
            P.op("act", lambda e: e.activation(out=dst, in_=b3[0:parts, 0:n], func=ACTF.Copy), reads=[b3t], writes=[wtok])

        wp, wt = panel(("in", l, "G"))

        def qk_item(c0cols, gain_ap, extra, do_rope, dst, t0, wtok, b, tq):
            tn = tq["n"]
            bk, bt = proj_fm(wp, wt, c0cols, 128, b)
            P.op("act", lambda e: e.activation(out=tq["sq"], in_=bk[:, :], func=ACTF.Square), reads=[bt], writes=[tn + "sq"])
            yield
            b2, b2t = bank()
            P.op("pe", lambda e: e.matmul(b2[:, :], lhsT=blk_b, rhs=tq["sq"], start=True, stop=True), reads=[tn + "sq", "matsb"], writes=[b2t])
            rstd_from_ss(b2[:, :], 64.0, tq["f1"], [b2t], tn + "f1")
            P.op("dve", lambda e: e.tensor_tensor(out=tq["f2"], in0=bk[:, :], in1=tq["f1"], op=ALU.mult), reads=[bt, tn + "f1"], writes=[tn + "f2"])
            if not do_rope:
                P.op("dve", lambda e: e.tensor_scalar(out=dst, in0=tq["f2"], scalar1=gain_ap, scalar2=extra, op0=ALU.mult, op1=ALU.mult), reads=[tn + "f2", "smallp"], writes=[wtok])
                return
            P.op("dve", lambda e: e.tensor_scalar(out=tq["f2"], in0=tq["f2"], scalar1=gain_ap, scalar2=extra, op0=ALU.mult, op1=ALU.mult), reads=[tn + "f2", "smallp"], writes=[tn + "f2"])
            P.op("dve", lambda e: e.tensor_tensor(out=tq["xc"], in0=tq["f2"], in1=cosG[:, t0:t0 + 512], op=ALU.mult), reads=[tn + "f2", "rope"], writes=[tn + "xc"])
            P.op("dve", lambda e: e.tensor_tensor(out=tq["xs"], in0=tq["f2"], in1=sinG[:, t0:t0 + 512], op=ALU.mult), reads=[tn + "f2", "rope"], writes=[tn + "xs"])
            yield
            b3, b3t = bank()
            def f(e):
                e.matmul(b3[:, :], lhsT=ident_b, rhs=tq["xc"], start=True, stop=False)
                return e.matmul(b3[:, :], lhsT=rotG, rhs=tq["xs"], start=False, stop=True)
            P.op("pe", f, reads=[tn + "xc", tn + "xs", "matsb"], writes=[b3t])
            P.op("act", lambda e: e.activation(out=dst, in_=b3[:, :], func=ACTF.Copy), reads=[b3t], writes=[wtok])

        items = []
        for b in range(NB):
            smp = b < 2
            for c in range(2):
                items.append((c * 128, smallp[:, l, 0:1], 0.125, smp, QT[:, c, blk(b)], b * 512, "QT", b))
            for kv in range(2):
                items.append((256 + kv * 128, smallp[:, l, 1:2], 1.0, smp, KT[:, kv, 512 + b * 512:1024 + b * 512], b * 512, "KT", b))
        active = []
        for ii, it in enumerate(items + [None, None]):
            if it is not None:
                active.append(qk_item(*it, tqs[ii % 3]))
            for g in list(reversed(active)):
                try:
                    next(g)
                except StopIteration:
                    active.remove(g)
        assert not active
        wp, wt = panel(("in", l, "GT"))
        for tt in range(12):
            pr = tt >= 8
            bk, bt = proj_tm(wp, wt, 0, 384 if pr else 256, tt * 128)
            P.op("act", lambda e, bk=bk, tt=tt: e.activation(out=Vg[:, 4 + tt, :], in_=bk[:, 0:256], func=ACTF.Copy), reads=[bt], writes=["Vg"])
            if pr:
                sq_, tl = tt - 8, otile[tt % 2]
                tn = f"otile{tt % 2}"
                seq, half = sq_ // 2, sq_ % 2
                P.op("dve", lambda e, bk=bk, tl=tl: e.tensor_copy(out=tl[:, 0:128].rearrange("p (h d) -> p h d", h=2), in_=bk[:, 0:256].rearrange("p (h r d) -> p h r d", h=2, r=2)[:, :, 0, :]), reads=[bt], writes=[tn])
                P.dma(lambda e, tl=tl, seq=seq, half=half: e.dma_start(out=ngv_o[seq, l, half * 128:(half + 1) * 128, :], in_=tl[:, 0:128]), reads=[tn])
                P.op("act", lambda e, bk=bk: e.activation(out=tq["f1"][:, 0:128], in_=bk[:, 256:384], func=ACTF.Square), reads=[bt], writes=["tq0_f1"])
                P.op("dve", lambda e: e.tensor_reduce(out=tq["f2"][:, 0:2], in_=tq["f1"][:, 0:128].rearrange("p (h d) -> p h d", h=2), axis=AX.X, op=ALU.add), reads=["tq0_f1"], writes=["tq0_f2"])
                rstd_from_ss(tq["f2"][:, 0:2], 64.0, tq["f2"][:, 0:2], ["tq0_f2"], "tq0_f2")
                P.op("dve", lambda e, bk=bk, tl=tl: e.tensor_tensor(out=tl[:, 128:256].rearrange("p (h d) -> p h d", h=2), in0=bk[:, 256:384].rearrange("p (h d) -> p h d", h=2), in1=tq["f2"][:, 0:2].unsqueeze(2).to_broadcast([128, 2, 64]), op=ALU.mult), reads=[bt, "tq0_f2"], writes=[tn])
                P.op("dve", lambda e, tl=tl: e.tensor_tensor(out=tl[:, 128:256].rearrange("p (h d) -> p h d", h=2), in0=tl[:, 128:256].rearrange("p (h d) -> p h d", h=2), in1=krow, op=ALU.mult), reads=[tn, "krow"], writes=[tn])
                P.dma(lambda e, tl=tl, seq=seq, half=half: e.dma_start(out=ngk_o[seq, l, half * 128:(half + 1) * 128, :], in_=tl[:, 128:256]), reads=[tn])
        for h in range(4):
            c, half = h // 2, h % 2
            kv = h // 2
            rows = slice(half * 64, half * 64 + 64)
            for qb in range(2):
                chunks = [(KT[rows, kv, i * 128:(i + 1) * 128], Vg[:, i, kv * 128:(kv + 1) * 128], None) for i in range(12)]
                attend(QT[rows, c, blk(qb)], 512, chunks, mixT[rows, c, blk(qb)], half, ["QT", "KT", "Vg"], f"mixT{qb}", atmp)
                if l == 0:
                    todo = [(0, j) for j in range(4, 12)] + [(l2, j) for l2 in range(1, NL) for j in range(12)]
                    slot_i = h * 2 + qb
                    for ti, (l2, j) in enumerate(todo):
                        if ti * 8 // len(todo) == slot_i:
                            mods_step(l2, j)
                            if (l2, j) == (0, 11):
                                mods_gs(0, 1)
                    if slot_i == 7:
                        for l2 in range(1, NL):
                            mods_finish(l2)
            for s in range(2):
                t0 = 1024 + 256 * s
                chunks = [(KT[rows, kv, 512 + t0 + i * 128:512 + t0 + (i + 1) * 128], Vg[:, 12 + 2 * s + i, kv * 128:(kv + 1) * 128], None) for i in range(2)]
                attend(QT[rows, c, t0:t0 + 256], 256, chunks, mixT[rows, c, t0:t0 + 256], half, ["QT", "KT", "Vg"], "mixT2", atmp)
        if l == 0:
            dbg("oa", mixT[:, 0:2, :], [128, 2, T], MIXT)
        if stage < 3:
            break
        P.barrier()
        A.reset()

        P.mark(f"L{l} na")
        QT = A.bf16(2 * T).rearrange("p (c t) -> p c t", c=2)
        KT = A.bf16(2 * 2048).rearrange("p (c t) -> p c t", c=2)
        Vn = A.bf16(16 * 512).rearrange("p (t c) -> p t c", t=16)
        ctm = A.bf16(4 * 256).rearrange("p (t c) -> p t c", t=4)
        natab = A.bf16(4 * 2 * 1152).rearrange("p (h t n) -> p h t n", h=4, t=2)
        atmp = {"name": f"na{l}", "pt": [A.bf16(512), A.bf16(512), A.bf16(512)], "rd": A.f32(512), "acc": A.f32(512), "acc_tok": "att_acc", "hl": (A.bf16(512), A.bf16(512))}
        otile = [A.f32(256), A.f32(256)]
        if "ntl" not in skip:
            P.dma(lambda e: e.dma_start(out=natab, in_=natab_d[l]), writes=["natab"], eng="pool")
        if "nch" not in skip:
            P.dma(lambda e: e.dma_start(out=ctm, in_=cnk_d[l].rearrange("(t p) c -> p t c", p=128)), writes=["ctm"], eng="pool")
            P.dma(lambda e: e.dma_start(out=Vn[:, 0:4, :], in_=cnv_d[l].rearrange("(t p) c -> p t c", p=128)), writes=["Vn"], eng="pool")
        for c in (range(2) if "nch" not in skip else []):
            bk, bt = bank()
            bkb = bk[:, :].bitcast(BF16)
            def trc(e, c=c, bkb=bkb):
                ins = None
                for t in range(4):
                    ins = e.transpose(bkb[:, t * 128:(t + 1) * 128], ctm[:, t, c * 128:(c + 1) * 128], ident_b)
                return ins
            P.op("pe", trc, reads=["ctm", "matsb"], writes=[bt])
            P.op("act", lambda e, c=c, bkb=bkb: e.activation(out=KT[:, c, 0:512], in_=bkb[:, 0:512], func=ACTF.Copy), reads=[bt], writes=["KT"])
        wp, wt = panel(("in", l, "N"))
        for b in (range(NB) if "npj" not in skip else []):
            for c in range(2):
                bk, bt = proj_fm(wp, wt, c * 128, 128, b)
                P.op("act", lambda e, bk=bk, c=c, b=b: e.activation(out=QT[:, c, blk(b)], in_=bk[:, :], func=ACTF.Copy, scale=0.125), reads=[bt], writes=["QT"])
                bk, bt = proj_fm(wp, wt, 256 + c * 128, 128, b)
                P.op("dve", lambda e, bk=bk, c=c, b=b: e.tensor_copy(out=KT[:, c, 512 + b * 512:1024 + b * 512], in_=bk[:, :]), reads=[bt], writes=["KT"])
        wp, wt = panel(("in", l, "NT"))
        for tt in (range(12) if "nvp" not in skip else []):
            bk, bt = proj_tm(wp, wt, 0, 512, tt * 128)
            if "nvc" not in skip:
                P.op("act", lambda e, bk=bk, tt=tt: e.activation(out=Vn[:, 4 + tt, :], in_=bk[:, :], func=ACTF.Copy), reads=[bt], writes=["Vn"])
            if tt >= 8 and "nvo" not in skip:
                sq_, tl = tt - 8, otile[tt % 2]
                tn = f"otile{tt % 2}"
                seq, half = sq_ // 2, sq_ % 2
                if "nvo1" not in skip:
                    P.op("act", lambda e, bk=bk, tl=tl: e.activation(out=tl[:, 0:256].rearrange("p (h d) -> p h d", h=4), in_=bk[:, 0:512].rearrange("p (h r d) -> p h r d", h=4, r=2)[:, :, 0, :], func=ACTF.Copy), reads=[bt], writes=[tn])
                if "nvo2" not in skip:
                    P.dma(lambda e, tl=tl, seq=seq, half=half: e.dma_start(out=nnv_o[seq, l, half * 128:(half + 1) * 128, :], in_=tl[:, 0:256]), reads=[tn])
        wp, wt = panel(("in", l, "MT"))
        MT_RANGE = range(8, 12) if "mt" not in skip else range(0)
        mrow = A.f32(128)
        P.dma(lambda e: e.dma_start(out=mrow, in_=mkg_row_d[:, l]), writes=["mrow"])
        tf1 = A.f32(128); tf2 = A.f32(2)
        for tt in MT_RANGE:
            sq_ = tt - 8
            seq, half = sq_ // 2, sq_ % 2
            bk, bt = proj_tm(wp, wt, 0, 416, tt * 128)
            tl = otile[tt % 2]
            tn = f"otile{tt % 2}"
            P.op("dve", lambda e, bk=bk, tl=tl: e.tensor_copy(out=tl[:, 0:256], in_=bk[:, 160:416]), reads=[bt], writes=[tn])
            P.dma(lambda e, tl=tl, seq=seq, half=half: e.dma_start(out=nnk_o[seq, l, half * 128:(half + 1) * 128, :], in_=tl[:, 0:256]), reads=[tn])
            tl2 = A.f32(160) if tt == 8 else tl2
            P.op("act", lambda e, bk=bk: e.activation(out=tf1, in_=bk[:, 0:128], func=ACTF.Square), reads=[bt], writes=["mt_tf1"])
            P.op("dve", lambda e: e.tensor_reduce(out=tf2[:, 0:1], in_=tf1, axis=AX.X, op=ALU.add), reads=["mt_tf1"], writes=["mt_tf2"])
            rstd_from_ss(tf2[:, 0:1], 128.0, tf2[:, 1:2], ["mt_tf2"], "mt_tf2b")
            P.op("dve", lambda e, bk=bk, tl2=tl2: e.scalar_tensor_tensor(out=tl2[:, 0:128], in0=bk[:, 0:128], scalar=tf2[:, 1:2], in1=mrow, op0=ALU.mult, op1=ALU.mult), reads=[bt, "mt_tf2b", "mrow"], writes=["mt_tl2"])
            P.op("act", lambda e, bk=bk, tl2=tl2: e.activation(out=tl2[:, 128:160], in_=bk[:, 128:160], func=ACTF.Copy), reads=[bt], writes=["mt_tl2"])
            P.dma(lambda e, tl2=tl2, seq=seq, half=half: e.dma_start(out=nckv_o[seq, l, half * 128:(half + 1) * 128, :], in_=tl2[:, 0:128]), reads=["mt_tl2"])
            P.dma(lambda e, tl2=tl2, seq=seq, half=half: e.dma_start(out=nkr_o[seq, l, half * 128:(half + 1) * 128, :], in_=tl2[:, 128:160]), reads=["mt_tl2"])
        NA_GROUPS = [(0, 4, 0, 4, 0), (4, 8, 0, 6, 1), (8, 13, 2, 8, 1), (13, 16, 4, 8, 0)]
        for h in range(4):
            c, half = h // 2, h % 2
            rows = slice(half * 64, half * 64 + 64)
            for (r0, r1, kc0, kc1, tab) in (NA_GROUPS if 'nas' not in skip else []):
                nq = (r1 - r0) * 64
                q0 = r0 * 64
                chunks = []
                for kc in range(kc0, kc1):
                    e_top = r0 - 2 * kc + 7
                    brhs = natab[:, h, tab, (e_top + 1) * 64:(e_top + 1) * 64 + nq] if "nab" not in skip else None
                    chunks.append((KT[rows, c, 512 + kc * 128:512 + (kc + 1) * 128], Vn[:, 4 + kc, h * 128:(h + 1) * 128], brhs))
                for i in range(4):
                    chunks.append((KT[rows, c, i * 128:(i + 1) * 128], Vn[:, i, h * 128:(h + 1) * 128], None))
                attend(QT[rows, c, q0:q0 + nq], nq, chunks, mixT[rows, 2 + c, q0:q0 + nq], half, ["QT", "KT", "Vn", "natab"], "mixT0", atmp)
            for s in (range(2) if "nap" not in skip else []):
                t0 = 1024 + 256 * s
                chunks = [(KT[rows, c, 512 + t0 + i * 128:512 + t0 + (i + 1) * 128], Vn[:, 12 + 2 * s + i, h * 128:(h + 1) * 128], None) for i in range(2)]
                attend(QT[rows, c, t0:t0 + 256], 256, chunks, mixT[rows, 2 + c, t0:t0 + 256], half, ["QT", "KT", "Vn"], "mixT2", atmp)
        if l == 0:
            dbg("ob", mixT[:, 2:4, :], [128, 2, T], MIXT)
        if stage < 4:
            break
        P.barrier()
        A.reset()

        P.mark(f"L{l} mla")
        QTm = A.bf16(4 * T, parts=96).rearrange("p (h t) -> p h t", h=4)
        KTm = A.bf16(4 * 2048, parts=96).rearrange("p (h t) -> p h t", h=4)
        Vm_raw = A.bf16(16 * 512)
        Vm = Vm_raw.rearrange("p (t c) -> p t c", t=16)
        cqn = Vm_raw[:, 0:2 * T].rearrange("p (c t) -> p c t", c=2)
        ckvn = A.bf16(2048)
        krT = A.bf16(2048, parts=96)
        wq = A.bf16(2 * 384).rearrange("p (k n) -> p k n", k=2)
        wkv = A.bf16(768)
        ctm = A.bf16(4 * 128).rearrange("p (t c) -> p t c", t=4)
        ktm = A.bf16(4 * 96).rearrange("p (t c) -> p t c", t=4)
        tqs = [{"sq": A.bf16(512), "f1": A.f32(512), "xc": A.bf16(512), "xs": A.bf16(512), "sq2": A.bf16(512), "n": f"tq{i}_"} for i in range(2)]
        tqi = [0]
        atmp = {"name": "tq0_", "pt": [A.bf16(512), A.bf16(512), A.bf16(512)], "rd": A.f32(512), "acc": tqs[0]["f1"], "acc_tok": "tq0_f1", "hl": (tqs[0]["xc"], tqs[0]["xs"])}
        P.dma(lambda e: e.dma_start(out=wq, in_=wq_d[l]), writes=["wq"], eng="pool")
        P.dma(lambda e: e.dma_start(out=wkv, in_=wkv_d[l]), writes=["wkv"], eng="pool")
        P.dma(lambda e: e.dma_start(out=ctm, in_=cckv_d[l].rearrange("(t p) c -> p t c", p=128)), writes=["ctm"], eng="pool")
        P.op("pool", lambda e: e.memset(ktm, 0.0), writes=["ktm"])
        P.dma(lambda e: e.dma_start(out=ktm[:, :, 64:96], in_=ckr_d[l].rearrange("(t p) c -> p t c", p=128)), reads=["ktm"], writes=["ktm"], eng="pool")
        bk, bt = bank()
        bkb = bk[:, :].bitcast(BF16)
        def trc(e, bkb=bkb):
            ins = None
            for t in range(4):
                ins = e.transpose(bkb[:, t * 128:(t + 1) * 128], ctm[:, t, :], ident_b)
            return ins
        P.op("pe", trc, reads=["ctm", "matsb"], writes=[bt])
        P.op("act", lambda e, bkb=bkb: e.activation(out=ckvn[:, 0:512], in_=bkb[:, 0:512], func=ACTF.Copy), reads=[bt], writes=["ckvn"])
        bk, bt = bank()
        bkb = bk[:, :].bitcast(BF16)
        def trk(e, bkb=bkb):
            ins = None
            for t in range(4):
                ins = e.transpose(bkb[0:96, t * 128:(t + 1) * 128], ktm[:, t, :], ident_b)
            return ins
        P.op("pe", trk, reads=["ktm", "matsb"], writes=[bt])
        P.op("act", lambda e, bkb=bkb: e.activation(out=krT[:, 0:512], in_=bkb[0:96, 0:512], func=ACTF.Copy), reads=[bt], writes=["krT"])

        wp, wt = panel(("in", l, "M"))
        for b in range(NB):
            smp = b < 2
            tqi[0] += 1
            tq = tqs[tqi[0] % 2]; sq2 = tq["sq2"]; tn = tq["n"]
            ba, bat = proj_fm(wp, wt, 0, 128, b)
            bb, bbt = proj_fm(wp, wt, 128, 128, b)
            P.op("act", lambda e, ba=ba: e.activation(out=tq["sq"], in_=ba[:, :], func=ACTF.Square), reads=[bat], writes=[tn + "sq"])
            P.op("act", lambda e, bb=bb: e.activation(out=sq2, in_=bb[:, :], func=ACTF.Square), reads=[bbt], writes=[tn + "sq2"])
            b2, b2t = bank()
            def ssf(e, b2=b2):
                e.matmul(b2[:, :], lhsT=ones_b, rhs=tq["sq"], start=True, stop=False)
                return e.matmul(b2[:, :], lhsT=ones_b, rhs=sq2, start=False, stop=True)
            P.op("pe", ssf, reads=[tn + "sq", tn + "sq2", "matsb"], writes=[b2t])
            rstd_from_ss(b2[:, :], 256.0, tq["f1"], [b2t], tn + "f1")
            for c, (bq, bqt) in enumerate(((ba, bat), (bb, bbt))):
                P.op("dve", lambda e, bq=bq, c=c, b=b: e.scalar_tensor_tensor(out=cqn[:, c, blk(b)], in0=bq[:, :], scalar=smallp[:, l, 2 + c:3 + c], in1=tq["f1"], op0=ALU.mult, op1=ALU.mult), reads=[bqt, tn + "f1", "smallp"], writes=["cqn"])
            tqi[0] += 1
            tq = tqs[tqi[0] % 2]; tn = tq["n"]
            bc, bct = proj_fm(wp, wt, 256, 128, b)
            P.op("act", lambda e, bc=bc: e.activation(out=tq["sq"], in_=bc[:, :], func=ACTF.Square), reads=[bct], writes=[tn + "sq"])
            b2, b2t = bank()
            P.op("pe", lambda e, b2=b2: e.matmul(b2[:, :], lhsT=ones_b, rhs=tq["sq"], start=True, stop=True), reads=[tn + "sq", "matsb"], writes=[b2t])
            rstd_from_ss(b2[:, :], 128.0, tq["f1"], [b2t], tn + "f1")
            P.op("dve", lambda e, bc=bc, b=b: e.scalar_tensor_tensor(out=ckvn[:, 512 + b * 512:1024 + b * 512], in0=bc[:, :], scalar=smallp[:, l, 4:5], in1=tq["f1"], op0=ALU.mult, op1=ALU.mult), reads=[bct, tn + "f1", "smallp"], writes=["ckvn"])
            bkr, bkrt = proj_fm(wp, wt, 384, 96, b)
            dstk = krT[:, 512 + b * 512:1024 + b * 512]
            if smp:
                rope_apply(bkr[0:96, :], dstk, b * 512, "krT", cosM, sinM, rotM, [bkrt], 512, parts=96)
            else:
                P.op("act", lambda e, bkr=bkr, dstk=dstk: e.activation(out=dstk, in_=bkr[0:96, :], func=ACTF.Copy), reads=[bkrt], writes=["krT"])
        for b in range(NB):
            for h in range(4):
                bq, bqt = bank()
                def qf(e, bq=bq, h=h, b=b):
                    e.matmul(bq[0:96, :], lhsT=wq[:, 0, h * 96:(h + 1) * 96], rhs=cqn[:, 0, blk(b)], start=True, stop=False)
                    return e.matmul(bq[0:96, :], lhsT=wq[:, 1, h * 96:(h + 1) * 96], rhs=cqn[:, 1, blk(b)], start=False, stop=True)
                P.op("pe", qf, reads=["wq", "cqn"], writes=[bqt])
                if b < 2:
                    rope_apply(bq[0:96, :], QTm[:, h, blk(b)], b * 512, "QTm", cosM, sinM, rotM, [bqt], 512, parts=96, pre=MLA_SCALE)
                else:
                    P.op("act", lambda e, bq=bq, h=h, b=b: e.activation(out=QTm[:, h, blk(b)], in_=bq[0:96, :], func=ACTF.Copy, scale=MLA_SCALE), reads=[bqt], writes=["QTm"])
        P.barrier()
        for kb in range(4):
            for h in range(4):
                bq, bqt = bank()
                P.op("pe", lambda e, bq=bq, h=h, kb=kb: e.matmul(bq[0:64, :], lhsT=wkv[:, h * 64:(h + 1) * 64], rhs=ckvn[:, kb * 512:(kb + 1) * 512], start=True, stop=True), reads=["wkv", "ckvn"], writes=[bqt])
                if h % 2 == 0:
                    P.op("act", lambda e, bq=bq, h=h, kb=kb: e.activation(out=KTm[0:64, h, kb * 512:(kb + 1) * 512], in_=bq[0:64, :], func=ACTF.Copy), reads=[bqt], writes=["KTm"])
                else:
                    P.op("dve", lambda e, bq=bq, h=h, kb=kb: e.tensor_copy(out=KTm[0:64, h, kb * 512:(kb + 1) * 512], in_=bq[0:64, :]), reads=[bqt], writes=["KTm"])
        for h in range(4):
            P.op("pool", lambda e, h=h: e.tensor_copy(out=KTm[64:96, h, :], in_=krT[64:96, :]), reads=["krT"], writes=["KTm"])
        for kt in range(16):
            bq, bqt = bank()
            P.op("pe", lambda e, bq=bq, kt=kt: e.matmul(bq[:, :], lhsT=ckvn[:, kt * 128:(kt + 1) * 128], rhs=wkv[:, 256:768], start=True, stop=True), reads=["wkv", "ckvn"], writes=[bqt])
            P.op("act", lambda e, bq=bq, kt=kt: e.activation(out=Vm[:, kt, :], in_=bq[:, :], func=ACTF.Copy), reads=[bqt], writes=["Vm"])
        for h in range(4):
            c, half = h // 2, h % 2
            rows = slice(half * 64, half * 64 + 64)
            for qb in range(2):
                chunks = [(KTm[:, h, i * 128:(i + 1) * 128], Vm[:, i, h * 128:(h + 1) * 128], None) for i in range(12)]
                attend(QTm[:, h, blk(qb)], 512, chunks, mixT[rows, 6 + c, blk(qb)], half, ["QTm", "KTm", "Vm"], f"mixT{qb}", atmp)
            for s in range(2):
                t0 = 1024 + 256 * s
                chunks = [(KTm[:, h, 512 + t0 + i * 128:512 + t0 + (i + 1) * 128], Vm[:, 12 + 2 * s + i, h * 128:(h + 1) * 128], None) for i in range(2)]
                attend(QTm[:, h, t0:t0 + 256], 256, chunks, mixT[rows, 6 + c, t0:t0 + 256], half, ["QTm", "KTm", "Vm"], "mixT2", atmp)
        if l == 0:
            dbg("od", mixT[:, 6:8, :], [128, 2, T], MIXT)
        if stage < 5:
            break
        P.barrier()

        P.mark(f"L{l} dn")
        def dn_phase(tok0, ntok, seqs, is_sample):
            A.reset()
            NC = ntok // 64
            nseq = len(seqs)
            QTd = A.bf16(4 * ntok, parts=64).rearrange("p (h t) -> p h t", h=4)
            KTd = A.bf16(4 * ntok, parts=64).rearrange("p (h t) -> p h t", h=4)
            VTd = A.bf16(2 * ntok).rearrange("p (c t) -> p c t", c=2)
            sg = A.bf16(NC * 256, parts=64).rearrange("p (c n) -> p c n", c=NC)
            oacc = A.bf16(NC * 256, parts=64).rearrange("p (c n) -> p c n", c=NC)
            ab = A.f32(NC * 16, parts=64).rearrange("p (c n) -> p c n", c=NC)
            g_all = A.f32(NC * 8, parts=64).rearrange("p (c n) -> p c n", c=NC)
            beta_all = A.f32(NC * 8, parts=64).rearrange("p (c n) -> p c n", c=NC)
            smalls = A.f32(64 + 64 + 16 + 16 + 32 + 8, parts=128)
            dng_row = smalls[0:64, 0:64]; cwqk = smalls[0:64, 64:96].rearrange("p (c j) -> p c j", c=8)
            cwv = smalls[:, 96:104].rearrange("p (c j) -> p c j", c=2)
            dndt = smalls[0:64, 104:112]; nea = smalls[0:64, 112:120]
            P.dma(lambda e: e.dma_start(out=dng_row, in_=dng_row_d[:, l]), writes=["dn_small"])
            P.dma(lambda e: e.dma_start(out=cwqk, in_=cwqk_d[:, l]), writes=["dn_small"])
            P.dma(lambda e: e.dma_start(out=cwv, in_=cwv_d[:, l]), writes=["dn_small"])
            P.dma(lambda e: e.dma_start(out=dndt, in_=dndt_d[:, l]), writes=["dn_small"])
            P.dma(lambda e: e.dma_start(out=nea, in_=dnal_d[:, l]), writes=["dn_small"])
            mark = A.off
            slen = ntok // nseq
            W = ntok + 3 * nseq
            zpads = [A.f32(W), A.f32(W)]; accs = [A.f32(W), A.f32(W)]
            sqbs = [A.bf16(512, parts=64), A.bf16(512, parts=64)]; rsts = [A.f32(512, parts=64), A.f32(512, parts=64)]
            for i_ in range(2):
                P.op("pool", lambda e, i_=i_: e.memset(zpads[i_], 0.0), writes=[f"zpad{i_}"])
            cvi = [0]
            nblk = ntok // 512
            b0 = tok0 // 512

            def zcopy(bk, bt, parts):
                for bi in range(1):
                    pass

            def conv_chunk(parts, wcol_fn, c0, m, wp, wt):
                cvi[0] += 1
                zpad = zpads[cvi[0] % 2]; acc = accs[cvi[0] % 2]
                zt = f"zpad{cvi[0] % 2}"; at_ = f"acc{cvi[0] % 2}"
                for bi in range(nblk):
                    bk, bt = proj_fm(wp, wt, c0, m, b0 + bi)
                    if is_sample:
                        P.op("act", lambda e, bk=bk, bi=bi: e.activation(out=zpad[0:parts, 1 + bi * 512:1 + (bi + 1) * 512], in_=bk[0:parts, :], func=ACTF.Copy), reads=[bt], writes=[zt])
                    else:
                        for s in range(2):
                            P.op("act", lambda e, bk=bk, s=s: e.activation(out=zpad[0:parts, 1 + 259 * s:257 + 259 * s], in_=bk[0:parts, 256 * s:256 * (s + 1)], func=ACTF.Copy), reads=[bt], writes=[zt])
                n = W - 3
                P.op("dve", lambda e: e.tensor_scalar(out=acc[0:parts, 0:n], in0=zpad[0:parts, 0:n], scalar1=wcol_fn(0), scalar2=None, op0=ALU.mult), reads=[zt, "dn_small"], writes=[at_])
                for j in range(1, 4):
                    P.op("dve", lambda e, j=j: e.scalar_tensor_tensor(out=acc[0:parts, 0:n], in0=zpad[0:parts, j:n + j], scalar=wcol_fn(j), in1=acc[0:parts, 0:n], op0=ALU.mult, op1=ALU.add), reads=[zt, at_, "dn_small"], writes=[at_])
                return acc, at_

            def segs():
                if is_sample:
                    return [(0, 0, 512), (512, 512, 512)]
                return [(0, 0, 256), (259, 256, 256)]

            wp, wt = panel(("in", l, "D1"))
            for hc in range(8):
                acc, at_ = conv_chunk(64, lambda j, hc=hc: cwqk[:, hc, j:j + 1], hc * 64, 64, wp, wt)
                P.op("act", lambda e, acc=acc: e.activation(out=acc[0:64, 0:W - 3], in_=acc[0:64, 0:W - 3], func=ACTF.Silu), reads=[at_], writes=[at_])
                for si_, (ao, to, n) in enumerate(segs()):
                    sqb = sqbs[si_ % 2]; rst = rsts[si_ % 2]; sqt = f"dn_sqb{si_ % 2}"; rtt = f"dn_rst{si_ % 2}"
                    P.op("act", lambda e, ao=ao, n=n, acc=acc, sqb=sqb: e.activation(out=sqb[:, 0:n], in_=acc[0:64, ao:ao + n], func=ACTF.Square), reads=[at_], writes=[sqt])
                    b2, b2t = bank()
                    P.op("pe", lambda e, b2=b2, n=n, sqb=sqb: e.matmul(b2[0:64, 0:n], lhsT=ones_b[0:64, 0:64], rhs=sqb[:, 0:n], start=True, stop=True), reads=[sqt, "matsb"], writes=[b2t])
                    rstd_from_ss(b2[0:64, 0:n], 1.0, rst[:, 0:n], [b2t], rtt)
                    dst = (QTd if hc < 4 else KTd)[:, hc % 4, to:to + n]
                    sc = 0.125 if hc < 4 else 1.0
                    P.op("dve", lambda e, ao=ao, n=n, dst=dst, sc=sc, acc=acc, rst=rst: e.scalar_tensor_tensor(out=dst, in0=acc[0:64, ao:ao + n], scalar=sc, in1=rst[:, 0:n], op0=ALU.mult, op1=ALU.mult), reads=[at_, rtt], writes=["dn_qk"])
            wp, wt = panel(("in", l, "D2"))
            for vc in range(2):
                acc, at_ = conv_chunk(128, lambda j, vc=vc: cwv[:, vc, j:j + 1], vc * 128, 128, wp, wt)
                for (ao, to, n) in segs():
                    P.op("act", lambda e, ao=ao, to=to, n=n, vc=vc, acc=acc: e.activation(out=VTd[:, vc, to:to + n], in_=acc[:, ao:ao + n], func=ACTF.Silu), reads=[at_], writes=["dn_v"])
            for c in range(NC):
                bk, bt = proj_tm(wp, wt, 256, 256, tok0 + c * 64, m=64)
                P.op("act", lambda e, bk=bk, c=c: e.activation(out=sg[:, c, :], in_=bk[0:64, 0:256], func=ACTF.Silu), reads=[bt], writes=["dn_sg"])
            wp, wt = panel(("in", l, "DAB"))
            bkg, bkgt = bank("a")
            for c in range(NC):
                proj_tm(wp, wt, 0, 16, tok0 + c * 64, m=64, bk_bt=(bkg, bkgt), col0=c * 16)
            P.op("act", lambda e: e.activation(out=ab, in_=bkg[0:64, 0:NC * 16].rearrange("p (c n) -> p c n", c=NC), func=ACTF.Copy), reads=[bkgt], writes=["dn_ab"])
            tg = A.f32(NC * 8, parts=64).rearrange("p (c n) -> p c n", c=NC)
            P.op("dve", lambda e: e.tensor_tensor(out=tg, in0=ab[:, :, 0:8], in1=dndt.unsqueeze(1).to_broadcast([64, NC, 8]), op=ALU.add), reads=["dn_ab", "dn_small"], writes=["dn_tg"])
            P.op("act", lambda e: e.activation(out=tg, in_=tg, func=ACTF.Exp), reads=["dn_tg"], writes=["dn_tg"])
            P.op("act", lambda e: e.activation(out=tg, in_=tg, func=ACTF.Ln, bias=1.0), reads=["dn_tg"], writes=["dn_tg"])
            P.op("act", lambda e: e.activation(out=nea, in_=nea, func=ACTF.Exp), reads=["dn_small"], writes=["dn_nea"])
            P.op("dve", lambda e: e.tensor_scalar(out=nea, in0=nea, scalar1=-1.0, scalar2=None, op0=ALU.mult), reads=["dn_nea"], writes=["dn_nea"])
            P.op("dve", lambda e: e.tensor_tensor(out=g_all, in0=tg, in1=nea.unsqueeze(1).to_broadcast([64, NC, 8]), op=ALU.mult), reads=["dn_tg", "dn_nea"], writes=["dn_g"])
            P.op("act", lambda e: e.activation(out=beta_all, in_=ab[:, :, 8:16], func=ACTF.Exp, scale=-1.0), reads=["dn_ab"], writes=["dn_beta"])
            P.op("dve", lambda e: e.tensor_scalar_add(out=beta_all, in0=beta_all, scalar1=1.0), reads=["dn_beta"], writes=["dn_beta"])
            P.op("dve", lambda e: e.reciprocal(out=beta_all, in_=beta_all), reads=["dn_beta"], writes=["dn_beta"])
            if l == 0 and is_sample:
                dbg("dn_q", QTd, [64, 4, ntok], ["dn_qk"])
                dbg("dn_k", KTd, [64, 4, ntok], ["dn_qk"])
                dbg("dn_v", VTd, [128, 2, ntok], ["dn_v"])
                dbg("dn_g", g_all, [64, NC, 8], ["dn_g"])
                dbg("dn_beta", beta_all, [64, NC, 8], ["dn_beta"])
            if "dnscan" in skip:
                return
            dnstop = ([int(x[6:]) for x in skip if x.startswith('dnstop')] + [0])[0]
            P.mark(f"L{l} dnscan{tok0}")
            P.barrier()
            A.reset(mark)
            def t512(dt):
                return (A.f32(512, parts=64) if dt == F32 else A.bf16(512, parts=64))
            Dm, Am, Bm, U, AN, T32, P32, W_, Pm, WT = (t512(F32) for _ in range(10))
            Vb, KbEg, kdec = Dm, Am, Bm
            dg = U
            db, de, KbT, QdT, qkT, ANb, ATb, ANb2, ATb2, Pb = (t512(BF16) for _ in range(10))
            vnew = A.f32(256, parts=64); vnew_b = A.bf16(256, parts=64); S = A.f32(256, parts=64); Sb = A.bf16(256, parts=64)
            of = A.f32(256, parts=64); otmp = vnew; oo = A.bf16(256, parts=64)
            sm8 = A.f32(8 * 8, parts=64)
            g8, beta8, gc, eg, tmg, ekd, gl, beg = (sm8[:, i * 8:(i + 1) * 8] for i in range(8))
            ss4 = A.f32(8, parts=64)
            I64f = ident_f[0:64, 0:64]; I64b = ident_b[0:64, 0:64]; O64f = ones_f[0:64, 0:64]; O64b = ones_b[0:64, 0:64]

            def v3(t):
                return t.rearrange("p (u i) -> p u i", u=8)

            def vh(t):
                return t.rearrange("p (h n) -> p h n", h=4)

            def bc8(s):
                return s.unsqueeze(2).to_broadcast([64, 8, 64])

            def mask_b(i):
                return masks[0:64, i, :].unsqueeze(1).to_broadcast([64, 8, 64])

            for d in range(2):
                MI_tri = 3 if d == 0 else 2
                M_sN, M_tT, M_sT = (0, 3, 1) if d == 0 else (1, 2, 0)
                for si, (sc0, snc) in enumerate(seqs):
                    if is_sample:
                        P.dma(lambda e, d=d: e.dma_start(out=S.rearrange("p (h v) -> p h v", h=4), in_=sdn_d[l, d].rearrange("h k v -> k h v")), writes=["dn_S"])
                    else:
                        P.op("pool", lambda e: e.memset(S, 0.0), writes=["dn_S"])
                    P.op("act", lambda e: e.activation(out=Sb, in_=S, func=ACTF.Copy), reads=["dn_S"], writes=["dn_Sb"])
                    pairs = list(range(sc0, sc0 + snc, 2))
                    if d == 1:
                        pairs = pairs[::-1]
                    for c0 in pairs:
                        t0 = c0 * 64
                        KT2 = KTd[:, :, t0:t0 + 128]; QT2 = QTd[:, :, t0:t0 + 128]
                        P.op("act", lambda e, c0=c0, d=d: e.activation(out=g8.rearrange("p (h j) -> p h j", h=4), in_=g_all[:, c0:c0 + 2, d * 4:d * 4 + 4].rearrange("p j h -> p h j"), func=ACTF.Copy), reads=["dn_g"], writes=["dn_g8"])
                        P.op("act", lambda e, c0=c0, d=d: e.activation(out=beta8.rearrange("p (h j) -> p h j", h=4), in_=beta_all[:, c0:c0 + 2, d * 4:d * 4 + 4].rearrange("p j h -> p h j"), func=ACTF.Copy), reads=["dn_beta"], writes=["dn_b8"])
                        if dnstop == 1:
                            return
                        bk1, bk1t = bank()
                        def cs_f(e, bk1=bk1, MI_tri=MI_tri):
                            e.matmul(bk1[0:64, 0:8], lhsT=masks[0:64, MI_tri, :], rhs=g8, start=True, stop=True)
                            return e.matmul(bk1[0:64, 8:16], lhsT=O64f, rhs=g8, start=True, stop=True)
                        P.op("pe", cs_f, reads=["dn_g8", "masks", "matsf"], writes=[bk1t])
                        P.op("act", lambda e, bk1=bk1: e.activation(out=gc, in_=bk1[0:64, 0:8], func=ACTF.Copy), reads=[bk1t], writes=["dn_gc"])
                        P.op("act", lambda e, bk1=bk1: e.activation(out=eg, in_=bk1[0:64, 0:8], func=ACTF.Exp), reads=[bk1t], writes=["dn_eg"])
                        P.op("act", lambda e, bk1=bk1: e.activation(out=gl, in_=bk1[0:64, 8:16], func=ACTF.Exp), reads=[bk1t], writes=["dn_gl"])
                        P.op("dve", lambda e, bk1=bk1: e.tensor_tensor(out=tmg, in0=bk1[0:64, 8:16], in1=gc, op=ALU.subtract), reads=[bk1t, "dn_gc"], writes=["dn_tmg"])
                        P.op("act", lambda e: e.activation(out=ekd, in_=tmg, func=ACTF.Exp), reads=["dn_tmg"], writes=["dn_ekd"])
                        P.op("dve", lambda e: e.tensor_tensor(out=beg, in0=beta8, in1=eg, op=ALU.mult), reads=["dn_b8", "dn_eg"], writes=["dn_beg"])
                        if dnstop == 2:
                            return
                        P.op("dve", lambda e: e.tensor_tensor(out=v3(dg), in0=I64f.unsqueeze(1).to_broadcast([64, 8, 64]), in1=bc8(gc), op=ALU.mult), reads=["matsf", "dn_gc"], writes=["dn_U"])
                        P.op("pool", lambda e: e.tensor_tensor(out=v3(db), in0=I64f.unsqueeze(1).to_broadcast([64, 8, 64]), in1=bc8(beta8), op=ALU.mult), reads=["matsf", "dn_b8"], writes=["dn_db"])
                        P.op("pool", lambda e: e.tensor_tensor(out=v3(de), in0=I64f.unsqueeze(1).to_broadcast([64, 8, 64]), in1=bc8(eg), op=ALU.mult), reads=["matsf", "dn_eg"], writes=["dn_de"])
                        Rg, Rgt = bank(); Rb, Rbt = bank(); Re, Ret = bank()
                        P.op("pe", lambda e, Rg=Rg: e.matmul(Rg[0:64, :], lhsT=O64f, rhs=dg, start=True, stop=True), reads=["dn_U", "matsf"], writes=[Rgt])
                        P.op("pe", lambda e, Rb=Rb: e.matmul(Rb[0:64, :], lhsT=O64b, rhs=db, start=True, stop=True), reads=["dn_db", "matsb"], writes=[Rbt])
                        P.op("pe", lambda e, Re=Re: e.matmul(Re[0:64, :], lhsT=O64b, rhs=de, start=True, stop=True), reads=["dn_de", "matsb"], writes=[Ret])
                        if dnstop == 3:
                            return
                        P.op("dve", lambda e, Rg=Rg: e.tensor_tensor(out=v3(Dm), in0=bc8(gc), in1=v3(Rg[0:64, :]), op=ALU.subtract), reads=[Rgt, "dn_gc"], writes=["dn_Dm"])
                        P.op("dve", lambda e: e.tensor_scalar_min(out=Am, in0=Dm, scalar1=0.0), reads=["dn_Dm"], writes=["dn_Am"])
                        P.op("dve", lambda e: e.tensor_scalar(out=Dm, in0=Dm, scalar1=-1.0, scalar2=0.0, op0=ALU.mult, op1=ALU.min), reads=["dn_Dm"], writes=["dn_Dm"])
                        P.op("act", lambda e: e.activation(out=Am, in_=Am, func=ACTF.Exp), reads=["dn_Am"], writes=["dn_Am"])
                        P.op("act", lambda e: e.activation(out=Dm, in_=Dm, func=ACTF.Exp), reads=["dn_Dm"], writes=["dn_Dm"])
                        P.op("pool", lambda e, M_sN=M_sN: e.tensor_tensor(out=v3(Am), in0=v3(Am), in1=mask_b(M_sN), op=ALU.mult), reads=["dn_Am", "masks"], writes=["dn_Am"])
                        P.op("pool", lambda e, M_tT=M_tT: e.tensor_tensor(out=v3(Bm), in0=v3(Dm), in1=mask_b(M_tT), op=ALU.mult), reads=["dn_Dm", "masks"], writes=["dn_Bm"])
                        P.op("pool", lambda e, M_sT=M_sT: e.tensor_tensor(out=v3(Dm), in0=v3(Dm), in1=mask_b(M_sT), op=ALU.mult), reads=["dn_Dm", "masks", "dn_Bm"], writes=["dn_Dm"])
                        if dnstop == 4:
                            return
                        P.op("dve", lambda e, Rb=Rb, KT2=KT2: e.tensor_tensor(out=vh(KbT), in0=KT2, in1=vh(Rb[0:64, :]), op=ALU.mult), reads=[Rbt, "dn_qk"], writes=["dn_KbT"])
                        P.op("dve", lambda e, Re=Re, QT2=QT2: e.tensor_tensor(out=vh(QdT), in0=QT2, in1=vh(Re[0:64, :]), op=ALU.mult), reads=[Ret, "dn_qk"], writes=["dn_QdT"])
                        if dnstop == 5:
                            return
                        pAN, pANt = bank(); pAT, pATt = bank(); pQK, pQKt = bank()
                        def prods(e, pAN=pAN, pAT=pAT, pQK=pQK, KT2=KT2, QT2=QT2):
                            ins = None
                            for h in range(4):
                                for j in range(2):
                                    u = h * 2 + j
                                    cs = slice(u * 64, u * 64 + 64)
                                    ks = KT2[:, h, j * 64:j * 64 + 64]
                                    e.matmul(pAN[0:64, cs], lhsT=KbT[:, cs], rhs=ks, start=True, stop=True)
                                    e.matmul(pAT[0:64, cs], lhsT=ks, rhs=KbT[:, cs], start=True, stop=True)
                                    ins = e.matmul(pQK[0:64, cs], lhsT=ks, rhs=QT2[:, h, j * 64:j * 64 + 64], start=True, stop=True)
                            return ins
                        P.op("pe", prods, reads=["dn_KbT", "dn_qk"], writes=[pANt, pATt, pQKt])
                        P.op("dve", lambda e, pAN=pAN: e.tensor_tensor(out=AN, in0=pAN[0:64, :], in1=Am, op=ALU.mult), reads=[pANt, "dn_Am"], writes=["dn_AN"])
                        P.op("dve", lambda e, pAT=pAT: e.tensor_tensor(out=ATb, in0=pAT[0:64, :], in1=Dm, op=ALU.mult), reads=[pATt, "dn_Dm"], writes=["dn_ATb"])
                        P.op("act", lambda e: e.activation(out=ANb, in_=AN, func=ACTF.Copy), reads=["dn_AN"], writes=["dn_ANb"])
                        P.op("dve", lambda e, pQK=pQK: e.tensor_tensor(out=qkT, in0=pQK[0:64, :], in1=Bm, op=ALU.mult), reads=[pQKt, "dn_Bm"], writes=["dn_qkT"])
                        P.op("pool", lambda e: e.tensor_tensor(out=v3(Pb), in0=I64b.unsqueeze(1).to_broadcast([64, 8, 64]), in1=v3(ATb), op=ALU.subtract), reads=["dn_ATb", "matsb"], writes=["dn_Pb"])
                        if dnstop == 6:
                            return
                        an, at, ant, att = ANb, ATb, "dn_ANb", "dn_ATb"
                        an_n, at_n, ant_n, att_n = ANb2, ATb2, "dn_ANb2", "dn_ATb2"
                        for lev in range(4):
                            pa, pat = bank(); pb, pbt = bank()
                            def sqf(e, pa=pa, pb=pb, an=an, at=at, lev=lev):
                                ins = None
                                for u in range(8):
                                    cs = slice(u * 64, u * 64 + 64)
                                    ins = e.matmul(pa[0:64, cs], lhsT=at[:, cs], rhs=an[:, cs], start=True, stop=True)
                                    if lev < 3:
                                        ins = e.matmul(pb[0:64, cs], lhsT=an[:, cs], rhs=at[:, cs], start=True, stop=True)
                                return ins
                            P.op("pe", sqf, reads=[ant, att], writes=[pat, pbt])
                            P.op("act", lambda e, pa=pa, an_n=an_n: e.activation(out=an_n, in_=pa[0:64, :], func=ACTF.Copy), reads=[pat], writes=[ant_n])
                            if lev < 3:
                                P.op("dve", lambda e, pb=pb, at_n=at_n: e.tensor_copy(out=at_n, in_=pb[0:64, :]), reads=[pbt], writes=[att_n])
                            pp, ppt = bank()
                            def apf(e, pp=pp, an_n=an_n):
                                ins = None
                                for u in range(8):
                                    cs = slice(u * 64, u * 64 + 64)
                                    ins = e.matmul(pp[0:64, cs], lhsT=an_n[:, cs], rhs=Pb[:, cs], start=True, stop=True)
                                return ins
                            P.op("pe", apf, reads=[ant_n, "dn_Pb"], writes=[ppt])
                            P.op("dve", lambda e, pp=pp: e.tensor_tensor(out=Pb, in0=pp[0:64, :], in1=Pb, op=ALU.add), reads=[ppt, "dn_Pb"], writes=["dn_Pb"])
                            an, at, ant, att, an_n, at_n, ant_n, att_n = an_n, at_n, ant_n, att_n, an, at, ant, att
                        ptp, ptpt = bank()
                        def trp(e, ptp=ptp):
                            ins = None
                            for u in range(8):
                                cs = slice(u * 64, u * 64 + 64)
                                ins = e.matmul(ptp[0:64, cs], lhsT=Pb[:, cs], rhs=I64b, start=True, stop=True)
                            return ins
                        P.op("pe", trp, reads=["dn_Pb", "matsb"], writes=[ptpt])
                        P.op("act", lambda e, ptp=ptp: e.activation(out=T32, in_=ptp[0:64, :], func=ACTF.Copy), reads=[ptpt], writes=["dn_T32"])
                        P.op("dve", lambda e: e.tensor_copy(out=P32, in_=Pb), reads=["dn_Pb"], writes=["dn_P32"])
                        pw1, pw1t = bank()
                        def w1f(e, pw1=pw1):
                            ins = None
                            for u in range(8):
                                cs = slice(u * 64, u * 64 + 64)
                                ins = e.matmul(pw1[0:64, cs], lhsT=AN[:, cs], rhs=P32[:, cs], start=True, stop=True)
                            return ins
                        P.op("pe", w1f, reads=["dn_AN", "dn_P32"], writes=[pw1t])
                        P.op("dve", lambda e, pw1=pw1: e.tensor_tensor(out=W_, in0=pw1[0:64, :], in1=P32, op=ALU.add), reads=[pw1t, "dn_P32"], writes=["dn_W"])
                        pw2, pw2t = bank()
                        def w2f(e, pw2=pw2):
                            ins = None
                            for u in range(8):
                                cs = slice(u * 64, u * 64 + 64)
                                ins = e.matmul(pw2[0:64, cs], lhsT=T32[:, cs], rhs=W_[:, cs], start=True, stop=True)
                            return ins
                        P.op("pe", w2f, reads=["dn_T32", "dn_W"], writes=[pw2t])
                        P.op("dve", lambda e, pw2=pw2: e.scalar_tensor_tensor(out=Pm, in0=P32, scalar=2.0, in1=pw2[0:64, :], op0=ALU.mult, op1=ALU.subtract), reads=[pw2t, "dn_P32"], writes=["dn_P"])
                        if dnstop == 7:
                            return
                        pk, pkt = bank(); pv_, pvt = bank()
                        def trf(e, pk=pk, pv_=pv_, t0=t0):
                            ins = None
                            for h in range(4):
                                for j in range(2):
                                    u = h * 2 + j
                                    tk = slice(t0 + j * 64, t0 + j * 64 + 64)
                                    e.matmul(pk[0:64, u * 64:u * 64 + 64], lhsT=KTd[:, h, tk], rhs=I64b, start=True, stop=True)
                                    ins = None
                            for h in range(4):
                                for j in range(2):
                                    u = h * 2 + j
                                    tk = slice(t0 + j * 64, t0 + j * 64 + 64)
                                    hl = h % 2
                                    ins = e.matmul(pv_[0:64, u * 64:u * 64 + 64], lhsT=VTd[:, h // 2, tk], rhs=ident_b[:, hl * 64:hl * 64 + 64], start=True, stop=True)
                            return ins
                        P.op("pe", trf, reads=["dn_qk", "dn_v", "matsb"], writes=[pkt, pvt])
                        P.op("dve", lambda e, pv_=pv_: e.tensor_tensor(out=v3(Vb), in0=bc8(beta8), in1=v3(pv_[0:64, :]), op=ALU.mult), reads=[pvt, "dn_b8"], writes=["dn_Dm"])
                        P.op("dve", lambda e, pk=pk: e.tensor_tensor(out=v3(KbEg), in0=bc8(beg), in1=v3(pk[0:64, :]), op=ALU.mult), reads=[pkt, "dn_beg"], writes=["dn_Am"])
                        P.op("dve", lambda e, pk=pk: e.tensor_tensor(out=v3(kdec), in0=bc8(ekd), in1=v3(pk[0:64, :]), op=ALU.mult), reads=[pkt, "dn_ekd"], writes=["dn_Bm"])
                        if dnstop == 8:
                            return
                        pu, put = bank(); pw, pwt = bank()
                        def uwf(e, pu=pu, pw=pw):
                            ins = None
                            for u in range(8):
                                cs = slice(u * 64, u * 64 + 64)
                                e.matmul(pu[0:64, cs], lhsT=Pm[:, cs], rhs=Vb[:, cs], start=True, stop=True)
                                ins = e.matmul(pw[0:64, cs], lhsT=KbEg[:, cs], rhs=Pm[:, cs], start=True, stop=True)
                            return ins
                        P.op("pe", uwf, reads=["dn_P", "dn_Dm", "dn_Am"], writes=[put, pwt])
                        P.op("act", lambda e, pu=pu: e.activation(out=U, in_=pu[0:64, :], func=ACTF.Copy), reads=[put], writes=["dn_U"])
                        P.op("dve", lambda e, pw=pw: e.tensor_copy(out=WT, in_=pw[0:64, :]), reads=[pwt], writes=["dn_WT"])
                        if dnstop == 9:
                            return
                        for j in ((0, 1) if d == 0 else (1, 0)):
                            c = c0 + j
                            def cs_(h, j=j):
                                return slice((h * 2 + j) * 64, (h * 2 + j) * 64 + 64)
                            pws, pwst = bank()
                            def wsf(e, pws=pws, cs_=cs_):
                                ins = None
                                for h in range(4):
                                    ins = e.matmul(pws[0:64, h * 64:h * 64 + 64], lhsT=WT[:, cs_(h)], rhs=S[:, h * 64:h * 64 + 64], start=True, stop=True)
                                return ins
                            P.op("pe", wsf, reads=["dn_WT", "dn_S"], writes=[pwst])
                            Uj = U.rearrange("p (h j i) -> p h j i", h=4, j=2)[:, :, j, :]
                            P.op("dve", lambda e, pws=pws, Uj=Uj: e.tensor_tensor(out=vnew.rearrange("p (h i) -> p h i", h=4), in0=Uj, in1=pws[0:64, 0:256].rearrange("p (h i) -> p h i", h=4), op=ALU.subtract), reads=[pwst, "dn_U"], writes=["dn_vnew"])
                            P.op("act", lambda e: e.activation(out=vnew_b, in_=vnew, func=ACTF.Copy), reads=["dn_vnew"], writes=["dn_vnewb"])
                            po, pot = bank(); psn, psnt = bank()
                            def osf(e, po=po, psn=psn, cs_=cs_, j=j, QT2=QT2):
                                ins = None
                                for h in range(4):
                                    hs = slice(h * 64, h * 64 + 64)
                                    e.matmul(po[0:64, hs], lhsT=QdT[:, cs_(h)], rhs=Sb[:, hs], start=True, stop=False)
                                    e.matmul(po[0:64, hs], lhsT=qkT[:, cs_(h)], rhs=vnew_b[:, hs], start=False, stop=True)
                                    ins = e.matmul(psn[0:64, hs], lhsT=kdec[:, cs_(h)], rhs=vnew[:, hs], start=True, stop=True)
                                return ins
                            P.op("pe", osf, reads=["dn_QdT", "dn_Sb", "dn_qkT", "dn_vnew", "dn_vnewb", "dn_Bm"], writes=[pot, psnt])
                            glj = gl.rearrange("p (h j) -> p h j", h=4)[:, :, j:j + 1].to_broadcast([64, 4, 64])
                            P.op("dve", lambda e, glj=glj: e.tensor_tensor(out=S.rearrange("p (h v) -> p h v", h=4), in0=S.rearrange("p (h v) -> p h v", h=4), in1=glj, op=ALU.mult), reads=["dn_S", "dn_gl"], writes=["dn_S"])
                            P.op("dve", lambda e, psn=psn: e.tensor_tensor(out=S, in0=psn[0:64, 0:256], in1=S, op=ALU.add), reads=[psnt, "dn_S"], writes=["dn_S"])
                            P.op("act", lambda e: e.activation(out=Sb, in_=S, func=ACTF.Copy), reads=["dn_S"], writes=["dn_Sb"])
                            if d == 0:
                                P.op("act", lambda e, po=po, c=c: e.activation(out=oacc[:, c, :], in_=po[0:64, 0:256], func=ACTF.Copy), reads=[pot], writes=["dn_oacc"])
                            else:
                                P.op("dve", lambda e, po=po, c=c: e.tensor_tensor(out=of, in0=po[0:64, 0:256], in1=oacc[:, c, :], op=ALU.add), reads=[pot, "dn_oacc"], writes=["dn_of"])
                                P.op("act", lambda e: e.activation(out=otmp, in_=of, func=ACTF.Square), reads=["dn_of"], writes=["dn_vnew"])
                                P.op("dve", lambda e: e.tensor_reduce(out=ss4[:, 0:4], in_=otmp.rearrange("p (h v) -> p h v", h=4), axis=AX.X, op=ALU.add), reads=["dn_vnew"], writes=["dn_ss4"])
                                rstd_from_ss(ss4[:, 0:4], 64.0, ss4[:, 4:8], ["dn_ss4"], "dn_ss4b")
                                P.op("dve", lambda e: e.tensor_tensor(out=of.rearrange("p (h v) -> p h v", h=4), in0=of.rearrange("p (h v) -> p h v", h=4), in1=ss4[:, 4:8].unsqueeze(2).to_broadcast([64, 4, 64]), op=ALU.mult), reads=["dn_of", "dn_ss4b"], writes=["dn_of"])
                                P.op("pool", lambda e: e.tensor_tensor(out=of.rearrange("p (h v) -> p h v", h=4), in0=of.rearrange("p (h v) -> p h v", h=4), in1=dng_row.unsqueeze(1).to_broadcast([64, 4, 64]), op=ALU.mult), reads=["dn_of", "dn_small"], writes=["dn_of"])
                                P.op("dve", lambda e, c=c: e.tensor_tensor(out=oo, in0=of, in1=sg[:, c, :], op=ALU.mult), reads=["dn_of", "dn_sg"], writes=["dn_oo"])
                                ptr, ptrt = bank()
                                def otr(e, ptr=ptr):
                                    e.matmul(ptr[:, 0:64], lhsT=oo[:, 0:128], rhs=I64b, start=True, stop=True)
                                    return e.matmul(ptr[:, 64:128], lhsT=oo[:, 128:256], rhs=I64b, start=True, stop=True)
                                P.op("pe", otr, reads=["dn_oo", "matsb"], writes=[ptrt])
                                tks = slice(tok0 + c * 64, tok0 + c * 64 + 64)
                                P.op("act", lambda e, ptr=ptr, tks=tks: e.activation(out=mixT[:, 4:6, tks], in_=ptr[:, 0:128].rearrange("p (c t) -> p c t", c=2), func=ACTF.Copy), reads=[ptrt], writes=[f"mixT{tok0 // 512 + (c * 64) // 512}"])
                    if not is_sample:
                        P.dma(lambda e, si=si, d=d: e.dma_start(out=ndn_o[si, l, d].rearrange("h k v -> k h v"), in_=S.rearrange("p (h v) -> p h v", h=4)), reads=["dn_S"])

        dn_phase(0, 1024, [(0, 16)], True)
        P.barrier()
        dn_phase(1024, 512, [(0, 4), (4, 4)], False)
        if l == 0:
            dbg("oc", mixT[:, 4:6, :], [128, 2, T], MIXT)
        if stage < 6:
            break
        P.barrier()
        A.reset()

        P.mark(f"L{l} wout")
        for j in range(2):
            wp, wt = panel(("out", l, j))
            for o in range(4):
                oc = j * 4 + o
                for b in range(NB):
                    g = grp_of_blk(b)
                    bk, bt = bank()
                    def wof(e, bk=bk, wp=wp, o=o, b=b):
                        ins = None
                        for k in range(8):
                            ins = e.matmul(bk[:, :], lhsT=wp[:, k, o * 128:(o + 1) * 128], rhs=mixT[:, k, blk(b)], start=(k == 0), stop=(k == 7))
                        return ins
                    P.op("pe", wof, reads=[wt, f"mixT{b}"], writes=[bt])
                    P.op("dve", lambda e, bk=bk, oc=oc, b=b, g=g: e.scalar_tensor_tensor(out=xT[:, oc, blk(b)], in0=bk[:, :], scalar=modsT[:, l, 16 + oc, g:g + 1], in1=xT[:, oc, blk(b)], op0=ALU.mult, op1=ALU.add), reads=[bt, "modsT0", "modsT1", f"xT{b}"], writes=[f"xT{b}"])
        if l == 0:
            dbg("x1", xT[:], [128, 8, T], ["xT0", "xT1", "xT2"])
        if stage < 7:
            break
        P.barrier()
        A.reset()
        P.mark(f"L{l} ffn")
        norm_fm(lambda c, g: gsb[:, l, 1, c, g:g + 1], lambda c, g: modsT[:, l, 24 + c, g:g + 1],
                lambda c, b: hT[:, c, blk(b)], lambda b: f"hT{b}", f"n2_{l}")
        P.barrier()
        A.reset()
        actT = A.bf16(12 * T).rearrange("p (f t) -> p f t", f=12)
        sgt = [A.f32(512), A.f32(512)]
        for hf, (f0, f1) in enumerate(FF_SPLIT):
            nf = f1 - f0
            for jp in range(f0 // 2, f1 // 2):
                wp, wt = panel(("gu", l, jp))
                for fi in range(2):
                    f = 2 * jp + fi - f0
                    for b in range(NB):
                        bg, bgt = proj_fm(wp, wt, fi * 128, 128, b)
                        bu, but = proj_fm(wp, wt, 256 + fi * 128, 128, b)
                        st = sgt[(f + b) % 2]
                        stn = f"sgt{(f + b) % 2}"
                        P.op("act", lambda e, bg=bg, st=st: e.activation(out=st, in_=bg[:, :], func=ACTF.Silu), reads=[bgt], writes=[stn])
                        P.op("dve", lambda e, bu=bu, st=st, f=f, b=b: e.tensor_tensor(out=actT[:, f, blk(b)], in0=bu[:, :], in1=st, op=ALU.mult), reads=[but, stn], writes=[f"actT{b}"])
            for q in range(4):
                wp, wt = panel(("dn", l, hf, q))
                for o in range(2):
                    oc = q * 2 + o
                    for b in range(NB):
                        g = grp_of_blk(b)
                        bk, bt = bank()
                        def dnf(e, bk=bk, wp=wp, o=o, b=b, nf=nf):
                            ins = None
                            for k in range(nf):
                                ins = e.matmul(bk[:, :], lhsT=wp[:, k, o * 128:(o + 1) * 128], rhs=actT[:, k, blk(b)], start=(k == 0), stop=(k == nf - 1))
                            return ins
                        P.op("pe", dnf, reads=[wt, f"actT{b}"], writes=[bt])
                        P.op("dve", lambda e, bk=bk, oc=oc, b=b, g=g: e.scalar_tensor_tensor(out=xT[:, oc, blk(b)], in0=bk[:, :], scalar=modsT[:, l, 40 + oc, g:g + 1], in1=xT[:, oc, blk(b)], op0=ALU.mult, op1=ALU.add), reads=[bt, "modsT0", "modsT1", f"xT{b}"], writes=[f"xT{b}"])
        if l == 0:
            dbg("x2", xT[:], [128, 8, T], ["xT0", "xT1", "xT2"])
        if stage < 8:
            break

    P.mark("final")
    if stage >= 9:
        P.barrier()
        A.reset()
        yT = A.f32(8 * T).rearrange("p (c t) -> p c t", c=8)
        norm_fm(lambda c, g: gfin[:, c:c + 1], None, lambda c, b: yT[:, c, blk(b)], lambda b: f"yT{b}", "nf")
        ytm = [A.f32(1024), A.f32(1024)]
        for t in range(12):
            yt = ytm[t % 2]
            ytn = f"ytm{t % 2}"
            for half in range(2):
                bk, bt = bank()
                def trf2(e, bk=bk, t=t, half=half):
                    ins = None
                    for c in range(4):
                        ins = e.transpose(bk[:, c * 128:(c + 1) * 128], yT[:, half * 4 + c, t * 128:(t + 1) * 128], ident_f)
                    return ins
                P.op("pe", trf2, reads=[f"yT{t // 4}", "matsf"], writes=[bt])
                if half == 0:
                    P.op("act", lambda e, bk=bk, yt=yt: e.activation(out=yt[:, 0:512], in_=bk[:, :], func=ACTF.Copy), reads=[bt], writes=[ytn])
                else:
                    P.op("dve", lambda e, bk=bk, yt=yt: e.tensor_copy(out=yt[:, 512:1024], in_=bk[:, :]), reads=[bt], writes=[ytn])
            P.dma(lambda e, yt=yt, t=t: e.dma_start(out=y_o[t * 128:(t + 1) * 128, :], in_=yt), reads=[ytn])

    P.barrier()
    P.final_wait()
    P.mark("end")
    P.emit()
    build.marks = P.marks
    return nc, dbg_outs


_CACHE = {}


def kernel(**inputs):
    inputs = {k: np.asarray(v) for k, v in inputs.items()}
    sh, per = prep_inputs(inputs)
    if "nc" not in _CACHE:
        _CACHE["nc"] = build()[0]
    nc = _CACHE["nc"]
    in_maps = [{**sh, **per[c]} for c in range(NCORES)]
    res = run_bass_kernel_spmd(nc, in_maps, core_ids=list(range(NCORES)))
    R = res.results
    f = np.float32
    y_p = np.zeros((16, 256, 1024), f); y_s = np.zeros((8, 1024, 1024), f)
    ngk = np.zeros((16, NL, 256, 2, 64), f); ngv = np.zeros((16, NL, 256, 2, 64), f)
    nnk = np.zeros((16, NL, 256, 4, 64), f); nnv = np.zeros((16, NL, 256, 4, 64), f)
    ndn = np.zeros((16, NL, 2, 4, 64, 64), f); nckv = np.zeros((16, NL, 256, 128), f); nkr = np.zeros((16, NL, 256, 32), f)
    for c in range(NCORES):
        r = R[c]
        y = r["y_o"]
        y_s[c] = y[0:1024]
        y_p[2 * c] = y[1024:1280]; y_p[2 * c + 1] = y[1280:1536]
        for s in range(2):
            b = 2 * c + s
            ngk[b] = r["ngk_o"][s].reshape(NL, 256, 2, 64); ngv[b] = r["ngv_o"][s].reshape(NL, 256, 2, 64)
            nnk[b] = r["nnk_o"][s].reshape(NL, 256, 4, 64); nnv[b] = r["nnv_o"][s].reshape(NL, 256, 4, 64)
            ndn[b] = r["ndn_o"][s]; nckv[b] = r["nckv_o"][s]; nkr[b] = r["nkr_o"][s]
    return (y_p, y_s, ngk, ngv, nnk, nnv, ndn, nckv, nkr)
```

```python
import numpy as np
from contextlib import ExitStack
import concourse.bass as bass
import concourse.mybir as mybir
from concourse.bass_utils import run_bass_kernel_spmd

F32 = mybir.dt.float32
BF16 = mybir.dt.bfloat16
ALU = mybir.AluOpType
ACTF = mybir.ActivationFunctionType
AX = mybir.AxisListType

NCORES = 8
NL = 2
D = 1024
T = 1536
NB = 3
EPS = 1e-6
NEG = -30000.0
MLA_SCALE = 96 ** -0.5
ENGS = ("pe", "act", "dve", "pool", "sp")
EPOCH = 3000


class Tok:
    __slots__ = ("name", "w", "r")

    def __init__(self, name):
        self.name = name
        self.w = None
        self.r = {}


class _Rec:
    def __init__(self):
        self.calls = []

    def __getattr__(self, name):
        def f(*a, **k):
            self.calls.append((name, a, k))
            return self
        return f


def _record(fn):
    r = _Rec()
    fn(r)
    calls = r.calls
    assert calls

    def replay(e):
        ins = None
        for name, a, k in calls:
            ins = getattr(e, name)(*a, **k)
        return ins
    n = 0
    for name, a, k in calls:
        if name == "matmul" and k.get("lhsT") is not None and k["lhsT"].dtype == F32:
            n += 2
        else:
            n += 1
    replay.n = n
    return replay


class Prog:
    def __init__(self, nc, n_dma_sems=28):
        self.nc = nc
        self.es = ExitStack()
        self.ops = {e: [] for e in ENGS}
        self.cnt = {e: 0 for e in ENGS}
        self.sems = {e: [] for e in ENGS}
        self.known = {e: {} for e in ENGS}
        self.dma_sems = [self.es.enter_context(nc.semaphore(f"dq{i}")) for i in range(n_dma_sems)]
        self.dma_val = [0] * n_dma_sems
        self.dma_i = 0
        self.dma_cnt = {}
        self.toks = {}
        self.nbuf = 0
        self.bank_i = {"s": 0, "a": 0}
        self.n_pe = 0
        self.marks = []

    def sb(self, shape, dtype, name=None):
        self.nbuf += 1
        return self.es.enter_context(self.nc.sbuf_tensor(name or f"b{self.nbuf}", list(shape), dtype))

    def ps(self, shape, dtype, name=None):
        self.nbuf += 1
        return self.es.enter_context(self.nc.psum_tensor(name or f"p{self.nbuf}", list(shape), dtype))

    def tok(self, name):
        t = self.toks.get(name)
        if t is None:
            t = self.toks[name] = Tok(name)
        return t

    def _sem_for(self, eng, k):
        ep = (k - 1) // EPOCH
        while len(self.sems[eng]) <= ep:
            self.sems[eng].append(self.es.enter_context(
                self.nc.semaphore(f"s_{eng}_{len(self.sems[eng])}")))
        return self.sems[eng][ep], (k - 1) % EPOCH + 1

    def _need(self, waiter, dep, waits):
        if dep is None:
            return
        if dep[0] == "e":
            _, eng, k = dep
            if eng == waiter and eng == "pe":
                return
            kn = self.known[waiter].get(("e", eng), 0)
            if kn >= k:
                return
            self.known[waiter][("e", eng)] = k
            waits.append(dep)
        else:
            _, si, val = dep
            kn = self.known[waiter].get(("d", si), 0)
            if kn >= val:
                return
            self.known[waiter][("d", si)] = val
            waits.append(dep)

    def _collect(self, eng, reads, writes):
        waits = []
        for t in reads:
            self._need(eng, t.w, waits)
        for t in writes:
            self._need(eng, t.w, waits)
            for d in t.r.values():
                self._need(eng, d, waits)
        best = {}
        for w in waits:
            key = w[:2]
            if key not in best or best[key][2] < w[2]:
                best[key] = w
        return list(best.values())

    def _commit(self, dep, reads, writes):
        for t in reads:
            old = t.r.get(dep[:2])
            if old is None or old[2] < dep[2]:
                t.r[dep[:2]] = dep
        for t in writes:
            t.w = dep
            t.r = {}

    def _toks(self, names):
        return [self.tok(t) if isinstance(t, str) else t for t in names]

    def op(self, eng, fn, reads=(), writes=()):
        reads, writes = self._toks(reads), self._toks(writes)
        waits = self._collect(eng, reads, writes)
        self.cnt[eng] += 1
        k = self.cnt[eng]
        self._sem_for(eng, k)
        rp = _record(fn)
        if eng == "pe":
            self.n_pe += rp.n
        self.ops[eng].append((waits, rp, ("e", eng, k)))
        self._commit(("e", eng, k), reads, writes)

    def dma(self, fn, reads=(), writes=(), eng="sp"):
        reads, writes = self._toks(reads), self._toks(writes)
        waits = self._collect(eng, reads, writes)
        half = len(self.dma_sems) // 2
        cnt = self.dma_cnt.setdefault(eng, 0)
        self.dma_cnt[eng] = cnt + 1
        si = (cnt % half) + (0 if eng == "sp" else half)
        prev = self.dma_val[si]
        if prev:
            self._need(eng, ("d", si, prev), waits)
        self.dma_val[si] = prev + 16
        dep = ("d", si, prev + 16)
        self.ops[eng].append((waits, _record(fn), dep))
        self._commit(dep, reads, writes)

    def mark(self, name):
        self.marks.append((name, self.n_pe))

    def barrier(self):
        deps = [("e", e, self.cnt[e]) for e in ENGS if self.cnt[e]]
        deps += [("d", i, v) for i, v in enumerate(self.dma_val) if v]
        for e in ENGS:
            waits = []
            for d in deps:
                self._need(e, d, waits)
            if waits:
                self.ops[e].append((waits, None, None))

    def final_wait(self, eng="sp"):
        waits = []
        for i, v in enumerate(self.dma_val):
            if v:
                self._need(eng, ("d", i, v), waits)
        for e in ENGS:
            if self.cnt[e]:
                self._need(eng, ("e", e, self.cnt[e]), waits)
        self.ops[eng].append((waits, None, None))

    def _emit_engine(self, ename, e):
        for waits, fn, dep in self.ops[ename]:
            for w in waits:
                if w[0] == "e":
                    sem, val = self._sem_for(w[1], w[2])
                    e.wait_ge(sem, val)
                else:
                    e.wait_ge(self.dma_sems[w[1]], w[2])
            if fn is None:
                continue
            ins = fn(e)
            if dep[0] == "e":
                sem, _ = self._sem_for(dep[1], dep[2])
                ins.then_inc(sem, 1)
            else:
                ins.then_inc(self.dma_sems[dep[1]], 16)

    def emit(self):
        with self.nc.Block() as block:
            @block.tensor
            def _(e):
                self._emit_engine("pe", e)

            @block.scalar
            def _(e):
                self._emit_engine("act", e)

            @block.vector
            def _(e):
                self._emit_engine("dve", e)

            @block.gpsimd
            def _(e):
                self._emit_engine("pool", e)

            @block.sync
            def _(e):
                self._emit_engine("sp", e)
        self.es.close()


def _tile_k(w):
    k, n = w.shape
    return np.ascontiguousarray(w.reshape(k // 128, 128, n).transpose(1, 0, 2))


def _r(a, b):
    return list(range(a, b))


O_AQ, O_AK, O_AV, O_BQ, O_BK, O_BV, O_CQKV, O_CG, O_CA, O_CB, O_DCQ, O_DCKV, O_DKR = (
    0, 256, 384, 512, 768, 1024, 1280, 2048, 2304, 2312, 2320, 2576, 2704)

PANELS_IN = {}


def _def_panels():
    p = {}
    p["G"] = _r(0, 256) + _r(256, 320) * 2 + _r(320, 384) * 2
    p["GT"] = _r(384, 448) * 2 + _r(448, 512) * 2 + _r(O_AK, O_AK + 128)
    p["N"] = _r(O_BQ, O_BQ + 256) + _r(O_BK, O_BK + 256)
    nt = []
    for h in range(4):
        nt += _r(O_BV + 64 * h, O_BV + 64 * h + 64) * 2
    p["NT"] = nt
    p["MT"] = _r(O_DCKV, O_DCKV + 128) + _r(O_DKR, O_DKR + 32) + _r(O_BK, O_BK + 256)
    m = _r(O_DCQ, O_DCQ + 256) + _r(O_DCKV, O_DCKV + 128) + _r(O_DKR, O_DKR + 32) * 3
    p["M"] = m
    dq = []
    for h in range(4):
        dq += _r(O_CQKV + 64 * h, O_CQKV + 64 * h + 64)
    dk = []
    for h in range(4):
        dk += _r(O_CQKV + 256 + 64 * h, O_CQKV + 256 + 64 * h + 64)
    p["D1"] = dq + dk
    p["D2"] = _r(O_CQKV + 512, O_CQKV + 768) + _r(O_CG, O_CG + 256)
    p["DAB"] = _r(O_CA, O_CA + 8) + _r(O_CB, O_CB + 8)
    return p


PANELS_IN = _def_panels()
PANEL_ORDER = ["G", "GT", "N", "NT", "MT", "M", "D1", "D2", "DAB"]
PANEL_OFF = {}
_o = 0
for _n in PANEL_ORDER:
    PANEL_OFF[_n] = _o
    _o += len(PANELS_IN[_n])
NCD = _o

FF = 2816
NFF = 22
FF_SPLIT = [(0, 12), (12, 22)]


def _na_tables(bias):
    c = np.arange(64)
    c0 = np.clip(c - 8, 0, 48)
    kq = c[:, None] - c[None, :]
    ci = np.clip(kq + 15, 0, 30)
    allowed = (c[:, None] >= c0[None, :]) & (c[:, None] < c0[None, :] + 16)
    out = np.full((128, 4, 2, 18, 64), NEG, np.float32)
    for h in range(4):
        for e in range(0, 15):
            dr = 7 - e
            tile = np.where(allowed, bias[h, 14 - e][ci], np.float32(NEG)).astype(np.float32)
            for tab in range(2):
                if tab == 1 and not (-4 <= dr <= 3):
                    continue
                out[0:64, h, tab, e + 1, :] = tile
                out[64:128, h, tab, e + 2, :] = tile
    return out.reshape(128, 4, 2, 18 * 64)


def _rope_tables():
    t = np.arange(1024)
    rowp, colp = (t // 64).astype(np.float64), (t % 64).astype(np.float64)

    def tabs(nd, part0, ndim_total):
        cos = np.ones((ndim_total, 1024)); sin = np.zeros((ndim_total, 1024))
        rot = np.zeros((ndim_total, ndim_total))
        half = nd // 2
        q = half // 2
        inv = 10000.0 ** (-np.arange(q) / q)
        for ax, pos in enumerate((rowp, colp)):
            base = part0 + ax * half
            ang = inv[:, None] * pos[None, :]
            for i in range(q):
                a, b = base + i, base + q + i
                cos[a] = np.cos(ang[i]); cos[b] = np.cos(ang[i])
                sin[a] = np.sin(ang[i]); sin[b] = -np.sin(ang[i])
                rot[a, b] = 1.0
                rot[b, a] = 1.0
        return cos, sin, rot
    cg = np.ones((128, 1024)); sg = np.zeros((128, 1024)); rg = np.zeros((128, 128))
    for hh in range(2):
        c_, s_, r_ = tabs(64, 64 * hh, 128)
        m = slice(64 * hh, 64 * hh + 64)
        cg[m] = c_[m]; sg[m] = s_[m]; rg[m, m] = r_[m, m]
    cm, sm, rm = tabs(32, 64, 128)
    f = np.float32
    return cg.astype(f), sg.astype(f), cm.astype(f), sm.astype(f), rg.astype(f), rm.astype(f)


def _consts():
    cg, sg, cm, sm, rg, rm = _rope_tables()
    ident = np.eye(128, dtype=np.float32)
    ones = np.ones((128, 128), np.float32)
    blk = np.zeros((128, 128), np.float32)
    blk[:64, :64] = 1; blk[64:, 64:] = 1
    p = np.arange(64)[:, None]; fr = np.arange(64)[None, :]
    masks = np.stack([(fr < p), (fr > p), (fr <= p), (fr >= p)]).astype(np.float32)
    masks128 = np.zeros((128, 4, 64), np.float32)
    masks128[:64] = masks.transpose(1, 0, 2)
    return {
        "c_rope": np.ascontiguousarray(np.stack([cg, sg, cm, sm], 1)),
        "c_mats": np.ascontiguousarray(np.stack([ident, ones, blk, rg, rm], 1)),
        "c_masks": masks128,
    }


def prep_inputs(inp):
    f = np.float32
    sh = dict(_consts())
    w_in = inp["w_in"]
    cols = []
    for n in PANEL_ORDER:
        cols += PANELS_IN[n]
    cols = np.asarray(cols)
    sh["w_in_d"] = np.stack([_tile_k(w_in[l][:, cols]) for l in range(NL)])
    sh["w_mod_d"] = np.stack([_tile_k(inp["w_mod"][l]) for l in range(NL)])
    sh["b_mod_d"] = np.ascontiguousarray(np.broadcast_to(inp["b_mod"][:, None, :], (NL, 2, 6144))).astype(f)
    sh["w_out_d"] = np.stack([_tile_k(inp["w_out"][l]) for l in range(NL)])
    gu = []
    for l in range(NL):
        g = _tile_k(inp["ffn_w_gate"][l]).reshape(128, 8, 11, 256)
        u = _tile_k(inp["ffn_w_up"][l]).reshape(128, 8, 11, 256)
        gu.append(np.concatenate([g, u], -1).reshape(128, 8, 11 * 512))
    sh["w_gu_d"] = np.stack(gu)
    sh["w_dn_d"] = np.stack([_tile_k(inp["ffn_w_down"][l]) for l in range(NL)])
    sh["wq_d"] = np.stack([_tile_k(inp["mla_wq_up"][l]) for l in range(NL)])
    wkv = inp["mla_wkv_up"]
    kcols = []
    vcols = []
    for h in range(4):
        kcols += _r(128 * h, 128 * h + 64)
        vcols += _r(128 * h + 64, 128 * h + 128) * 2
    sh["wkv_d"] = np.ascontiguousarray(wkv[:, :, np.asarray(kcols + vcols)])
    def fm(v):
        L_, n = v.shape
        return np.ascontiguousarray(v.reshape(L_, n // 128, 128).transpose(2, 0, 1))
    sh["g1_d"] = fm(inp["norm1_g"]); sh["g2_d"] = fm(inp["norm2_g"])
    sh["gf_d"] = fm(inp["final_g"][None])
    sh["gqg_d"] = np.ascontiguousarray(np.stack([np.tile(inp["gqa_qn_g"], (1, 2)), np.tile(inp["gqa_kn_g"], (1, 2))], -1).transpose(1, 0, 2))
    sh["gqk_row_d"] = np.ascontiguousarray(np.broadcast_to(inp["gqa_kn_g"][None, :, None, :], (128, NL, 2, 64))).astype(f)
    sh["mqg_d"] = fm(inp["mla_qn_g"])
    sh["mkg_d"] = fm(inp["mla_kvn_g"])
    sh["mkg_row_d"] = np.ascontiguousarray(np.broadcast_to(inp["mla_kvn_g"][None], (128, NL, 128))).astype(f)
    cw = inp["dn_conv_w"]
    cq = cw[:, :, 0:512].reshape(NL, 4, 8, 64).transpose(3, 0, 2, 1)
    sh["cwqk_d"] = np.ascontiguousarray(cq)
    cv = cw[:, :, 512:768].reshape(NL, 4, 2, 128).transpose(3, 0, 2, 1)
    sh["cwv_d"] = np.ascontiguousarray(cv)
    sh["dng_row_d"] = np.ascontiguousarray(np.broadcast_to(inp["dn_out_g"][None], (64, NL, 64))).astype(f)
    sh["dnal_d"] = np.ascontiguousarray(np.broadcast_to(inp["dn_a_log"].reshape(1, NL, 8), (64, NL, 8))).astype(f)
    sh["dndt_d"] = np.ascontiguousarray(np.broadcast_to(inp["dn_dt_bias"].reshape(1, NL, 8), (64, NL, 8))).astype(f)
    sh["natab_d"] = np.stack([_na_tables(inp["na_bias"][l]) for l in range(NL)])
    per = []
    for c in range(NCORES):
        d = {}
        d["x_d"] = np.ascontiguousarray(np.concatenate([inp["x_sample"][c], inp["x_prompt"][2 * c], inp["x_prompt"][2 * c + 1]], 0))
        cond = np.stack([inp["c"][c], inp["c_ctx"]], -1)
        d["cond_d"] = np.ascontiguousarray(cond.reshape(8, 128, 2).transpose(1, 0, 2))
        gk = inp["cache_gqa_k"][c]
        d["cgk_d"] = np.ascontiguousarray(gk[:, :, [0, 0, 1, 1], :].reshape(NL, 512, 256))
        gv = inp["cache_gqa_v"][c]
        d["cgv_d"] = np.ascontiguousarray(gv[:, :, [0, 0, 1, 1], :].reshape(NL, 512, 256))
        d["cnk_d"] = np.ascontiguousarray(inp["cache_na_k"][c].reshape(NL, 512, 256))
        nv = inp["cache_na_v"][c]
        d["cnv_d"] = np.ascontiguousarray(nv[:, :, [0, 0, 1, 1, 2, 2, 3, 3], :].reshape(NL, 512, 512))
        d["sdn_d"] = np.ascontiguousarray(inp["state_dn"][c])
        d["cckv_d"] = np.ascontiguousarray(inp["cache_mla_ckv"][c])
        d["ckr_d"] = np.ascontiguousarray(inp["cache_mla_krope"][c])
        per.append(d)
    return sh, per


class Arena:
    def __init__(self, ten, n):
        self.ten = ten
        self.n = n
        self.off = 0

    def reset(self, off=0):
        self.off = off

    def f32(self, n, parts=128):
        assert self.off + n <= self.n, (self.off, n, self.n)
        ap = self.ten[0:parts, self.off:self.off + n]
        self.off += n
        return ap

    def bf16(self, n, parts=128):
        m = (n + 1) // 2
        assert self.off + m <= self.n, (self.off, m, self.n)
        ap = self.ten[0:parts, self.off:self.off + m].bitcast(BF16)
        self.off += m
        return ap[:, 0:n]


def build(stage=99, debug=(), skip=()):
    nc = bass.Bass("TRN2", target_bir_lowering=False)
    P = Prog(nc)
    dbg_outs = {}

    def din(name, shape):
        return nc.dram_tensor(name, list(shape), F32, kind="ExternalInput").ap()

    def dout(name, shape):
        return nc.dram_tensor(name, list(shape), F32, kind="ExternalOutput").ap()

    x_d = din("x_d", [T, D]); cond_d = din("cond_d", [128, 8, 2])
    cgk_d = din("cgk_d", [NL, 512, 256]); cgv_d = din("cgv_d", [NL, 512, 256])
    cnk_d = din("cnk_d", [NL, 512, 256]); cnv_d = din("cnv_d", [NL, 512, 512])
    sdn_d = din("sdn_d", [NL, 2, 4, 64, 64]); cckv_d = din("cckv_d", [NL, 512, 128]); ckr_d = din("ckr_d", [NL, 512, 32])
    c_rope = din("c_rope", [128, 4, 1024]); c_mats = din("c_mats", [128, 5, 128]); c_masks = din("c_masks", [128, 4, 64])
    w_in_d = din("w_in_d", [NL, 128, 8, NCD]); w_mod_d = din("w_mod_d", [NL, 128, 8, 6144]); b_mod_d = din("b_mod_d", [NL, 2, 6144])
    w_out_d = din("w_out_d", [NL, 128, 8, 1024]); w_gu_d = din("w_gu_d", [NL, 128, 8, 5632]); w_dn_d = din("w_dn_d", [NL, 128, 22, 1024])
    wq_d = din("wq_d", [NL, 128, 2, 384]); wkv_d = din("wkv_d", [NL, 128, 768])
    g1_d = din("g1_d", [128, NL, 8]); g2_d = din("g2_d", [128, NL, 8]); gf_d = din("gf_d", [128, 1, 8])
    gqg_d = din("gqg_d", [128, NL, 2]); gqk_row_d = din("gqk_row_d", [128, NL, 2, 64])
    mqg_d = din("mqg_d", [128, NL, 2]); mkg_d = din("mkg_d", [128, NL, 1]); mkg_row_d = din("mkg_row_d", [128, NL, 128])
    cwqk_d = din("cwqk_d", [64, NL, 8, 4]); cwv_d = din("cwv_d", [128, NL, 2, 4])
    dng_row_d = din("dng_row_d", [64, NL, 64]); dnal_d = din("dnal_d", [64, NL, 8]); dndt_d = din("dndt_d", [64, NL, 8])
    natab_d = din("natab_d", [NL, 128, 4, 2, 1152])

    y_o = dout("y_o", [T, D])
    ngk_o = dout("ngk_o", [2, NL, 256, 128]); ngv_o = dout("ngv_o", [2, NL, 256, 128])
    nnk_o = dout("nnk_o", [2, NL, 256, 256]); nnv_o = dout("nnv_o", [2, NL, 256, 256])
    ndn_o = dout("ndn_o", [2, NL, 2, 4, 64, 64]); nckv_o = dout("nckv_o", [2, NL, 256, 128]); nkr_o = dout("nkr_o", [2, NL, 256, 32])

    xT = P.sb([128, 8, T], F32, "xT")
    hT = P.sb([128, 8, T], BF16, "hT")
    mixT = P.sb([128, 8, T], BF16, "mixT")
    NSLOT = 3
    slots = [P.sb([128, 4096], BF16, f"wslot{i}") for i in range(NSLOT)]
    rope = P.sb([128, 4, 1024], BF16, "rope")
    matsb = P.sb([128, 5, 128], BF16, "matsb")
    matsf = P.sb([128, 2, 128], F32, "matsf")
    masks = P.sb([128, 4, 64], F32, "masks")
    modsT = P.sb([128, NL, 48, 2], F32, "modsT")
    gsb = P.sb([128, NL, 2, 8, 2], F32, "gsb")
    g12 = P.sb([128, 2, NL, 8], F32, "g12")
    gfin = P.sb([128, 8], F32, "gfin")
    smallp = P.sb([128, NL, 8], F32, "smallp")
    scT = P.sb([128, 16], BF16, "scT")
    ARENA_N = 19200
    arena_t = P.sb([128, ARENA_N], F32, "arena")
    A = Arena(arena_t, ARENA_N)
    banks = [P.ps([128, 512], F32, f"bank{i}") for i in range(8)]

    ident_b = matsb[:, 0, :]; ones_b = matsb[:, 1, :]; blk_b = matsb[:, 2, :]; rotG = matsb[:, 3, :]; rotM = matsb[:, 4, :]
    ident_f = matsf[:, 0, :]; ones_f = matsf[:, 1, :]
    cosG = rope[:, 0, :]; sinG = rope[:, 1, :]; cosM = rope[:, 2, :]; sinM = rope[:, 3, :]

    def bank(kind="s"):
        if kind == "s":
            i = P.bank_i["s"] % 5
            P.bank_i["s"] += 1
        else:
            i = 5 + P.bank_i["a"] % 3
            P.bank_i["a"] += 1
        return banks[i], f"bank{i}"

    def grp_of_blk(b):
        return 0 if b < 2 else 1

    def blk(b):
        return slice(b * 512, (b + 1) * 512)

    sched = []

    def add_panel(key, ap, nk, ncols):
        sched.append((key, ap, nk, ncols))

    for j in range(4):
        add_panel(("mod", 0, j), w_mod_d[0][:, :, j * 512:(j + 1) * 512], 8, 512)
    for l in range(NL):
        for n in ["G", "GT", "MODS", "N", "NT", "MT", "M", "D1", "D2", "DAB", "D1", "D2", "DAB"]:
            if n == "MODS":
                if l == 0:
                    for j in range(4, 12):
                        add_panel(("mod", 0, j), w_mod_d[0][:, :, j * 512:(j + 1) * 512], 8, 512)
                    for l2 in range(1, NL):
                        for j in range(12):
                            add_panel(("mod", l2, j), w_mod_d[l2][:, :, j * 512:(j + 1) * 512], 8, 512)
                continue
            o = PANEL_OFF[n]
            add_panel(("in", l, n), w_in_d[l][:, :, o:o + len(PANELS_IN[n])], 8, len(PANELS_IN[n]))
        for j in range(2):
            add_panel(("out", l, j), w_out_d[l][:, :, j * 512:(j + 1) * 512], 8, 512)
        for hf, (f0, f1) in enumerate(FF_SPLIT):
            for j in range(f0 // 2, f1 // 2):
                add_panel(("gu", l, j), w_gu_d[l][:, :, j * 512:(j + 1) * 512], 8, 512)
            for q in range(4):
                add_panel(("dn", l, hf, q), w_dn_d[l][:, f0:f1, q * 256:(q + 1) * 256], f1 - f0, 256)
    pstate = {"issued": 0, "next": 0}
    LOOKAHEAD = 2

    def _issue_panel(i):
        key, ap, nk, ncols = sched[i]
        s = i % NSLOT
        dst = slots[s][:, 0:nk * ncols].rearrange("p (k n) -> p k n", k=nk)
        P.dma(lambda e, dst=dst, ap=ap: e.dma_start(out=dst, in_=ap), writes=[f"wslot{s}"], eng="pool")

    def panel(key):
        i = pstate["next"]
        assert sched[i][0] == key, (sched[i][0], key)
        while pstate["issued"] < min(len(sched), i + LOOKAHEAD):
            _issue_panel(pstate["issued"])
            pstate["issued"] += 1
        pstate["next"] += 1
        s = i % NSLOT
        _, _, nk, ncols = sched[i]
        return slots[s][:, 0:nk * ncols].rearrange("p (k n) -> p k n", k=nk), f"wslot{s}"

    def dbg(name, ap, shape, toks):
        if name not in debug:
            return
        o = dout("dbg_" + name, shape)
        P.dma(lambda e: e.dma_start(out=o, in_=ap), reads=toks, eng="pool")
        dbg_outs[name] = shape

    P.dma(lambda e: e.dma_start(out=rope[:], in_=c_rope), writes=["rope"], eng="pool")
    P.dma(lambda e: e.dma_start(out=matsb[:], in_=c_mats), writes=["matsb"], eng="pool")
    P.dma(lambda e: e.dma_start(out=matsf[:], in_=c_mats[:, 0:2, :]), writes=["matsf"])
    P.dma(lambda e: e.dma_start(out=masks[:], in_=c_masks), writes=["masks"])
    P.dma(lambda e: e.dma_start(out=g12[:, 0], in_=g1_d), writes=["g12"])
    P.dma(lambda e: e.dma_start(out=g12[:, 1], in_=g2_d), writes=["g12"])
    P.dma(lambda e: e.dma_start(out=gfin[:], in_=gf_d[:, 0, :]), writes=["gfin"])
    P.dma(lambda e: e.dma_start(out=smallp[:, :, 0:2], in_=gqg_d, allow_slow_non_contiguous=True), writes=["smallp"])
    P.dma(lambda e: e.dma_start(out=smallp[:, :, 2:4], in_=mqg_d, allow_slow_non_contiguous=True), writes=["smallp"])
    P.dma(lambda e: e.dma_start(out=smallp[:, :, 4:5], in_=mkg_d, allow_slow_non_contiguous=True), writes=["smallp"])
    CONST = ["rope", "matsb", "matsf", "masks", "smallp"]

    A.reset()
    xin = [A.f32(1024), A.f32(1024)]
    for t in range(12):
        xi = xin[t % 2]
        tk = f"xin{t % 2}"
        P.dma(lambda e, xi=xi, t=t: e.dma_start(out=xi, in_=x_d[t * 128:(t + 1) * 128, :]), writes=[tk])
        for half in range(2):
            bk, bt = bank()
            def tr(e, xi=xi, bk=bk, half=half):
                ins = None
                for c in range(4):
                    cc = half * 4 + c
                    ins = e.transpose(bk[:, c * 128:(c + 1) * 128], xi[:, cc * 128:(cc + 1) * 128], ident_f)
                return ins
            P.op("pe", tr, reads=[tk, "matsf"], writes=[bt])
            dst = xT[:, half * 4:half * 4 + 4, t * 128:(t + 1) * 128]
            src = bk[:, :].rearrange("p (c n) -> p c n", c=4)
            eng = "act" if half == 0 else "dve"
            if eng == "act":
                P.op("act", lambda e, dst=dst, src=src: e.activation(out=dst, in_=src, func=ACTF.Copy), reads=[bt], writes=[f"xT{t // 4}"])
            else:
                P.op("dve", lambda e, dst=dst, src=src: e.tensor_copy(out=dst, in_=src), reads=[bt], writes=[f"xT{t // 4}"])

    P.mark("mods")
    condf = A.f32(16); tmp16 = A.f32(16)
    condf3 = condf.rearrange("p (k g) -> p k g", k=8); scT3 = scT[:, :].rearrange("p (k g) -> p k g", k=8)
    P.dma(lambda e: e.dma_start(out=condf3, in_=cond_d), writes=["condf"])
    P.op("act", lambda e: e.activation(out=tmp16, in_=condf, func=ACTF.Exp, scale=-1.0), reads=["condf"], writes=["tmp16"])
    P.op("dve", lambda e: e.tensor_scalar_add(out=tmp16, in0=tmp16, scalar1=1.0), reads=["tmp16"], writes=["tmp16"])
    P.op("dve", lambda e: e.reciprocal(out=tmp16, in_=tmp16), reads=["tmp16"], writes=["tmp16"])
    P.op("dve", lambda e: e.tensor_tensor(out=scT[:, :], in0=condf, in1=tmp16, op=ALU.mult), reads=["tmp16", "condf"], writes=["scT"])
    modbuf = {"rowm": [A.f32(512, parts=2), A.f32(512, parts=2)], "bmr": [A.f32(512, parts=2), A.f32(512, parts=2)]}

    def mods_step(l, j):
        rowm, bmr = modbuf["rowm"], modbuf["bmr"]
        wp, wt = panel(("mod", l, j))
        bk, bt = bank()
        i2 = j % 2
        P.dma(lambda e: e.dma_start(out=bmr[i2], in_=b_mod_d[l][:, j * 512:(j + 1) * 512]), writes=[f"bmr{i2}"])
        def mm(e):
            ins = None
            for k in range(8):
                ins = e.matmul(bk[0:2, :], lhsT=scT3[:, k, :], rhs=wp[:, k, :], start=(k == 0), stop=(k == 7))
            return ins
        P.op("pe", mm, reads=[wt, "scT"], writes=[bt])
        P.op("dve", lambda e: e.tensor_tensor(out=rowm[i2], in0=bk[0:2, :], in1=bmr[i2], op=ALU.add), reads=[bt, f"bmr{i2}"], writes=[f"rowm{i2}"])
        bk2, bt2 = bank()
        def trm(e):
            ins = None
            for c in range(4):
                ins = e.matmul(bk2[:, 2 * c:2 * c + 2], lhsT=rowm[i2][:, c * 128:(c + 1) * 128], rhs=ident_f[0:2, 0:2], start=True, stop=True)
            return ins
        P.op("pe", trm, reads=[f"rowm{i2}", "matsf"], writes=[bt2])
        P.op("act", lambda e: e.activation(out=modsT[:, l, 4 * j:4 * j + 4, :], in_=bk2[:, 0:8].rearrange("p (c g) -> p c g", c=4), func=ACTF.Copy), reads=[bt2], writes=[f"modsT{l}"])

    def mods_finish(l):
        for w, base in ((0, 8), (1, 32)):
            P.op("dve", lambda e, w=w, base=base: e.tensor_scalar_add(out=gsb[:, l, w], in0=modsT[:, l, base:base + 8, :], scalar1=1.0), reads=[f"modsT{l}"], writes=[f"gsb{l}"])
            P.op("dve", lambda e, w=w: e.tensor_tensor(out=gsb[:, l, w], in0=gsb[:, l, w], in1=g12[:, w, l, :].unsqueeze(2).to_broadcast([128, 8, 2]), op=ALU.mult), reads=[f"gsb{l}", "g12"], writes=[f"gsb{l}"])

    def mods_gs(l, which):
        w, base = ((0, 8), (1, 32))[which]
        P.op("dve", lambda e: e.tensor_scalar_add(out=gsb[:, l, w], in0=modsT[:, l, base:base + 8, :], scalar1=1.0), reads=[f"modsT{l}"], writes=[f"gsb{l}"])
        P.op("dve", lambda e: e.tensor_tensor(out=gsb[:, l, w], in0=gsb[:, l, w], in1=g12[:, w, l, :].unsqueeze(2).to_broadcast([128, 8, 2]), op=ALU.mult), reads=[f"gsb{l}", "g12"], writes=[f"gsb{l}"])

    for j in range(4):
        mods_step(0, j)
    mods_gs(0, 0)
    dbg("modsT", modsT[:], [128, NL, 48, 2], ["modsT0", "modsT1"])
    dbg("xT", xT[:], [128, 8, T], ["xT0", "xT1", "xT2"])

    def rstd_from_ss(ps_ap, n, dst, rtoks, wtok):
        P.op("act", lambda e: e.activation(out=dst, in_=ps_ap, func=ACTF.Ln, bias=EPS, scale=1.0 / n), reads=rtoks, writes=[wtok])
        P.op("act", lambda e: e.activation(out=dst, in_=dst, func=ACTF.Exp, scale=-0.5), reads=[wtok], writes=[wtok])

    def norm_fm(gs_ap_fn, shift_ap_fn, out_fn, out_tok_fn, tmpn):
        sq = [A.bf16(512), A.bf16(512)]
        rst = A.f32(512)
        tmpf = [A.f32(512), A.f32(512)]
        for b in range(NB):
            g = grp_of_blk(b)
            bk, bt = bank("a")
            for c in range(8):
                s = sq[c % 2]
                P.op("act", lambda e, s=s, c=c, b=b: e.activation(out=s, in_=xT[:, c, blk(b)], func=ACTF.Square), reads=[f"xT{b}"], writes=[f"{tmpn}sq{c % 2}"])
                P.op("pe", lambda e, s=s, c=c, bk=bk: e.matmul(bk[:, :], lhsT=ones_b, rhs=s, start=(c == 0), stop=(c == 7)), reads=[f"{tmpn}sq{c % 2}", "matsb"], writes=[bt])
            rstd_from_ss(bk[:, :], float(D), rst, [bt], f"{tmpn}rst")
            for c in range(8):
                tf = tmpf[c % 2]
                P.op("dve", lambda e, tf=tf, c=c, b=b: e.tensor_tensor(out=tf, in0=xT[:, c, blk(b)], in1=rst, op=ALU.mult), reads=[f"xT{b}", f"{tmpn}rst"], writes=[f"{tmpn}tf{c % 2}"])
                o = out_fn(c, b)
                if shift_ap_fn is not None:
                    P.op("act", lambda e, tf=tf, o=o, c=c, g=g: e.activation(out=o, in_=tf, func=ACTF.Identity, bias=shift_ap_fn(c, g), scale=gs_ap_fn(c, g)), reads=[f"{tmpn}tf{c % 2}", "gsb0", "gsb1", "modsT0", "modsT1", "gfin"], writes=[out_tok_fn(b)])
                else:
                    P.op("act", lambda e, tf=tf, o=o, c=c, g=g: e.activation(out=o, in_=tf, func=ACTF.Copy, scale=gs_ap_fn(c, g)), reads=[f"{tmpn}tf{c % 2}", "gsb0", "gsb1", "modsT0", "modsT1", "gfin"], writes=[out_tok_fn(b)])

    def proj_fm(wp, wt, c0, m, b, kind="s"):
        bk, bt = bank(kind)
        def f(e):
            ins = None
            for k in range(8):
                ins = e.matmul(bk[0:m, :], lhsT=wp[:, k, c0:c0 + m], rhs=hT[:, k, blk(b)], start=(k == 0), stop=(k == 7))
            return ins
        P.op("pe", f, reads=[wt, f"hT{b}"], writes=[bt])
        return bk, bt

    def proj_tm(wp, wt, c0, n, t0, m=128, kind="s", bk_bt=None, col0=0):
        bk, bt = bk_bt if bk_bt is not None else bank(kind)
        def f(e):
            ins = None
            for k in range(8):
                ins = e.matmul(bk[0:m, col0:col0 + n], lhsT=hT[:, k, t0:t0 + m], rhs=wp[:, k, c0:c0 + n], start=(k == 0), stop=(k == 7))
            return ins
        P.op("pe", f, reads=[wt, f"hT{t0 // 512}"], writes=[bt])
        return bk, bt

    HT = ["hT0", "hT1", "hT2"]
    MIXT = ["mixT0", "mixT1", "mixT2"]

    def attend(QT, nq, chunks, dst, half, rd_toks, wr_tok, tmp):
        bo, bot = bank("a"); bd, bdt = bank("a")
        n = len(chunks)
        pendq = []
        sl = slice(half * 64, half * 64 + 64)
        for i, (KT, V, brhs) in enumerate(chunks):
            bs, bst = bank()
            def qk(e, KT=KT, bs=bs, brhs=brhs):
                ins = e.matmul(bs[:, 0:nq], lhsT=KT, rhs=QT, start=True, stop=(brhs is None))
                if brhs is not None:
                    ins = e.matmul(bs[:, 0:nq], lhsT=ident_b, rhs=brhs, start=False, stop=True)
                return ins
            P.op("pe", qk, reads=rd_toks + ["matsb"], writes=[bst])
            if len(pendq) >= 2:
                pendq.pop(0)()
            pt = tmp["pt"][i % len(tmp["pt"])]
            ptt = f"{tmp['name']}pt{i % len(tmp['pt'])}"
            P.op("act", lambda e, pt=pt, bs=bs: e.activation(out=pt[:, 0:nq], in_=bs[:, 0:nq], func=ACTF.Exp), reads=[bst], writes=[ptt])
            acc, acct = tmp["acc"], tmp["acc_tok"]
            if i == 0:
                P.op("dve", lambda e, pt=pt: e.tensor_copy(out=acc[:, 0:nq], in_=pt[:, 0:nq]), reads=[ptt], writes=[acct])
            else:
                P.op("dve", lambda e, pt=pt: e.tensor_tensor(out=acc[:, 0:nq], in0=acc[:, 0:nq], in1=pt[:, 0:nq], op=ALU.add), reads=[ptt, acct], writes=[acct])
            def pv(i=i, V=V, pt=pt, ptt=ptt):
                P.op("pe", lambda e: e.matmul(bo[:, 0:nq], lhsT=V, rhs=pt[:, 0:nq], start=(i == 0), stop=(i == n - 1)), reads=rd_toks + [ptt], writes=[bot])
            pendq.append(pv)
        for f_ in pendq:
            f_()
        hi, lo = tmp["hl"]
        hit, lot = tmp["name"] + "hi", tmp["name"] + "lo"
        P.op("act", lambda e: e.activation(out=hi[:, 0:nq], in_=acc[:, 0:nq], func=ACTF.Copy), reads=[acct], writes=[hit])
        P.op("dve", lambda e: e.tensor_tensor(out=lo[:, 0:nq], in0=acc[:, 0:nq], in1=hi[:, 0:nq], op=ALU.subtract), reads=[acct, hit], writes=[lot])
        def denf(e):
            e.matmul(bd[:, 0:nq], lhsT=ones_b, rhs=hi[:, 0:nq], start=True, stop=False)
            return e.matmul(bd[:, 0:nq], lhsT=ones_b, rhs=lo[:, 0:nq], start=False, stop=True)
        P.op("pe", denf, reads=[hit, lot, "matsb"], writes=[bdt])
        rd = tmp["rd"]
        P.op("dve", lambda e: e.reciprocal(out=rd[sl, 0:nq], in_=bd[sl, 0:nq]), reads=[bdt], writes=[tmp["name"] + "rd"])
        P.op("dve", lambda e: e.tensor_tensor(out=dst, in0=bo[sl, 0:nq], in1=rd[sl, 0:nq], op=ALU.mult), reads=[bot, tmp["name"] + "rd"], writes=[wr_tok])

    ARENA0 = A.off

    def load_cache_T(src_dram, ncol, dstT_fn, name, pad_to=None):
        pass

    for l in range(NL):
        if stage < 1:
            break
        P.barrier()
        A.reset()
        P.mark(f"L{l} norm1")
        norm_fm(lambda c, g: gsb[:, l, 0, c, g:g + 1], lambda c, g: modsT[:, l, c, g:g + 1],
                lambda c, b: hT[:, c, blk(b)], lambda b: f"hT{b}", f"n1_{l}")
        if l == 0:
            dbg("hT", hT[:], [128, 8, T], HT)
        if stage < 2:
            break
        P.barrier()
        A.reset()

        P.mark(f"L{l} gqa")
        QT = A.bf16(2 * T).rearrange("p (c t) -> p c t", c=2)
        KT = A.bf16(2 * 2048).rearrange("p (c t) -> p c t", c=2)
        Vg = A.bf16(16 * 256).rearrange("p (t c) -> p t c", t=16)
        ctm = A.bf16(4 * 256).rearrange("p (t c) -> p t c", t=4)
        tqs = [{"sq": A.bf16(512), "f1": A.f32(512), "f2": A.f32(512), "xc": A.bf16(512), "xs": A.bf16(512), "n": f"tq{i}_"} for i in range(3)]
        tqi = [0]
        tq = tqs[0]
        atmp = {"name": f"ga{l}", "pt": [A.bf16(512), A.bf16(512), A.bf16(512)], "rd": A.f32(512), "acc": A.f32(512), "acc_tok": "att_acc", "hl": (A.bf16(512), A.bf16(512))}
        otile = [A.f32(256), A.f32(256)]
        krow = A.f32(2 * 64).rearrange("p (h d) -> p h d", h=2)
        P.dma(lambda e: e.dma_start(out=krow, in_=gqk_row_d[:, l]), writes=["krow"])
        if l == 0:
            modbuf["rowm"] = [A.f32(512, parts=2), A.f32(512, parts=2)]
            modbuf["bmr"] = [A.f32(512, parts=2), A.f32(512, parts=2)]
        P.dma(lambda e: e.dma_start(out=ctm, in_=cgk_d[l].rearrange("(t p) c -> p t c", p=128)), writes=["ctm"], eng="pool")
        P.dma(lambda e: e.dma_start(out=Vg[:, 0:4, :], in_=cgv_d[l].rearrange("(t p) c -> p t c", p=128)), writes=["Vg"], eng="pool")
        for kv in range(2):
            bk, bt = bank()
            bkb = bk[:, :].bitcast(BF16)
            def trc(e, kv=kv, bkb=bkb):
                ins = None
                for t in range(4):
                    ins = e.transpose(bkb[:, t * 128:(t + 1) * 128], ctm[:, t, kv * 128:(kv + 1) * 128], ident_b)
                return ins
            P.op("pe", trc, reads=["ctm", "matsb"], writes=[bt])
            P.op("act", lambda e, kv=kv, bkb=bkb: e.activation(out=KT[:, kv, 0:512], in_=bkb[:, 0:512], func=ACTF.Copy), reads=[bt], writes=["KT"])

        def qk_chunk(bk, bt, gain_ap, extra, do_rope, dst, t0, wtok, nparts=128, blkmat=None, cos=None, sin=None, rot=None, n=512):
            blkmat = blk_b if blkmat is None else blkmat
            tqi[0] += 1
            tq = tqs[tqi[0] % 2]
            tn = tq["n"]
            P.op("act", lambda e: e.activation(out=tq["sq"][:, 0:n], in_=bk[:, 0:n], func=ACTF.Square), reads=[bt], writes=[tn + "sq"])
            b2, b2t = bank()
            P.op("pe", lambda e: e.matmul(b2[:, 0:n], lhsT=blkmat, rhs=tq["sq"][:, 0:n], start=True, stop=True), reads=[tn + "sq", "matsb"], writes=[b2t])
            rstd_from_ss(b2[:, 0:n], 64.0, tq["f1"][:, 0:n], [b2t], tn + "f1")
            P.op("dve", lambda e: e.tensor_tensor(out=tq["f2"][:, 0:n], in0=bk[:, 0:n], in1=tq["f1"][:, 0:n], op=ALU.mult), reads=[bt, tn + "f1"], writes=[tn + "f2"])
            if not do_rope:
                P.op("dve", lambda e: e.tensor_scalar(out=dst, in0=tq["f2"][:, 0:n], scalar1=gain_ap, scalar2=extra, op0=ALU.mult, op1=ALU.mult), reads=[tn + "f2", "smallp"], writes=[wtok])
                return
            P.op("dve", lambda e: e.tensor_scalar(out=tq["f2"][:, 0:n], in0=tq["f2"][:, 0:n], scalar1=gain_ap, scalar2=extra, op0=ALU.mult, op1=ALU.mult), reads=[tn + "f2", "smallp"], writes=[tn + "f2"])
            rope_apply(tq["f2"][:, 0:n], dst, t0, wtok, cosG, sinG, rotG, [tn + "f2"], n, tq=tq)

        def rope_apply(src, dst, t0, wtok, cos, sin, rot, rtoks, n, parts=128, pre=None, tq=None):
            if tq is None:
                tqi[0] += 1
                tq = tqs[tqi[0] % 2]
            tn = tq["n"]
            xc = tq["xc"][0:parts, 0:n]; xs = tq["xs"][0:parts, 0:n]
            if pre is None:
                P.op("dve", lambda e: e.tensor_tensor(out=xc, in0=src, in1=cos[0:parts, t0:t0 + n], op=ALU.mult), reads=rtoks + ["rope"], writes=[tn + "xc"])
                P.op("dve", lambda e: e.tensor_tensor(out=xs, in0=src, in1=sin[0:parts, t0:t0 + n], op=ALU.mult), reads=rtoks + ["rope"], writes=[tn + "xs"])
            else:
                P.op("dve", lambda e: e.scalar_tensor_tensor(out=xc, in0=src, scalar=pre, in1=cos[0:parts, t0:t0 + n], op0=ALU.mult, op1=ALU.mult), reads=rtoks + ["rope"], writes=[tn + "xc"])
                P.op("dve", lambda e: e.scalar_tensor_tensor(out=xs, in0=src, scalar=pre, in1=sin[0:parts, t0:t0 + n], op0=ALU.mult, op1=ALU.mult), reads=rtoks + ["rope"], writes=[tn + "xs"])
            b3, b3t = bank()
            def f(e):
                e.matmul(b3[0:parts, 0:n], lhsT=ident_b[0:parts, 0:parts], rhs=xc, start=True, stop=False)
                return e.matmul(b3[0:parts, 0:n], lhsT=rot[0:parts, 0:parts], rhs=xs, start=False, stop=True)
            P.op("pe", f, reads=[tn + "xc", tn + "xs", "matsb"], writes=[b3t])
            P.op("act", lambda e: e.activation(out=dst, in_=b3[0:parts, 0:n], func=ACTF.Copy), reads=[b3t], writes=[wtok])

        wp, wt = panel(("in", l, "G"))

        def qk_item(c0cols, gain_ap, extra, do_rope, dst, t0, wtok, b, tq):
            tn = tq["n"]
            bk, bt = proj_fm(wp, wt, c0cols, 128, b)
            P.op("act", lambda e: e.activation(out=tq["sq"], in_=bk[:, :], func=ACTF.Square), reads=[bt], writes=[tn + "sq"])
            yield
            b2, b2t = bank()
            P.op("pe", lambda e: e.matmul(b2[:, :], lhsT=blk_b, rhs=tq["sq"], start=True, stop=True), reads=[tn + "sq", "matsb"], writes=[b2t])
            rstd_from_ss(b2[:, :], 64.0, tq["f1"], [b2t], tn + "f1")
            P.op("dve", lambda e: e.tensor_tensor(out=tq["f2"], in0=bk[:, :], in1=tq["f1"], op=ALU.mult), reads=[bt, tn + "f1"], writes=[tn + "f2"])
            if not do_rope:
                P.op("dve", lambda e: e.tensor_scalar(out=dst, in0=tq["f2"], scalar1=gain_ap, scalar2=extra, op0=ALU.mult, op1=ALU.mult), reads=[tn + "f2", "smallp"], writes=[wtok])
                return
            P.op("dve", lambda e: e.tensor_scalar(out=tq["f2"], in0=tq["f2"], scalar1=gain_ap, scalar2=extra, op0=ALU.mult, op1=ALU.mult), reads=[tn + "f2", "smallp"], writes=[tn + "f2"])
            P.op("dve", lambda e: e.tensor_tensor(out=tq["xc"], in0=tq["f2"], in1=cosG[:, t0:t0 + 512], op=ALU.mult), reads=[tn + "f2", "rope"], writes=[tn + "xc"])
            P.op("dve", lambda e: e.tensor_tensor(out=tq["xs"], in0=tq["f2"], in1=sinG[:, t0:t0 + 512], op=ALU.mult), reads=[tn + "f2", "rope"], writes=[tn + "xs"])
            yield
            b3, b3t = bank()
            def f(e):
                e.matmul(b3[:, :], lhsT=ident_b, rhs=tq["xc"], start=True, stop=False)
                return e.matmul(b3[:, :], lhsT=rotG, rhs=tq["xs"], start=False, stop=True)
            P.op("pe", f, reads=[tn + "xc", tn + "xs", "matsb"], writes=[b3t])
            P.op("act", lambda e: e.activation(out=dst, in_=b3[:, :], func=ACTF.Copy), reads=[b3t], writes=[wtok])

        items = []
        for b in range(NB):
            smp = b < 2
            for c in range(2):
                items.append((c * 128, smallp[:, l, 0:1], 0.125, smp, QT[:, c, blk(b)], b * 512, "QT", b))
            for kv in range(2):
                items.append((256 + kv * 128, smallp[:, l, 1:2], 1.0, smp, KT[:, kv, 512 + b * 512:1024 + b * 512], b * 512, "KT", b))
        active = []
        for ii, it in enumerate(items + [None, None]):
            if it is not None:
                active.append(qk_item(*it, tqs[ii % 3]))
            for g in list(reversed(active)):
                try:
                    next(g)
                except StopIteration:
                    active.remove(g)
        assert not active
        wp, wt = panel(("in", l, "GT"))
        for tt in range(12):
            pr = tt >= 8
            bk, bt = proj_tm(wp, wt, 0, 384 if pr else 256, tt * 128)
            P.op("act", lambda e, bk=bk, tt=tt: e.activation(out=Vg[:, 4 + tt, :], in_=bk[:, 0:256], func=ACTF.Copy), reads=[bt], writes=["Vg"])
            if pr:
                sq_, tl = tt - 8, otile[tt % 2]
                tn = f"otile{tt % 2}"
                seq, half = sq_ // 2, sq_ % 2
                P.op("dve", lambda e, bk=bk, tl=tl: e.tensor_copy(out=tl[:, 0:128].rearrange("p (h d) -> p h d", h=2), in_=bk[:, 0:256].rearrange("p (h r d) -> p h r d", h=2, r=2)[:, :, 0, :]), reads=[bt], writes=[tn])
                P.dma(lambda e, tl=tl, seq=seq, half=half: e.dma_start(out=ngv_o[seq, l, half * 128:(half + 1) * 128, :], in_=tl[:, 0:128]), reads=[tn])
                P.op("act", lambda e, bk=bk: e.activation(out=tq["f1"][:, 0:128], in_=bk[:, 256:384], func=ACTF.Square), reads=[bt], writes=["tq0_f1"])
                P.op("dve", lambda e: e.tensor_reduce(out=tq["f2"][:, 0:2], in_=tq["f1"][:, 0:128].rearrange("p (h d) -> p h d", h=2), axis=AX.X, op=ALU.add), reads=["tq0_f1"], writes=["tq0_f2"])
                rstd_from_ss(tq["f2"][:, 0:2], 64.0, tq["f2"][:, 0:2], ["tq0_f2"], "tq0_f2")
                P.op("dve", lambda e, bk=bk, tl=tl: e.tensor_tensor(out=tl[:, 128:256].rearrange("p (h d) -> p h d", h=2), in0=bk[:, 256:384].rearrange("p (h d) -> p h d", h=2), in1=tq["f2"][:, 0:2].unsqueeze(2).to_broadcast([128, 2, 64]), op=ALU.mult), reads=[bt, "tq0_f2"], writes=[tn])
                P.op("dve", lambda e, tl=tl: e.tensor_tensor(out=tl[:, 128:256].rearrange("p (h d) -> p h d", h=2), in0=tl[:, 128:256].rearrange("p (h d) -> p h d", h=2), in1=krow, op=ALU.mult), reads=[tn, "krow"], writes=[tn])
                P.dma(lambda e, tl=tl, seq=seq, half=half: e.dma_start(out=ngk_o[seq, l, half * 128:(half + 1) * 128, :], in_=tl[:, 128:256]), reads=[tn])
        for h in range(4):
            c, half = h // 2, h % 2
            kv = h // 2
            rows = slice(half * 64, half * 64 + 64)
            for qb in range(2):
                chunks = [(KT[rows, kv, i * 128:(i + 1) * 128], Vg[:, i, kv * 128:(kv + 1) * 128], None) for i in range(12)]
                attend(QT[rows, c, blk(qb)], 512, chunks, mixT[rows, c, blk(qb)], half, ["QT", "KT", "Vg"], f"mixT{qb}", atmp)
                if l == 0:
                    todo = [(0, j) for j in range(4, 12)] + [(l2, j) for l2 in range(1, NL) for j in range(12)]
                    slot_i = h * 2 + qb
                    for ti, (l2, j) in enumerate(todo):
                        if ti * 8 // len(todo) == slot_i:
                            mods_step(l2, j)
                            if (l2, j) == (0, 11):
                                mods_gs(0, 1)
                    if slot_i == 7:
                        for l2 in range(1, NL):
                            mods_finish(l2)
            for s in range(2):
                t0 = 1024 + 256 * s
                chunks = [(KT[rows, kv, 512 + t0 + i * 128:512 + t0 + (i + 1) * 128], Vg[:, 12 + 2 * s + i, kv * 128:(kv + 1) * 128], None) for i in range(2)]
                attend(QT[rows, c, t0:t0 + 256], 256, chunks, mixT[rows, c, t0:t0 + 256], half, ["QT", "KT", "Vg"], "mixT2", atmp)
        if l == 0:
            dbg("oa", mixT[:, 0:2, :], [128, 2, T], MIXT)
        if stage < 3:
            break
        P.barrier()
        A.reset()

        P.mark(f"L{l} na")
        QT = A.bf16(2 * T).rearrange("p (c t) -> p c t", c=2)
        KT = A.bf16(2 * 2048).rearrange("p (c t) -> p c t", c=2)
        Vn = A.bf16(16 * 512).rearrange("p (t c) -> p t c", t=16)
        ctm = A.bf16(4 * 256).rearrange("p (t c) -> p t c", t=4)
        natab = A.bf16(4 * 2 * 1152).rearrange("p (h t n) -> p h t n", h=4, t=2)
        atmp = {"name": f"na{l}", "pt": [A.bf16(512), A.bf16(512), A.bf16(512)], "rd": A.f32(512), "acc": A.f32(512), "acc_tok": "att_acc", "hl": (A.bf16(512), A.bf16(512))}
        otile = [A.f32(256), A.f32(256)]
        if "ntl" not in skip:
            P.dma(lambda e: e.dma_start(out=natab, in_=natab_d[l]), writes=["natab"], eng="pool")
        if "nch" not in skip:
            P.dma(lambda e: e.dma_start(out=ctm, in_=cnk_d[l].rearrange("(t p) c -> p t c", p=128)), writes=["ctm"], eng="pool")
            P.dma(lambda e: e.dma_start(out=Vn[:, 0:4, :], in_=cnv_d[l].rearrange("(t p) c -> p t c", p=128)), writes=["Vn"], eng="pool")
        for c in (range(2) if "nch" not in skip else []):
            bk, bt = bank()
            bkb = bk[:, :].bitcast(BF16)
            def trc(e, c=c, bkb=bkb):
                ins = None
                for t in range(4):
                    ins = e.transpose(bkb[:, t * 128:(t + 1) * 128], ctm[:, t, c * 128:(c + 1) * 128], ident_b)
                return ins
            P.op("pe", trc, reads=["ctm", "matsb"], writes=[bt])
            P.op("act", lambda e, c=c, bkb=bkb: e.activation(out=KT[:, c, 0:512], in_=bkb[:, 0:512], func=ACTF.Copy), reads=[bt], writes=["KT"])
        wp, wt = panel(("in", l, "N"))
        for b in (range(NB) if "npj" not in skip else []):
            for c in range(2):
                bk, bt = proj_fm(wp, wt, c * 128, 128, b)
                P.op("act", lambda e, bk=bk, c=c, b=b: e.activation(out=QT[:, c, blk(b)], in_=bk[:, :], func=ACTF.Copy, scale=0.125), reads=[bt], writes=["QT"])
                bk, bt = proj_fm(wp, wt, 256 + c * 128, 128, b)
                P.op("dve", lambda e, bk=bk, c=c, b=b: e.tensor_copy(out=KT[:, c, 512 + b * 512:1024 + b * 512], in_=bk[:, :]), reads=[bt], writes=["KT"])
        wp, wt = panel(("in", l, "NT"))
        for tt in (range(12) if "nvp" not in skip else []):
            bk, bt = proj_tm(wp, wt, 0, 512, tt * 128)
            if "nvc" not in skip:
                P.op("act", lambda e, bk=bk, tt=tt: e.activation(out=Vn[:, 4 + tt, :], in_=bk[:, :], func=ACTF.Copy), reads=[bt], writes=["Vn"])
            if tt >= 8 and "nvo" not in skip:
                sq_, tl = tt - 8, otile[tt % 2]
                tn = f"otile{tt % 2}"
                seq, half = sq_ // 2, sq_ % 2
                if "nvo1" not in skip:
                    P.op("act", lambda e, bk=bk, tl=tl: e.activation(out=tl[:, 0:256].rearrange("p (h d) -> p h d", h=4), in_=bk[:, 0:512].rearrange("p (h r d) -> p h r d", h=4, r=2)[:, :, 0, :], func=ACTF.Copy), reads=[bt], writes=[tn])
                if "nvo2" not in skip:
                    P.dma(lambda e, tl=tl, seq=seq, half=half: e.dma_start(out=nnv_o[seq, l, half * 128:(half + 1) * 128, :], in_=tl[:, 0:256]), reads=[tn])
        wp, wt = panel(("in", l, "MT"))
        MT_RANGE = range(8, 12) if "mt" not in skip else range(0)
        mrow = A.f32(128)
        P.dma(lambda e: e.dma_start(out=mrow, in_=mkg_row_d[:, l]), writes=["mrow"])
        tf1 = A.f32(128); tf2 = A.f32(2)
        for tt in MT_RANGE:
            sq_ = tt - 8
            seq, half = sq_ // 2, sq_ % 2
            bk, bt = proj_tm(wp, wt, 0, 416, tt * 128)
            tl = otile[tt % 2]
            tn = f"otile{tt % 2}"
            P.op("dve", lambda e, bk=bk, tl=tl: e.tensor_copy(out=tl[:, 0:256], in_=bk[:, 160:416]), reads=[bt], writes=[tn])
            P.dma(lambda e, tl=tl, seq=seq, half=half: e.dma_start(out=nnk_o[seq, l, half * 128:(half + 1) * 128, :], in_=tl[:, 0:256]), reads=[tn])
            tl2 = A.f32(160) if tt == 8 else tl2
            P.op("act", lambda e, bk=bk: e.activation(out=tf1, in_=bk[:, 0:128], func=ACTF.Square), reads=[bt], writes=["mt_tf1"])
            P.op("dve", lambda e: e.tensor_reduce(out=tf2[:, 0:1], in_=tf1, axis=AX.X, op=ALU.add), reads=["mt_tf1"], writes=["mt_tf2"])
            rstd_from_ss(tf2[:, 0:1], 128.0, tf2[:, 1:2], ["mt_tf2"], "mt_tf2b")
            P.op("dve", lambda e, bk=bk, tl2=tl2: e.scalar_tensor_tensor(out=tl2[:, 0:128], in0=bk[:, 0:128], scalar=tf2[:, 1:2], in1=mrow, op0=ALU.mult, op1=ALU.mult), reads=[bt, "mt_tf2b", "mrow"], writes=["mt_tl2"])
            P.op("act", lambda e, bk=bk, tl2=tl2: e.activation(out=tl2[:, 128:160], in_=bk[:, 128:160], func=ACTF.Copy), reads=[bt], writes=["mt_tl2"])
            P.dma(lambda e, tl2=tl2, seq=seq, half=half: e.dma_start(out=nckv_o[seq, l, half * 128:(half + 1) * 128, :], in_=tl2[:, 0:128]), reads=["mt_tl2"])
            P.dma(lambda e, tl2=tl2, seq=seq, half=half: e.dma_start(out=nkr_o[seq, l, half * 128:(half + 1) * 128, :], in_=tl2[:, 128:160]), reads=["mt_tl2"])
        NA_GROUPS = [(0, 4, 0, 4, 0), (4, 8, 0, 6, 1), (8, 13, 2, 8, 1), (13, 16, 4, 8, 0)]
        for h in range(4):
            c, half = h // 2, h % 2
            rows = slice(half * 64, half * 64 + 64)
            for (r0, r1, kc0, kc1, tab) in (NA_GROUPS if 'nas' not in skip else []):
                nq = (r1 - r0) * 64
                q0 = r0 * 64
                chunks = []
                for kc in range(kc0, kc1):
                    e_top = r0 - 2 * kc + 7
                    brhs = natab[:, h, tab, (e_top + 1) * 64:(e_top + 1) * 64 + nq] if "nab" not in skip else None
                    chunks.append((KT[rows, c, 512 + kc * 128:512 + (kc + 1) * 128], Vn[:, 4 + kc, h * 128:(h + 1) * 128], brhs))
                for i in range(4):
                    chunks.append((KT[rows, c, i * 128:(i + 1) * 128], Vn[:, i, h * 128:(h + 1) * 128], None))
                attend(QT[rows, c, q0:q0 + nq], nq, chunks, mixT[rows, 2 + c, q0:q0 + nq], half, ["QT", "KT", "Vn", "natab"], "mixT0", atmp)
            for s in (range(2) if "nap" not in skip else []):
                t0 = 1024 + 256 * s
                chunks = [(KT[rows, c, 512 + t0 + i * 128:512 + t0 + (i + 1) * 128], Vn[:, 12 + 2 * s + i, h * 128:(h + 1) * 128], None) for i in range(2)]
                attend(QT[rows, c, t0:t0 + 256], 256, chunks, mixT[rows, 2 + c, t0:t0 + 256], half, ["QT", "KT", "Vn"], "mixT2", atmp)
        if l == 0:
            dbg("ob", mixT[:, 2:4, :], [128, 2, T], MIXT)
        if stage < 4:
            break
        P.barrier()
        A.reset()

        P.mark(f"L{l} mla")
        QTm = A.bf16(4 * T, parts=96).rearrange("p (h t) -> p h t", h=4)
        KTm = A.bf16(4 * 2048, parts=96).rearrange("p (h t) -> p h t", h=4)
        Vm_raw = A.bf16(16 * 512)
        Vm = Vm_raw.rearrange("p (t c) -> p t c", t=16)
        cqn = Vm_raw[:, 0:2 * T].rearrange("p (c t) -> p c t", c=2)
        ckvn = A.bf16(2048)
        krT = A.bf16(2048, parts=96)
        wq = A.bf16(2 * 384).rearrange("p (k n) -> p k n", k=2)
        wkv = A.bf16(768)
        ctm = A.bf16(4 * 128).rearrange("p (t c) -> p t c", t=4)
        ktm = A.bf16(4 * 96).rearrange("p (t c) -> p t c", t=4)
        tqs = [{"sq": A.bf16(512), "f1": A.f32(512), "xc": A.bf16(512), "xs": A.bf16(512), "sq2": A.bf16(512), "n": f"tq{i}_"} for i in range(2)]
        tqi = [0]
        atmp = {"name": "tq0_", "pt": [A.bf16(512), A.bf16(512), A.bf16(512)], "rd": A.f32(512), "acc": tqs[0]["f1"], "acc_tok": "tq0_f1", "hl": (tqs[0]["xc"], tqs[0]["xs"])}
        P.dma(lambda e: e.dma_start(out=wq, in_=wq_d[l]), writes=["wq"], eng="pool")
        P.dma(lambda e: e.dma_start(out=wkv, in_=wkv_d[l]), writes=["wkv"], eng="pool")
        P.dma(lambda e: e.dma_start(out=ctm, in_=cckv_d[l].rearrange("(t p) c -> p t c", p=128)), writes=["ctm"], eng="pool")
        P.op("pool", lambda e: e.memset(ktm, 0.0), writes=["ktm"])
        P.dma(lambda e: e.dma_start(out=ktm[:, :, 64:96], in_=ckr_d[l].rearrange("(t p) c -> p t c", p=128)), reads=["ktm"], writes=["ktm"], eng="pool")
        bk, bt = bank()
        bkb = bk[:, :].bitcast(BF16)
        def trc(e, bkb=bkb):
            ins = None
            for t in range(4):
                ins = e.transpose(bkb[:, t * 128:(t + 1) * 128], ctm[:, t, :], ident_b)
            return ins
        P.op("pe", trc, reads=["ctm", "matsb"], writes=[bt])
        P.op("act", lambda e, bkb=bkb: e.activation(out=ckvn[:, 0:512], in_=bkb[:, 0:512], func=ACTF.Copy), reads=[bt], writes=["ckvn"])
        bk, bt = bank()
        bkb = bk[:, :].bitcast(BF16)
        def trk(e, bkb=bkb):
            ins = None
            for t in range(4):
                ins = e.transpose(bkb[0:96, t * 128:(t + 1) * 128], ktm[:, t, :], ident_b)
            return ins
        P.op("pe", trk, reads=["ktm", "matsb"], writes=[bt])
        P.op("act", lambda e, bkb=bkb: e.activation(out=krT[:, 0:512], in_=bkb[0:96, 0:512], func=ACTF.Copy), reads=[bt], writes=["krT"])

        wp, wt = panel(("in", l, "M"))
        for b in range(NB):
            smp = b < 2
            tqi[0] += 1
            tq = tqs[tqi[0] % 2]; sq2 = tq["sq2"]; tn = tq["n"]
            ba, bat = proj_fm(wp, wt, 0, 128, b)
            bb, bbt = proj_fm(wp, wt, 128, 128, b)
            P.op("act", lambda e, ba=ba: e.activation(out=tq["sq"], in_=ba[:, :], func=ACTF.Square), reads=[bat], writes=[tn + "sq"])
            P.op("act", lambda e, bb=bb: e.activation(out=sq2, in_=bb[:, :], func=ACTF.Square), reads=[bbt], writes=[tn + "sq2"])
            b2, b2t = bank()
            def ssf(e, b2=b2):
                e.matmul(b2[:, :], lhsT=ones_b, rhs=tq["sq"], start=True, stop=False)
                return e.matmul(b2[:, :], lhsT=ones_b, rhs=sq2, start=False, stop=True)
            P.op("pe", ssf, reads=[tn + "sq", tn + "sq2", "matsb"], writes=[b2t])
            rstd_from_ss(b2[:, :], 256.0, tq["f1"], [b2t], tn + "f1")
            for c, (bq, bqt) in enumerate(((ba, bat), (bb, bbt))):
                P.op("dve", lambda e, bq=bq, c=c, b=b: e.scalar_tensor_tensor(out=cqn[:, c, blk(b)], in0=bq[:, :], scalar=smallp[:, l, 2 + c:3 + c], in1=tq["f1"], op0=ALU.mult, op1=ALU.mult), reads=[bqt, tn + "f1", "smallp"], writes=["cqn"])
            tqi[0] += 1
            tq = tqs[tqi[0] % 2]; tn = tq["n"]
            bc, bct = proj_fm(wp, wt, 256, 128, b)
            P.op("act", lambda e, bc=bc: e.activation(out=tq["sq"], in_=bc[:, :], func=ACTF.Square), reads=[bct], writes=[tn + "sq"])
            b2, b2t = bank()
            P.op("pe", lambda e, b2=b2: e.matmul(b2[:, :], lhsT=ones_b, rhs=tq["sq"], start=True, stop=True), reads=[tn + "sq", "matsb"], writes=[b2t])
            rstd_from_ss(b2[:, :], 128.0, tq["f1"], [b2t], tn + "f1")
            P.op("dve", lambda e, bc=bc, b=b: e.scalar_tensor_tensor(out=ckvn[:, 512 + b * 512:1024 + b * 512], in0=bc[:, :], scalar=smallp[:, l, 4:5], in1=tq["f1"], op0=ALU.mult, op1=ALU.mult), reads=[bct, tn + "f1", "smallp"], writes=["ckvn"])
            bkr, bkrt = proj_fm(wp, wt, 384, 96, b)
            dstk = krT[:, 512 + b * 512:1024 + b * 512]
            if smp:
                rope_apply(bkr[0:96, :], dstk, b * 512, "krT", cosM, sinM, rotM, [bkrt], 512, parts=96)
            else:
                P.op("act", lambda e, bkr=bkr, dstk=dstk: e.activation(out=dstk, in_=bkr[0:96, :], func=ACTF.Copy), reads=[bkrt], writes=["krT"])
        for b in range(NB):
            for h in range(4):
                bq, bqt = bank()
                def qf(e, bq=bq, h=h, b=b):
                    e.matmul(bq[0:96, :], lhsT=wq[:, 0, h * 96:(h + 1) * 96], rhs=cqn[:, 0, blk(b)], start=True, stop=False)
                    return e.matmul(bq[0:96, :], lhsT=wq[:, 1, h * 96:(h + 1) * 96], rhs=cqn[:, 1, blk(b)], start=False, stop=True)
                P.op("pe", qf, reads=["wq", "cqn"], writes=[bqt])
                if b < 2:
                    rope_apply(bq[0:96, :], QTm[:, h, blk(b)], b * 512, "QTm", cosM, sinM, rotM, [bqt], 512, parts=96, pre=MLA_SCALE)
                else:
                    P.op("act", lambda e, bq=bq, h=h, b=b: e.activation(out=QTm[:, h, blk(b)], in_=bq[0:96, :], func=ACTF.Copy, scale=MLA_SCALE), reads=[bqt], writes=["QTm"])
        P.barrier()
        for kb in range(4):
            for h in range(4):
                bq, bqt = bank()
                P.op("pe", lambda e, bq=bq, h=h, kb=kb: e.matmul(bq[0:64, :], lhsT=wkv[:, h * 64:(h + 1) * 64], rhs=ckvn[:, kb * 512:(kb + 1) * 512], start=True, stop=True), reads=["wkv", "ckvn"], writes=[bqt])
                if h % 2 == 0:
                    P.op("act", lambda e, bq=bq, h=h, kb=kb: e.activation(out=KTm[0:64, h, kb * 512:(kb + 1) * 512], in_=bq[0:64, :], func=ACTF.Copy), reads=[bqt], writes=["KTm"])
                else:
                    P.op("dve", lambda e, bq=bq, h=h, kb=kb: e.tensor_copy(out=KTm[0:64, h, kb * 512:(kb + 1) * 512], in_=bq[0:64, :]), reads=[bqt], writes=["KTm"])
        for h in range(4):
            P.op("pool", lambda e, h=h: e.tensor_copy(out=KTm[64:96, h, :], in_=krT[64:96, :]), reads=["krT"], writes=["KTm"])
        for kt in range(16):
            bq, bqt = bank()
            P.op("pe", lambda e, bq=bq, kt=kt: e.matmul(bq[:, :], lhsT=ckvn[:, kt * 128:(kt + 1) * 128], rhs=wkv[:, 256:768], start=True, stop=True), reads=["wkv", "ckvn"], writes=[bqt])
            P.op("act", lambda e, bq=bq, kt=kt: e.activation(out=Vm[:, kt, :], in_=bq[:, :], func=ACTF.Copy), reads=[bqt], writes=["Vm"])
        for h in range(4):
            c, half = h // 2, h % 2
            rows = slice(half * 64, half * 64 + 64)
            for qb in range(2):
                chunks = [(KTm[:, h, i * 128:(i + 1) * 128], Vm[:, i, h * 128:(h + 1) * 128], None) for i in range(12)]
                attend(QTm[:, h, blk(qb)], 512, chunks, mixT[rows, 6 + c, blk(qb)], half, ["QTm", "KTm", "Vm"], f"mixT{qb}", atmp)
            for s in range(2):
                t0 = 1024 + 256 * s
                chunks = [(KTm[:, h, 512 + t0 + i * 128:512 + t0 + (i + 1) * 128], Vm[:, 12 + 2 * s + i, h * 128:(h + 1) * 128], None) for i in range(2)]
                attend(QTm[:, h, t0:t0 + 256], 256, chunks, mixT[rows, 6 + c, t0:t0 + 256], half, ["QTm", "KTm", "Vm"], "mixT2", atmp)
        if l == 0:
            dbg("od", mixT[:, 6:8, :], [128, 2, T], MIXT)
        if stage < 5:
            break
        P.barrier()

        P.mark(f"L{l} dn")
        def dn_phase(tok0, ntok, seqs, is_sample):
            A.reset()
            NC = ntok // 64
            nseq = len(seqs)
            QTd = A.bf16(4 * ntok, parts=64).rearrange("p (h t) -> p h t", h=4)
            KTd = A.bf16(4 * ntok, parts=64).rearrange("p (h t) -> p h t", h=4)
            VTd = A.bf16(2 * ntok).rearrange("p (c t) -> p c t", c=2)
            sg = A.bf16(NC * 256, parts=64).rearrange("p (c n) -> p c n", c=NC)
            oacc = A.bf16(NC * 256, parts=64).rearrange("p (c n) -> p c n", c=NC)
            ab = A.f32(NC * 16, parts=64).rearrange("p (c n) -> p c n", c=NC)
            g_all = A.f32(NC * 8, parts=64).rearrange("p (c n) -> p c n", c=NC)
            beta_all = A.f32(NC * 8, parts=64).rearrange("p (c n) -> p c n", c=NC)
            smalls = A.f32(64 + 64 + 16 + 16 + 32 + 8, parts=128)
            dng_row = smalls[0:64, 0:64]; cwqk = smalls[0:64, 64:96].rearrange("p (c j) -> p c j", c=8)
            cwv = smalls[:, 96:104].rearrange("p (c j) -> p c j", c=2)
            dndt = smalls[0:64, 104:112]; nea = smalls[0:64, 112:120]
            P.dma(lambda e: e.dma_start(out=dng_row, in_=dng_row_d[:, l]), writes=["dn_small"])
            P.dma(lambda e: e.dma_start(out=cwqk, in_=cwqk_d[:, l]), writes=["dn_small"])
            P.dma(lambda e: e.dma_start(out=cwv, in_=cwv_d[:, l]), writes=["dn_small"])
            P.dma(lambda e: e.dma_start(out=dndt, in_=dndt_d[:, l]), writes=["dn_small"])
            P.dma(lambda e: e.dma_start(out=nea, in_=dnal_d[:, l]), writes=["dn_small"])
            mark = A.off
            slen = ntok // nseq
            W = ntok + 3 * nseq
            zpads = [A.f32(W), A.f32(W)]; accs = [A.f32(W), A.f32(W)]
            sqbs = [A.bf16(512, parts=64), A.bf16(512, parts=64)]; rsts = [A.f32(512, parts=64), A.f32(512, parts=64)]
            for i_ in range(2):
                P.op("pool", lambda e, i_=i_: e.memset(zpads[i_], 0.0), writes=[f"zpad{i_}"])
            cvi = [0]
            nblk = ntok // 512
            b0 = tok0 // 512

            def zcopy(bk, bt, parts):
                for bi in range(1):
                    pass

            def conv_chunk(parts, wcol_fn, c0, m, wp, wt):
                cvi[0] += 1
                zpad = zpads[cvi[0] % 2]; acc = accs[cvi[0] % 2]
                zt = f"zpad{cvi[0] % 2}"; at_ = f"acc{cvi[0] % 2}"
                for bi in range(nblk):
                    bk, bt = proj_fm(wp, wt, c0, m, b0 + bi)
                    if is_sample:
                        P.op("act", lambda e, bk=bk, bi=bi: e.activation(out=zpad[0:parts, 1 + bi * 512:1 + (bi + 1) * 512], in_=bk[0:parts, :], func=ACTF.Copy), reads=[bt], writes=[zt])
                    else:
                        for s in range(2):
                            P.op("act", lambda e, bk=bk, s=s: e.activation(out=zpad[0:parts, 1 + 259 * s:257 + 259 * s], in_=bk[0:parts, 256 * s:256 * (s + 1)], func=ACTF.Copy), reads=[bt], writes=[zt])
                n = W - 3
                P.op("dve", lambda e: e.tensor_scalar(out=acc[0:parts, 0:n], in0=zpad[0:parts, 0:n], scalar1=wcol_fn(0), scalar2=None, op0=ALU.mult), reads=[zt, "dn_small"], writes=[at_])
                for j in range(1, 4):
                    P.op("dve", lambda e, j=j: e.scalar_tensor_tensor(out=acc[0:parts, 0:n], in0=zpad[0:parts, j:n + j], scalar=wcol_fn(j), in1=acc[0:parts, 0:n], op0=ALU.mult, op1=ALU.add), reads=[zt, at_, "dn_small"], writes=[at_])
                return acc, at_

            def segs():
                if is_sample:
                    return [(0, 0, 512), (512, 512, 512)]
                return [(0, 0, 256), (259, 256, 256)]

            wp, wt = panel(("in", l, "D1"))
            for hc in range(8):
                acc, at_ = conv_chunk(64, lambda j, hc=hc: cwqk[:, hc, j:j + 1], hc * 64, 64, wp, wt)
                P.op("act", lambda e, acc=acc: e.activation(out=acc[0:64, 0:W - 3], in_=acc[0:64, 0:W - 3], func=ACTF.Silu), reads=[at_], writes=[at_])
                for si_, (ao, to, n) in enumerate(segs()):
                    sqb = sqbs[si_ % 2]; rst = rsts[si_ % 2]; sqt = f"dn_sqb{si_ % 2}"; rtt = f"dn_rst{si_ % 2}"
                    P.op("act", lambda e, ao=ao, n=n, acc=acc, sqb=sqb: e.activation(out=sqb[:, 0:n], in_=acc[0:64, ao:ao + n], func=ACTF.Square), reads=[at_], writes=[sqt])
                    b2, b2t = bank()
                    P.op("pe", lambda e, b2=b2, n=n, sqb=sqb: e.matmul(b2[0:64, 0:n], lhsT=ones_b[0:64, 0:64], rhs=sqb[:, 0:n], start=True, stop=True), reads=[sqt, "matsb"], writes=[b2t])
                    rstd_from_ss(b2[0:64, 0:n], 1.0, rst[:, 0:n], [b2t], rtt)
                    dst = (QTd if hc < 4 else KTd)[:, hc % 4, to:to + n]
                    sc = 0.125 if hc < 4 else 1.0
                    P.op("dve", lambda e, ao=ao, n=n, dst=dst, sc=sc, acc=acc, rst=rst: e.scalar_tensor_tensor(out=dst, in0=acc[0:64, ao:ao + n], scalar=sc, in1=rst[:, 0:n], op0=ALU.mult, op1=ALU.mult), reads=[at_, rtt], writes=["dn_qk"])
            wp, wt = panel(("in", l, "D2"))
            for vc in range(2):
                acc, at_ = conv_chunk(128, lambda j, vc=vc: cwv[:, vc, j:j + 1], vc * 128, 128, wp, wt)
                for (ao, to, n) in segs():
                    P.op("act", lambda e, ao=ao, to=to, n=n, vc=vc, acc=acc: e.activation(out=VTd[:, vc, to:to + n], in_=acc[:, ao:ao + n], func=ACTF.Silu), reads=[at_], writes=["dn_v"])
            for c in range(NC):
                bk, bt = proj_tm(wp, wt, 256, 256, tok0 + c * 64, m=64)
                P.op("act", lambda e, bk=bk, c=c: e.activation(out=sg[:, c, :], in_=bk[0:64, 0:256], func=ACTF.Silu), reads=[bt], writes=["dn_sg"])
            wp, wt = panel(("in", l, "DAB"))
            bkg, bkgt = bank("a")
            for c in range(NC):
                proj_tm(wp, wt, 0, 16, tok0 + c * 64, m=64, bk_bt=(bkg, bkgt), col0=c * 16)
            P.op("act", lambda e: e.activation(out=ab, in_=bkg[0:64, 0:NC * 16].rearrange("p (c n) -> p c n", c=NC), func=ACTF.Copy), reads=[bkgt], writes=["dn_ab"])
            tg = A.f32(NC * 8, parts=64).rearrange("p (c n) -> p c n", c=NC)
            P.op("dve", lambda e: e.tensor_tensor(out=tg, in0=ab[:, :, 0:8], in1=dndt.unsqueeze(1).to_broadcast([64, NC, 8]), op=ALU.add), reads=["dn_ab", "dn_small"], writes=["dn_tg"])
            P.op("act", lambda e: e.activation(out=tg, in_=tg, func=ACTF.Exp), reads=["dn_tg"], writes=["dn_tg"])
            P.op("act", lambda e: e.activation(out=tg, in_=tg, func=ACTF.Ln, bias=1.0), reads=["dn_tg"], writes=["dn_tg"])
            P.op("act", lambda e: e.activation(out=nea, in_=nea, func=ACTF.Exp), reads=["dn_small"], writes=["dn_nea"])
            P.op("dve", lambda e: e.tensor_scalar(out=nea, in0=nea, scalar1=-1.0, scalar2=None, op0=ALU.mult), reads=["dn_nea"], writes=["dn_nea"])
            P.op("dve", lambda e: e.tensor_tensor(out=g_all, in0=tg, in1=nea.unsqueeze(1).to_broadcast([64, NC, 8]), op=ALU.mult), reads=["dn_tg", "dn_nea"], writes=["dn_g"])
            P.op("act", lambda e: e.activation(out=beta_all, in_=ab[:, :, 8:16], func=ACTF.Exp, scale=-1.0), reads=["dn_ab"], writes=["dn_beta"])
            P.op("dve", lambda e: e.tensor_scalar_add(out=beta_all, in0=beta_all, scalar1=1.0), reads=["dn_beta"], writes=["dn_beta"])
            P.op("dve", lambda e: e.reciprocal(out=beta_all, in_=beta_all), reads=["dn_beta"], writes=["dn_beta"])
            if l == 0 and is_sample:
                dbg("dn_q", QTd, [64, 4, ntok], ["dn_qk"])
                dbg("dn_k", KTd, [64, 4, ntok], ["dn_qk"])
                dbg("dn_v", VTd, [128, 2, ntok], ["dn_v"])
                dbg("dn_g", g_all, [64, NC, 8], ["dn_g"])
                dbg("dn_beta", beta_all, [64, NC, 8], ["dn_beta"])
            if "dnscan" in skip:
                return
            dnstop = ([int(x[6:]) for x in skip if x.startswith('dnstop')] + [0])[0]
            P.mark(f"L{l} dnscan{tok0}")
            P.barrier()
            A.reset(mark)
            class _Multi:
                def __init__(self, ars):
                    self.ars = ars

                def f32(self, n, parts=128):
                    for a in self.ars:
                        if a.off + n <= a.n:
                            return a.f32(n, parts)
                    raise AssertionError('out of scratch')

                def bf16(self, n, parts=128):
                    m = (n + 1) // 2
                    for a in self.ars:
                        if a.off + m <= a.n:
                            return a.bf16(n, parts)
                    raise AssertionError('out of scratch')

            def make_chain(cid, AL, nch):
                def tk(s):
                    return f"{s}_{cid}"
                cb_i = [0]

                def cbank():
                    if nch == 1:
                        return bank()
                    i = 4 * cid + cb_i[0] % 4
                    cb_i[0] += 1
                    return banks[i], f"bank{i}"
                def t512(dt):
                    return (AL.f32(512, parts=64) if dt == F32 else AL.bf16(512, parts=64))
                Dm, Am, Bm, U, AN, T32, P32, W_, Pm, WT = (t512(F32) for _ in range(10))
                Vb, KbEg, kdec = Dm, Am, Bm
                dg = U
                db, de, KbT, QdT, qkT, ANb, ATb, ANb2, ATb2, Pb = (t512(BF16) for _ in range(10))
                vnew = AL.f32(256, parts=64); vnew_b = AL.bf16(256, parts=64); S = AL.f32(256, parts=64); Sb = AL.bf16(256, parts=64)
                of = AL.f32(256, parts=64); otmp = vnew; oo = AL.bf16(256, parts=64)
                sm8 = AL.f32(8 * 8, parts=64)
                g8, beta8, gc, eg, tmg, ekd, gl, beg = (sm8[:, i * 8:(i + 1) * 8] for i in range(8))
                ss4 = AL.f32(8, parts=64)
                I64f = ident_f[0:64, 0:64]; I64b = ident_b[0:64, 0:64]; O64f = ones_f[0:64, 0:64]; O64b = ones_b[0:64, 0:64]

                def v3(t):
                    return t.rearrange("p (u i) -> p u i", u=8)

                def vh(t):
                    return t.rearrange("p (h n) -> p h n", h=4)

                def bc8(s):
                    return s.unsqueeze(2).to_broadcast([64, 8, 64])

                def mask_b(i):
                    return masks[0:64, i, :].unsqueeze(1).to_broadcast([64, 8, 64])

                def run(d, si, sc0, snc):
                    MI_tri = 3 if d == 0 else 2
                    M_sN, M_tT, M_sT = (0, 3, 1) if d == 0 else (1, 2, 0)
                    if is_sample:
                        P.dma(lambda e, d=d: e.dma_start(out=S.rearrange("p (h v) -> p h v", h=4), in_=sdn_d[l, d].rearrange("h k v -> k h v")), writes=[tk("dn_S")])
                    else:
                        P.op("pool", lambda e: e.memset(S, 0.0), writes=[tk("dn_S")])
                    P.op("act", lambda e: e.activation(out=Sb, in_=S, func=ACTF.Copy), reads=[tk("dn_S")], writes=[tk("dn_Sb")])
                    pairs = list(range(sc0, sc0 + snc, 2))
                    if d == 1:
                        pairs = pairs[::-1]
                    for c0 in pairs:
                        t0 = c0 * 64
                        KT2 = KTd[:, :, t0:t0 + 128]; QT2 = QTd[:, :, t0:t0 + 128]
                        P.op("act", lambda e, c0=c0, d=d: e.activation(out=g8.rearrange("p (h j) -> p h j", h=4), in_=g_all[:, c0:c0 + 2, d * 4:d * 4 + 4].rearrange("p j h -> p h j"), func=ACTF.Copy), reads=["dn_g"], writes=[tk("dn_g8")])
                        P.op("act", lambda e, c0=c0, d=d: e.activation(out=beta8.rearrange("p (h j) -> p h j", h=4), in_=beta_all[:, c0:c0 + 2, d * 4:d * 4 + 4].rearrange("p j h -> p h j"), func=ACTF.Copy), reads=["dn_beta"], writes=[tk("dn_b8")])
                        if dnstop == 1:
                            return
                        bk1, bk1t = cbank()
                        def cs_f(e, bk1=bk1, MI_tri=MI_tri):
                            e.matmul(bk1[0:64, 0:8], lhsT=masks[0:64, MI_tri, :], rhs=g8, start=True, stop=True)
                            return e.matmul(bk1[0:64, 8:16], lhsT=O64f, rhs=g8, start=True, stop=True)
                        P.op("pe", cs_f, reads=[tk("dn_g8"), "masks", "matsf"], writes=[bk1t])
                        yield
                        P.op("act", lambda e, bk1=bk1: e.activation(out=gc, in_=bk1[0:64, 0:8], func=ACTF.Copy), reads=[bk1t], writes=[tk("dn_gc")])
                        P.op("act", lambda e, bk1=bk1: e.activation(out=eg, in_=bk1[0:64, 0:8], func=ACTF.Exp), reads=[bk1t], writes=[tk("dn_eg")])
                        P.op("act", lambda e, bk1=bk1: e.activation(out=gl, in_=bk1[0:64, 8:16], func=ACTF.Exp), reads=[bk1t], writes=[tk("dn_gl")])
                        P.op("dve", lambda e, bk1=bk1: e.tensor_tensor(out=tmg, in0=bk1[0:64, 8:16], in1=gc, op=ALU.subtract), reads=[bk1t, tk("dn_gc")], writes=[tk("dn_tmg")])
                        P.op("act", lambda e: e.activation(out=ekd, in_=tmg, func=ACTF.Exp), reads=[tk("dn_tmg")], writes=[tk("dn_ekd")])
                        P.op("dve", lambda e: e.tensor_tensor(out=beg, in0=beta8, in1=eg, op=ALU.mult), reads=[tk("dn_b8"), tk("dn_eg")], writes=[tk("dn_beg")])
                        if dnstop == 2:
                            return
                        P.op("dve", lambda e: e.tensor_tensor(out=v3(dg), in0=I64f.unsqueeze(1).to_broadcast([64, 8, 64]), in1=bc8(gc), op=ALU.mult), reads=["matsf", tk("dn_gc")], writes=[tk("dn_U")])
                        P.op("pool", lambda e: e.tensor_tensor(out=v3(db), in0=I64f.unsqueeze(1).to_broadcast([64, 8, 64]), in1=bc8(beta8), op=ALU.mult), reads=["matsf", tk("dn_b8")], writes=[tk("dn_db")])
                        P.op("pool", lambda e: e.tensor_tensor(out=v3(de), in0=I64f.unsqueeze(1).to_broadcast([64, 8, 64]), in1=bc8(eg), op=ALU.mult), reads=["matsf", tk("dn_eg")], writes=[tk("dn_de")])
                        Rg, Rgt = cbank(); Rb, Rbt = cbank(); Re, Ret = cbank()
                        P.op("pe", lambda e, Rg=Rg: e.matmul(Rg[0:64, :], lhsT=O64f, rhs=dg, start=True, stop=True), reads=[tk("dn_U"), "matsf"], writes=[Rgt])
                        yield
                        P.op("pe", lambda e, Rb=Rb: e.matmul(Rb[0:64, :], lhsT=O64b, rhs=db, start=True, stop=True), reads=[tk("dn_db"), "matsb"], writes=[Rbt])
                        yield
                        P.op("pe", lambda e, Re=Re: e.matmul(Re[0:64, :], lhsT=O64b, rhs=de, start=True, stop=True), reads=[tk("dn_de"), "matsb"], writes=[Ret])
                        yield
                        if dnstop == 3:
                            return
                        P.op("dve", lambda e, Rg=Rg: e.tensor_tensor(out=v3(Dm), in0=bc8(gc), in1=v3(Rg[0:64, :]), op=ALU.subtract), reads=[Rgt, tk("dn_gc")], writes=[tk("dn_Dm")])
                        P.op("dve", lambda e: e.tensor_scalar_min(out=Am, in0=Dm, scalar1=0.0), reads=[tk("dn_Dm")], writes=[tk("dn_Am")])
                        P.op("dve", lambda e: e.tensor_scalar(out=Dm, in0=Dm, scalar1=-1.0, scalar2=0.0, op0=ALU.mult, op1=ALU.min), reads=[tk("dn_Dm")], writes=[tk("dn_Dm")])
                        P.op("act", lambda e: e.activation(out=Am, in_=Am, func=ACTF.Exp), reads=[tk("dn_Am")], writes=[tk("dn_Am")])
                        P.op("act", lambda e: e.activation(out=Dm, in_=Dm, func=ACTF.Exp), reads=[tk("dn_Dm")], writes=[tk("dn_Dm")])
                        P.op("pool", lambda e, M_sN=M_sN: e.tensor_tensor(out=v3(Am), in0=v3(Am), in1=mask_b(M_sN), op=ALU.mult), reads=[tk("dn_Am"), "masks"], writes=[tk("dn_Am")])
                        P.op("pool", lambda e, M_tT=M_tT: e.tensor_tensor(out=v3(Bm), in0=v3(Dm), in1=mask_b(M_tT), op=ALU.mult), reads=[tk("dn_Dm"), "masks"], writes=[tk("dn_Bm")])
                        P.op("pool", lambda e, M_sT=M_sT: e.tensor_tensor(out=v3(Dm), in0=v3(Dm), in1=mask_b(M_sT), op=ALU.mult), reads=[tk("dn_Dm"), "masks", tk("dn_Bm")], writes=[tk("dn_Dm")])
                        if dnstop == 4:
                            return
                        P.op("dve", lambda e, Rb=Rb, KT2=KT2: e.tensor_tensor(out=vh(KbT), in0=KT2, in1=vh(Rb[0:64, :]), op=ALU.mult), reads=[Rbt, "dn_qk"], writes=[tk("dn_KbT")])
                        P.op("dve", lambda e, Re=Re, QT2=QT2: e.tensor_tensor(out=vh(QdT), in0=QT2, in1=vh(Re[0:64, :]), op=ALU.mult), reads=[Ret, "dn_qk"], writes=[tk("dn_QdT")])
                        if dnstop == 5:
                            return
                        pAN, pANt = cbank(); pAT, pATt = cbank(); pQK, pQKt = cbank()
                        def prods(e, pAN=pAN, pAT=pAT, pQK=pQK, KT2=KT2, QT2=QT2):
                            ins = None
                            for h in range(4):
                                for j in range(2):
                                    u = h * 2 + j
                                    cs = slice(u * 64, u * 64 + 64)
                                    ks = KT2[:, h, j * 64:j * 64 + 64]
                                    e.matmul(pAN[0:64, cs], lhsT=KbT[:, cs], rhs=ks, start=True, stop=True)
                                    e.matmul(pAT[0:64, cs], lhsT=ks, rhs=KbT[:, cs], start=True, stop=True)
                                    ins = e.matmul(pQK[0:64, cs], lhsT=ks, rhs=QT2[:, h, j * 64:j * 64 + 64], start=True, stop=True)
                            return ins
                        P.op("pe", prods, reads=[tk("dn_KbT"), "dn_qk"], writes=[pANt, pATt, pQKt])
                        yield
                        P.op("dve", lambda e, pAN=pAN: e.tensor_tensor(out=AN, in0=pAN[0:64, :], in1=Am, op=ALU.mult), reads=[pANt, tk("dn_Am")], writes=[tk("dn_AN")])
                        P.op("dve", lambda e, pAT=pAT: e.tensor_tensor(out=ATb, in0=pAT[0:64, :], in1=Dm, op=ALU.mult), reads=[pATt, tk("dn_Dm")], writes=[tk("dn_ATb")])
                        P.op("act", lambda e: e.activation(out=ANb, in_=AN, func=ACTF.Copy), reads=[tk("dn_AN")], writes=[tk("dn_ANb")])
                        P.op("dve", lambda e, pQK=pQK: e.tensor_tensor(out=qkT, in0=pQK[0:64, :], in1=Bm, op=ALU.mult), reads=[pQKt, tk("dn_Bm")], writes=[tk("dn_qkT")])
                        P.op("pool", lambda e: e.tensor_tensor(out=v3(Pb), in0=I64b.unsqueeze(1).to_broadcast([64, 8, 64]), in1=v3(ATb), op=ALU.subtract), reads=[tk("dn_ATb"), "matsb"], writes=[tk("dn_Pb")])
                        if dnstop == 6:
                            return
                        an, at, ant, att = ANb, ATb, tk("dn_ANb"), tk("dn_ATb")
                        an_n, at_n, ant_n, att_n = ANb2, ATb2, tk("dn_ANb2"), tk("dn_ATb2")
                        for lev in range(4):
                            pa, pat = cbank(); pb, pbt = cbank()
                            def sqf(e, pa=pa, pb=pb, an=an, at=at, lev=lev):
                                ins = None
                                for u in range(8):
                                    cs = slice(u * 64, u * 64 + 64)
                                    ins = e.matmul(pa[0:64, cs], lhsT=at[:, cs], rhs=an[:, cs], start=True, stop=True)
                                    if lev < 3:
                                        ins = e.matmul(pb[0:64, cs], lhsT=an[:, cs], rhs=at[:, cs], start=True, stop=True)
                                return ins
                            P.op("pe", sqf, reads=[ant, att], writes=[pat, pbt])
                            yield
                            P.op("act", lambda e, pa=pa, an_n=an_n: e.activation(out=an_n, in_=pa[0:64, :], func=ACTF.Copy), reads=[pat], writes=[ant_n])
                            if lev < 3:
                                P.op("dve", lambda e, pb=pb, at_n=at_n: e.tensor_copy(out=at_n, in_=pb[0:64, :]), reads=[pbt], writes=[att_n])
                            pp, ppt = cbank()
                            def apf(e, pp=pp, an_n=an_n):
                                ins = None
                                for u in range(8):
                                    cs = slice(u * 64, u * 64 + 64)
                                    ins = e.matmul(pp[0:64, cs], lhsT=an_n[:, cs], rhs=Pb[:, cs], start=True, stop=True)
                                return ins
                            P.op("pe", apf, reads=[ant_n, tk("dn_Pb")], writes=[ppt])
                            yield
                            P.op("dve", lambda e, pp=pp: e.tensor_tensor(out=Pb, in0=pp[0:64, :], in1=Pb, op=ALU.add), reads=[ppt, tk("dn_Pb")], writes=[tk("dn_Pb")])
                            an, at, ant, att, an_n, at_n, ant_n, att_n = an_n, at_n, ant_n, att_n, an, at, ant, att
                        ptp, ptpt = cbank()
                        def trp(e, ptp=ptp):
                            ins = None
                            for u in range(8):
                                cs = slice(u * 64, u * 64 + 64)
                                ins = e.matmul(ptp[0:64, cs], lhsT=Pb[:, cs], rhs=I64b, start=True, stop=True)
                            return ins
                        P.op("pe", trp, reads=[tk("dn_Pb"), "matsb"], writes=[ptpt])
                        yield
                        P.op("act", lambda e, ptp=ptp: e.activation(out=T32, in_=ptp[0:64, :], func=ACTF.Copy), reads=[ptpt], writes=[tk("dn_T32")])
                        P.op("dve", lambda e: e.tensor_copy(out=P32, in_=Pb), reads=[tk("dn_Pb")], writes=[tk("dn_P32")])
                        pw1, pw1t = cbank()
                        def w1f(e, pw1=pw1):
                            ins = None
                            for u in range(8):
                                cs = slice(u * 64, u * 64 + 64)
                                ins = e.matmul(pw1[0:64, cs], lhsT=AN[:, cs], rhs=P32[:, cs], start=True, stop=True)
                            return ins
                        P.op("pe", w1f, reads=[tk("dn_AN"), tk("dn_P32")], writes=[pw1t])
                        yield
                        P.op("dve", lambda e, pw1=pw1: e.tensor_tensor(out=W_, in0=pw1[0:64, :], in1=P32, op=ALU.add), reads=[pw1t, tk("dn_P32")], writes=[tk("dn_W")])
                        pw2, pw2t = cbank()
                        def w2f(e, pw2=pw2):
                            ins = None
                            for u in range(8):
                                cs = slice(u * 64, u * 64 + 64)
                                ins = e.matmul(pw2[0:64, cs], lhsT=T32[:, cs], rhs=W_[:, cs], start=True, stop=True)
                            return ins
                        P.op("pe", w2f, reads=[tk("dn_T32"), tk("dn_W")], writes=[pw2t])
                        yield
                        P.op("dve", lambda e, pw2=pw2: e.scalar_tensor_tensor(out=Pm, in0=P32, scalar=2.0, in1=pw2[0:64, :], op0=ALU.mult, op1=ALU.subtract), reads=[pw2t, tk("dn_P32")], writes=[tk("dn_P")])
                        if dnstop == 7:
                            return
                        pk, pkt = cbank(); pv_, pvt = cbank()
                        def trf(e, pk=pk, pv_=pv_, t0=t0):
                            ins = None
                            for h in range(4):
                                for j in range(2):
                                    u = h * 2 + j
                                    tk = slice(t0 + j * 64, t0 + j * 64 + 64)
                                    e.matmul(pk[0:64, u * 64:u * 64 + 64], lhsT=KTd[:, h, tk], rhs=I64b, start=True, stop=True)
                                    ins = None
                            for h in range(4):
                                for j in range(2):
                                    u = h * 2 + j
                                    tk = slice(t0 + j * 64, t0 + j * 64 + 64)
                                    hl = h % 2
                                    ins = e.matmul(pv_[0:64, u * 64:u * 64 + 64], lhsT=VTd[:, h // 2, tk], rhs=ident_b[:, hl * 64:hl * 64 + 64], start=True, stop=True)
                            return ins
                        P.op("pe", trf, reads=["dn_qk", "dn_v", "matsb"], writes=[pkt, pvt])
                        yield
                        P.op("dve", lambda e, pv_=pv_: e.tensor_tensor(out=v3(Vb), in0=bc8(beta8), in1=v3(pv_[0:64, :]), op=ALU.mult), reads=[pvt, tk("dn_b8")], writes=[tk("dn_Dm")])
                        P.op("dve", lambda e, pk=pk: e.tensor_tensor(out=v3(KbEg), in0=bc8(beg), in1=v3(pk[0:64, :]), op=ALU.mult), reads=[pkt, tk("dn_beg")], writes=[tk("dn_Am")])
                        P.op("dve", lambda e, pk=pk: e.tensor_tensor(out=v3(kdec), in0=bc8(ekd), in1=v3(pk[0:64, :]), op=ALU.mult), reads=[pkt, tk("dn_ekd")], writes=[tk("dn_Bm")])
                        if dnstop == 8:
                            return
                        pu, put = cbank(); pw, pwt = cbank()
                        def uwf(e, pu=pu, pw=pw):
                            ins = None
                            for u in range(8):
                                cs = slice(u * 64, u * 64 + 64)
                                e.matmul(pu[0:64, cs], lhsT=Pm[:, cs], rhs=Vb[:, cs], start=True, stop=True)
                                ins = e.matmul(pw[0:64, cs], lhsT=KbEg[:, cs], rhs=Pm[:, cs], start=True, stop=True)
                            return ins
                        P.op("pe", uwf, reads=[tk("dn_P"), tk("dn_Dm"), tk("dn_Am")], writes=[put, pwt])
                        yield
                        P.op("act", lambda e, pu=pu: e.activation(out=U, in_=pu[0:64, :], func=ACTF.Copy), reads=[put], writes=[tk("dn_U")])
                        P.op("dve", lambda e, pw=pw: e.tensor_copy(out=WT, in_=pw[0:64, :]), reads=[pwt], writes=[tk("dn_WT")])
                        if dnstop == 9:
                            return
                        for j in ((0, 1) if d == 0 else (1, 0)):
                            c = c0 + j
                            def cs_(h, j=j):
                                return slice((h * 2 + j) * 64, (h * 2 + j) * 64 + 64)
                            pws, pwst = cbank()
                            def wsf(e, pws=pws, cs_=cs_):
                                ins = None
                                for h in range(4):
                                    ins = e.matmul(pws[0:64, h * 64:h * 64 + 64], lhsT=WT[:, cs_(h)], rhs=S[:, h * 64:h * 64 + 64], start=True, stop=True)
                                return ins
                            P.op("pe", wsf, reads=[tk("dn_WT"), tk("dn_S")], writes=[pwst])
                            yield
                            Uj = U.rearrange("p (h j i) -> p h j i", h=4, j=2)[:, :, j, :]
                            P.op("dve", lambda e, pws=pws, Uj=Uj: e.tensor_tensor(out=vnew.rearrange("p (h i) -> p h i", h=4), in0=Uj, in1=pws[0:64, 0:256].rearrange("p (h i) -> p h i", h=4), op=ALU.subtract), reads=[pwst, tk("dn_U")], writes=[tk("dn_vnew")])
                            P.op("act", lambda e: e.activation(out=vnew_b, in_=vnew, func=ACTF.Copy), reads=[tk("dn_vnew")], writes=[tk("dn_vnewb")])
                            po, pot = cbank(); psn, psnt = cbank()
                            def osf(e, po=po, psn=psn, cs_=cs_, j=j, QT2=QT2):
                                ins = None
                                for h in range(4):
                                    hs = slice(h * 64, h * 64 + 64)
                                    e.matmul(po[0:64, hs], lhsT=QdT[:, cs_(h)], rhs=Sb[:, hs], start=True, stop=False)
                                    e.matmul(po[0:64, hs], lhsT=qkT[:, cs_(h)], rhs=vnew_b[:, hs], start=False, stop=True)
                                    ins = e.matmul(psn[0:64, hs], lhsT=kdec[:, cs_(h)], rhs=vnew[:, hs], start=True, stop=True)
                                return ins
                            P.op("pe", osf, reads=[tk("dn_QdT"), tk("dn_Sb"), tk("dn_qkT"), tk("dn_vnew"), tk("dn_vnewb"), tk("dn_Bm")], writes=[pot, psnt])
                            yield
                            glj = gl.rearrange("p (h j) -> p h j", h=4)[:, :, j:j + 1].to_broadcast([64, 4, 64])
                            P.op("dve", lambda e, glj=glj: e.tensor_tensor(out=S.rearrange("p (h v) -> p h v", h=4), in0=S.rearrange("p (h v) -> p h v", h=4), in1=glj, op=ALU.mult), reads=[tk("dn_S"), tk("dn_gl")], writes=[tk("dn_S")])
                            P.op("dve", lambda e, psn=psn: e.tensor_tensor(out=S, in0=psn[0:64, 0:256], in1=S, op=ALU.add), reads=[psnt, tk("dn_S")], writes=[tk("dn_S")])
                            P.op("act", lambda e: e.activation(out=Sb, in_=S, func=ACTF.Copy), reads=[tk("dn_S")], writes=[tk("dn_Sb")])
                            if d == 0:
                                P.op("act", lambda e, po=po, c=c: e.activation(out=oacc[:, c, :], in_=po[0:64, 0:256], func=ACTF.Copy), reads=[pot], writes=[tk("dn_oacc")])
                            else:
                                P.op("dve", lambda e, po=po, c=c: e.tensor_tensor(out=of, in0=po[0:64, 0:256], in1=oacc[:, c, :], op=ALU.add), reads=[pot, tk("dn_oacc")], writes=[tk("dn_of")])
                                P.op("act", lambda e: e.activation(out=otmp, in_=of, func=ACTF.Square), reads=[tk("dn_of")], writes=[tk("dn_vnew")])
                                P.op("dve", lambda e: e.tensor_reduce(out=ss4[:, 0:4], in_=otmp.rearrange("p (h v) -> p h v", h=4), axis=AX.X, op=ALU.add), reads=[tk("dn_vnew")], writes=[tk("dn_ss4")])
                                rstd_from_ss(ss4[:, 0:4], 64.0, ss4[:, 4:8], [tk("dn_ss4")], tk("dn_ss4b"))
                                P.op("dve", lambda e: e.tensor_tensor(out=of.rearrange("p (h v) -> p h v", h=4), in0=of.rearrange("p (h v) -> p h v", h=4), in1=ss4[:, 4:8].unsqueeze(2).to_broadcast([64, 4, 64]), op=ALU.mult), reads=[tk("dn_of"), tk("dn_ss4b")], writes=[tk("dn_of")])
                                P.op("pool", lambda e: e.tensor_tensor(out=of.rearrange("p (h v) -> p h v", h=4), in0=of.rearrange("p (h v) -> p h v", h=4), in1=dng_row.unsqueeze(1).to_broadcast([64, 4, 64]), op=ALU.mult), reads=[tk("dn_of"), "dn_small"], writes=[tk("dn_of")])
                                P.op("dve", lambda e, c=c: e.tensor_tensor(out=oo, in0=of, in1=sg[:, c, :], op=ALU.mult), reads=[tk("dn_of"), "dn_sg"], writes=[tk("dn_oo")])
                                ptr, ptrt = cbank()
                                def otr(e, ptr=ptr):
                                    e.matmul(ptr[:, 0:64], lhsT=oo[:, 0:128], rhs=I64b, start=True, stop=True)
                                    return e.matmul(ptr[:, 64:128], lhsT=oo[:, 128:256], rhs=I64b, start=True, stop=True)
                                P.op("pe", otr, reads=[tk("dn_oo"), "matsb"], writes=[ptrt])
                                yield
                                tks = slice(tok0 + c * 64, tok0 + c * 64 + 64)
                                P.op("act", lambda e, ptr=ptr, tks=tks: e.activation(out=mixT[:, 4:6, tks], in_=ptr[:, 0:128].rearrange("p (c t) -> p c t", c=2), func=ACTF.Copy), reads=[ptrt], writes=[f"mixT{tok0 // 512 + (c * 64) // 512}"])
                    if not is_sample:
                        P.dma(lambda e, si=si, d=d: e.dma_start(out=ndn_o[si, l, d].rearrange("h k v -> k h v"), in_=S.rearrange("p (h v) -> p h v", h=4)), reads=[tk("dn_S")])
                return run

            chains = [make_chain(0, A, len(seqs))]
            if len(seqs) > 1:
                A2 = Arena(hT[:, :, :].rearrange("p c t -> p (c t)").bitcast(F32), 8 * T // 2)
                chains.append(make_chain(1, _Multi([A, A2]), len(seqs)))
            for d in range(2):
                gens = [chains[si % len(chains)](d, si, sc0, snc) for si, (sc0, snc) in enumerate(seqs)]
                if len(chains) == 1:
                    for g in gens:
                        for _ in g:
                            pass
                else:
                    while gens:
                        for g in list(gens):
                            try:
                                next(g)
                            except StopIteration:
                                gens.remove(g)

        dn_phase(0, 1024, [(0, 16)], True)
        P.barrier()
        dn_phase(1024, 512, [(0, 4), (4, 4)], False)
        if l == 0:
            dbg("oc", mixT[:, 4:6, :], [128, 2, T], MIXT)
        if stage < 6:
            break
        P.barrier()
        A.reset()

        P.mark(f"L{l} wout")
        for j in range(2):
            wp, wt = panel(("out", l, j))
            for o in range(4):
                oc = j * 4 + o
                for b in range(NB):
                    g = grp_of_blk(b)
                    bk, bt = bank()
                    def wof(e, bk=bk, wp=wp, o=o, b=b):
                        ins = None
                        for k in range(8):
                            ins = e.matmul(bk[:, :], lhsT=wp[:, k, o * 128:(o + 1) * 128], rhs=mixT[:, k, blk(b)], start=(k == 0), stop=(k == 7))
                        return ins
                    P.op("pe", wof, reads=[wt, f"mixT{b}"], writes=[bt])
                    P.op("dve", lambda e, bk=bk, oc=oc, b=b, g=g: e.scalar_tensor_tensor(out=xT[:, oc, blk(b)], in0=bk[:, :], scalar=modsT[:, l, 16 + oc, g:g + 1], in1=xT[:, oc, blk(b)], op0=ALU.mult, op1=ALU.add), reads=[bt, "modsT0", "modsT1", f"xT{b}"], writes=[f"xT{b}"])
        if l == 0:
            dbg("x1", xT[:], [128, 8, T], ["xT0", "xT1", "xT2"])
        if stage < 7:
            break
        P.barrier()
        A.reset()
        P.mark(f"L{l} ffn")
        norm_fm(lambda c, g: gsb[:, l, 1, c, g:g + 1], lambda c, g: modsT[:, l, 24 + c, g:g + 1],
                lambda c, b: hT[:, c, blk(b)], lambda b: f"hT{b}", f"n2_{l}")
        P.barrier()
        A.reset()
        actT = A.bf16(12 * T).rearrange("p (f t) -> p f t", f=12)
        sgt = [A.f32(512), A.f32(512)]
        for hf, (f0, f1) in enumerate(FF_SPLIT):
            nf = f1 - f0
            for jp in range(f0 // 2, f1 // 2):
                wp, wt = panel(("gu", l, jp))
                for fi in range(2):
                    f = 2 * jp + fi - f0
                    for b in range(NB):
                        bg, bgt = proj_fm(wp, wt, fi * 128, 128, b)
                        bu, but = proj_fm(wp, wt, 256 + fi * 128, 128, b)
                        st = sgt[(f + b) % 2]
                        stn = f"sgt{(f + b) % 2}"
                        P.op("act", lambda e, bg=bg, st=st: e.activation(out=st, in_=bg[:, :], func=ACTF.Silu), reads=[bgt], writes=[stn])
                        P.op("dve", lambda e, bu=bu, st=st, f=f, b=b: e.tensor_tensor(out=actT[:, f, blk(b)], in0=bu[:, :], in1=st, op=ALU.mult), reads=[but, stn], writes=[f"actT{b}"])
            for q in range(4):
                wp, wt = panel(("dn", l, hf, q))
                for o in range(2):
                    oc = q * 2 + o
                    for b in range(NB):
                        g = grp_of_blk(b)
                        bk, bt = bank()
                        def dnf(e, bk=bk, wp=wp, o=o, b=b, nf=nf):
                            ins = None
                            for k in range(nf):
                                ins = e.matmul(bk[:, :], lhsT=wp[:, k, o * 128:(o + 1) * 128], rhs=actT[:, k, blk(b)], start=(k == 0), stop=(k == nf - 1))
                            return ins
                        P.op("pe", dnf, reads=[wt, f"actT{b}"], writes=[bt])
                        P.op("dve", lambda e, bk=bk, oc=oc, b=b, g=g: e.scalar_tensor_tensor(out=xT[:, oc, blk(b)], in0=bk[:, :], scalar=modsT[:, l, 40 + oc, g:g + 1], in1=xT[:, oc, blk(b)], op0=ALU.mult, op1=ALU.add), reads=[bt, "modsT0", "modsT1", f"xT{b}"], writes=[f"xT{b}"])
        if l == 0:
            dbg("x2", xT[:], [128, 8, T], ["xT0", "xT1", "xT2"])
        if stage < 8:
            break

    P.mark("final")
    if stage >= 9:
        P.barrier()
        A.reset()
        yT = A.f32(8 * T).rearrange("p (c t) -> p c t", c=8)
        norm_fm(lambda c, g: gfin[:, c:c + 1], None, lambda c, b: yT[:, c, blk(b)], lambda b: f"yT{b}", "nf")
        ytm = [A.f32(1024), A.f32(1024)]
        for t in range(12):
            yt = ytm[t % 2]
            ytn = f"ytm{t % 2}"
            for half in range(2):
                bk, bt = bank()
                def trf2(e, bk=bk, t=t, half=half):
                    ins = None
                    for c in range(4):
                        ins = e.transpose(bk[:, c * 128:(c + 1) * 128], yT[:, half * 4 + c, t * 128:(t + 1) * 128], ident_f)
                    return ins
                P.op("pe", trf2, reads=[f"yT{t // 4}", "matsf"], writes=[bt])
                if half == 0:
                    P.op("act", lambda e, bk=bk, yt=yt: e.activation(out=yt[:, 0:512], in_=bk[:, :], func=ACTF.Copy), reads=[bt], writes=[ytn])
                else:
                    P.op("dve", lambda e, bk=bk, yt=yt: e.tensor_copy(out=yt[:, 512:1024], in_=bk[:, :]), reads=[bt], writes=[ytn])
            P.dma(lambda e, yt=yt, t=t: e.dma_start(out=y_o[t * 128:(t + 1) * 128, :], in_=yt), reads=[ytn])

    P.barrier()
    P.final_wait()
    P.mark("end")
    P.emit()
    build.marks = P.marks
    return nc, dbg_outs


_CACHE = {}


def kernel(**inputs):
    inputs = {k: np.asarray(v) for k, v in inputs.items()}
    sh, per = prep_inputs(inputs)
    if "nc" not in _CACHE:
        _CACHE["nc"] = build()[0]
    nc = _CACHE["nc"]
    in_maps = [{**sh, **per[c]} for c in range(NCORES)]
    res = run_bass_kernel_spmd(nc, in_maps, core_ids=list(range(NCORES)))
    R = res.results
    f = np.float32
    y_p = np.zeros((16, 256, 1024), f); y_s = np.zeros((8, 1024, 1024), f)
    ngk = np.zeros((16, NL, 256, 2, 64), f); ngv = np.zeros((16, NL, 256, 2, 64), f)
    nnk = np.zeros((16, NL, 256, 4, 64), f); nnv = np.zeros((16, NL, 256, 4, 64), f)
    ndn = np.zeros((16, NL, 2, 4, 64, 64), f); nckv = np.zeros((16, NL, 256, 128), f); nkr = np.zeros((16, NL, 256, 32), f)
    for c in range(NCORES):
        r = R[c]
        y = r["y_o"]
        y_s[c] = y[0:1024]
        y_p[2 * c] = y[1024:1280]; y_p[2 * c + 1] = y[1280:1536]
        for s in range(2):
            b = 2 * c + s
            ngk[b] = r["ngk_o"][s].reshape(NL, 256, 2, 64); ngv[b] = r["ngv_o"][s].reshape(NL, 256, 2, 64)
            nnk[b] = r["nnk_o"][s].reshape(NL, 256, 4, 64); nnv[b] = r["nnv_o"][s].reshape(NL, 256, 4, 64)
            ndn[b] = r["ndn_o"][s]; nckv[b] = r["nckv_o"][s]; nkr[b] = r["nkr_o"][s]
    return (y_p, y_s, ngk, ngv, nnk, nnv, ndn, nckv, nkr)
```
